# Optimizing a Trainium2 kernel written in Bass

```python
import math
import jax
import jax.numpy as jnp
from jax import lax
import numpy as np

D_MODEL = 1024
BATCH = 32
SEQ = 256
DEPTH = 2
DEC_BATCH = 8
DEC_SEQ = 1024
PAST_LEN = 256

GRID_W = 64
N_EVEN = (DEPTH + 1) // 2
N_ODD = DEPTH // 2
MIX_A = D_MODEL // 2
H_B = D_MODEL // 256
DH_B = 64
W_B = H_B * 2 * DH_B
H_C = D_MODEL // 128
DH_C = 64
W_C = H_C * DH_C
H_D = D_MODEL // 128
KV_D = H_D // 4
DH_D = 64
W_D = H_D * DH_D
MIX_WIDTH = MIX_A + W_B
IN_EVEN = 3 * MIX_A + 3 * W_B
IN_ODD = 3 * W_C + W_D + 2 * KV_D * DH_D
D_FF = 4 * D_MODEL
CONV_W = 3
NA_WIN_R = 8
NA_WIN_C = 16
SWA_WINDOW = 128
SWA_BLOCK = 128
Q_BLOCK = 128
DENSE_KEY_LIMIT = 2048
ROPE_BASE = 10000.0
EPS = 1e-6
NEG = -1e30

kernel_name = 'hybrid_diffusion_ctx_prefix_step'


def rmsnorm(x, g):
    xf = x.astype(jnp.float32)
    y = xf * lax.rsqrt(jnp.mean(xf * xf, axis=-1, keepdims=True) + EPS)
    return (y * g.astype(jnp.float32)).astype(x.dtype)


def to_heads(x, n):
    b, t, _ = x.shape
    return x.reshape(b, t, n, -1).transpose(0, 2, 1, 3)


def from_heads(x):
    b, h, t, d = x.shape
    return x.transpose(0, 2, 1, 3).reshape(b, t, h * d)


def axial_rope(x):
    t_len, dh = x.shape[1], x.shape[-1]
    q4 = dh // 4
    t = jnp.arange(t_len)
    rows = (t // GRID_W).astype(jnp.float32)
    cols = (t % GRID_W).astype(jnp.float32)
    inv = 1.0 / (ROPE_BASE ** (jnp.arange(q4, dtype=jnp.float32) / q4))
    ang = jnp.stack([rows[:, None] * inv, cols[:, None] * inv], axis=1)
    cos = jnp.cos(ang)[None, :, None].astype(x.dtype)
    sin = jnp.sin(ang)[None, :, None].astype(x.dtype)
    xs = x.reshape(x.shape[:-1] + (2, 2, q4))
    a, b = xs[..., 0, :], xs[..., 1, :]
    return jnp.stack([a * cos - b * sin, b * cos + a * sin], axis=-2).reshape(x.shape)


def short_conv(u, w):
    return lax.conv_general_dilated(u, w[:, None, :].astype(u.dtype), window_strides=(1,), padding=[(1, 1)], dimension_numbers=('NWC', 'WIO', 'NWC'), feature_group_count=u.shape[-1])


def sweep_queries(core, q, q_axis, out_axis, n_keys):
    if n_keys < DENSE_KEY_LIMIT:
        return core(q)
    t_q = q.shape[q_axis]
    nb = t_q // Q_BLOCK
    qb = q.reshape(q.shape[:q_axis] + (nb, Q_BLOCK) + q.shape[q_axis + 1:])
    ob = jnp.moveaxis(lax.map(core, jnp.moveaxis(qb, q_axis, 0)), 0, out_axis)
    return ob.reshape(ob.shape[:out_axis] + (t_q,) + ob.shape[out_axis + 2:])


def diff_attention(q, k, v, lam, lam_init, subln):
    scale = q.shape[-1] ** -0.5

    def core(qb):
        s = jnp.einsum('bhmqd,bhmkd->bhmqk', qb, k).astype(jnp.float32) * scale
        p = jax.nn.softmax(s, axis=-1)
        pd = (p[:, :, 0] - lam * p[:, :, 1]).astype(v.dtype)
        return rmsnorm(jnp.einsum('bhqk,bhkd->bhqd', pd, v), subln) * (1.0 - lam_init)

    return sweep_queries(core, q, 3, 2, k.shape[3])


def full_attention(q, k, v, sink):
    scale = q.shape[-1] ** -0.5

    def core(qb):
        s = jnp.einsum('bgmqd,bgkd->bgmqk', qb, k).astype(jnp.float32) * scale
        if sink is None:
            p = jax.nn.softmax(s, axis=-1)
        else:
            sk = jnp.broadcast_to(sink.astype(jnp.float32).reshape(1, s.shape[1], s.shape[2], 1, 1), s.shape[:-1] + (1,))
            p = jax.nn.softmax(jnp.concatenate([s, sk], axis=-1), axis=-1)[..., :-1]
        return jnp.einsum('bgmqk,bgkd->bgmqd', p.astype(v.dtype), v)

    return sweep_queries(core, q, 3, 3, k.shape[2])


def neighbourhood_attention(q, k, v, kc, vc, rpb):
    bn, h, t, dh = q.shape
    rows = t // GRID_W
    wr = min(NA_WIN_R, rows)
    wc = NA_WIN_C
    scale = dh ** -0.5
    qg = q.reshape(bn, h, rows, GRID_W, dh)
    r = jnp.arange(rows)
    row_idx = jnp.clip(r - wr // 2, 0, rows - wr)[:, None] + jnp.arange(wr)
    k_rows = k.reshape(bn, h, rows, GRID_W, dh)[:, :, row_idx]
    v_rows = v.reshape(bn, h, rows, GRID_W, dh)[:, :, row_idx]
    col = jnp.arange(GRID_W)
    col_start = jnp.clip(col - wc // 2, 0, GRID_W - wc)
    col_ok = (col[None, :] >= col_start[:, None]) & (col[None, :] < col_start[:, None] + wc)
    dr = row_idx - r[:, None] + NA_WIN_R - 1
    dc = jnp.clip(col[None, :] - col[:, None] + NA_WIN_C - 1, 0, 2 * NA_WIN_C - 2)
    bias = rpb[:, dr[:, None, :, None], dc[None, :, None, :]].astype(jnp.float32)
    s = jnp.einsum('bhrqd,bhrjkd->bhrqjk', qg, k_rows).astype(jnp.float32) * scale + bias[None]
    s = jnp.where(col_ok[:, None, :], s, NEG).reshape(bn, h, rows, GRID_W, wr * GRID_W)
    sc = jnp.einsum('bhrqd,bhcd->bhrqc', qg, kc).astype(jnp.float32) * scale
    p = jax.nn.softmax(jnp.concatenate([s, sc], axis=-1), axis=-1)
    n_lat = wr * GRID_W
    p_lat = p[..., :n_lat].reshape(bn, h, rows, GRID_W, wr, GRID_W).astype(v.dtype)
    o = jnp.einsum('bhrqjk,bhrjkd->bhrqd', p_lat, v_rows) + jnp.einsum('bhrqc,bhcd->bhrqd', p[..., n_lat:].astype(v.dtype), vc)
    return o.reshape(bn, h, t, dh)


def window_attention(q, k, v, kc, vc, sink):
    bn, g, m, t, dh = q.shape
    nb = t // SWA_BLOCK
    scale = dh ** -0.5
    qb = q.reshape(bn, g, m, nb, SWA_BLOCK, dh)

    def band(x):
        xb = jnp.pad(x, ((0, 0), (0, 0), (SWA_BLOCK, SWA_BLOCK), (0, 0))).reshape(bn, g, nb + 2, SWA_BLOCK, dh)
        return jnp.concatenate([xb[:, :, 0:nb], xb[:, :, 1:nb + 1], xb[:, :, 2:nb + 2]], axis=3)

    kb, vb = band(k), band(v)
    blk = jnp.arange(nb)
    qpos = blk[:, None] * SWA_BLOCK + jnp.arange(SWA_BLOCK)
    kpos = (blk[:, None] - 1) * SWA_BLOCK + jnp.arange(3 * SWA_BLOCK)
    ok = (jnp.abs(qpos[:, :, None] - kpos[:, None, :]) <= SWA_WINDOW) & (kpos[:, None, :] >= 0) & (kpos[:, None, :] < t)
    s = jnp.einsum('bgmnqd,bgnkd->bgmnqk', qb, kb).astype(jnp.float32) * scale
    s = jnp.where(ok, s, NEG)
    sc = jnp.einsum('bgmnqd,bgcd->bgmnqc', qb, kc).astype(jnp.float32) * scale
    sk = jnp.broadcast_to(sink.astype(jnp.float32).reshape(1, g, m, 1, 1, 1), s.shape[:-1] + (1,))
    p = jax.nn.softmax(jnp.concatenate([s, sc, sk], axis=-1), axis=-1)
    n_lat = 3 * SWA_BLOCK
    n_ctx = kc.shape[2]
    o = jnp.einsum('bgmnqk,bgnkd->bgmnqd', p[..., :n_lat].astype(v.dtype), vb) + jnp.einsum('bgmnqc,bgcd->bgmnqd', p[..., n_lat:n_lat + n_ctx].astype(v.dtype), vc)
    return o.reshape(bn, g, m, t, dh)


def even_mixer(h, w_in, conv_w, lam, lam_init, subln, w_out, ctx_kv):
    bn, t, _ = h.shape
    a_b, a_c, a_x, q, k, v = jnp.split(h @ w_in, [MIX_A, 2 * MIX_A, 3 * MIX_A, 3 * MIX_A + W_B, 3 * MIX_A + 2 * W_B], axis=-1)
    y_a = a_b * short_conv(a_c * a_x, conv_w)
    q = q.reshape(bn, t, 2 * H_B, DH_B)
    k = k.reshape(bn, t, 2 * H_B, DH_B)
    if ctx_kv is not None:
        q, k = axial_rope(q), axial_rope(k)
    q = q.reshape(bn, t, H_B, 2, DH_B).transpose(0, 2, 3, 1, 4)
    k = k.reshape(bn, t, H_B, 2, DH_B).transpose(0, 2, 3, 1, 4)
    v = to_heads(v, H_B)
    if ctx_kv is None:
        k_all, v_all = k, v
    else:
        k_all = jnp.concatenate([k, ctx_kv[0]], axis=3)
        v_all = jnp.concatenate([v, ctx_kv[1]], axis=2)
    y_b = diff_attention(q, k_all, v_all, lam, lam_init, subln)
    y = jnp.concatenate([y_a, from_heads(y_b)], axis=-1) @ w_out
    return y, (k, v)


def odd_mixer(h, w_in, rpb, sink, w_out, ctx_kv):
    bn, t, _ = h.shape
    cq, ck, cv, dq, dk, dv = jnp.split(h @ w_in, [W_C, 2 * W_C, 3 * W_C, 3 * W_C + W_D, 3 * W_C + W_D + KV_D * DH_D], axis=-1)
    cq, ck, cv = to_heads(cq, H_C), to_heads(ck, H_C), to_heads(cv, H_C)
    dq = dq.reshape(bn, t, H_D, DH_D)
    dk = dk.reshape(bn, t, KV_D, DH_D)
    dv = to_heads(dv, KV_D)
    if ctx_kv is None:
        dk = dk.transpose(0, 2, 1, 3)
        y_c = full_attention(cq[:, :, None], ck, cv, None)[:, :, 0]
        y_d = full_attention(dq.transpose(0, 2, 1, 3).reshape(bn, KV_D, H_D // KV_D, t, DH_D), dk, dv, sink)
    else:
        ck_ctx, cv_ctx, dk_ctx, dv_ctx = ctx_kv
        y_c = neighbourhood_attention(cq, ck, cv, ck_ctx, cv_ctx, rpb)
        dq = axial_rope(dq)
        dk = axial_rope(dk).transpose(0, 2, 1, 3)
        y_d = window_attention(dq.transpose(0, 2, 1, 3).reshape(bn, KV_D, H_D // KV_D, t, DH_D), dk, dv, dk_ctx, dv_ctx, sink)
    y_d = y_d.reshape(bn, H_D, t, DH_D)
    y = jnp.concatenate([from_heads(y_c), from_heads(y_d)], axis=-1) @ w_out
    return y, (ck, cv, dk, dv)


def setup_inputs(seed: int = 0) -> dict:
    key = jax.random.key(seed)
    ks = iter(jax.random.split(key, 40))

    def nrm(shape, scale):
        return jax.random.normal(next(ks), shape, jnp.float32) * scale

    def gain(shape):
        return 1.0 + nrm(shape, 0.05)

    return {
        'x_prompt': nrm((BATCH, SEQ, D_MODEL), 1.0),
        'x_sample': nrm((DEC_BATCH, DEC_SEQ, D_MODEL), 1.0),
        'cache_diff_k': nrm((DEC_BATCH, N_EVEN, H_B, 2, PAST_LEN, DH_B), 1.0),
        'cache_diff_v': nrm((DEC_BATCH, N_EVEN, H_B, PAST_LEN, 2 * DH_B), 1.0),
        'cache_na_k': nrm((DEC_BATCH, N_ODD, H_C, PAST_LEN, DH_C), 1.0),
        'cache_na_v': nrm((DEC_BATCH, N_ODD, H_C, PAST_LEN, DH_C), 1.0),
        'cache_swa_k': nrm((DEC_BATCH, N_ODD, KV_D, PAST_LEN, DH_D), 1.0),
        'cache_swa_v': nrm((DEC_BATCH, N_ODD, KV_D, PAST_LEN, DH_D), 1.0),
        'c': nrm((DEC_BATCH, D_MODEL), 1.0),
        'c_ctx': nrm((D_MODEL,), 1.0),
        'mod_w': nrm((DEPTH, D_MODEL, 6 * D_MODEL), D_MODEL ** -0.5),
        'mod_b': nrm((DEPTH, 6 * D_MODEL), 0.02),
        'norm_mix_pre': gain((DEPTH, D_MODEL)),
        'norm_mix_post': gain((DEPTH, D_MODEL)),
        'norm_mlp_pre': gain((DEPTH, D_MODEL)),
        'norm_mlp_post': gain((DEPTH, D_MODEL)),
        'w_in_even': nrm((N_EVEN, D_MODEL, IN_EVEN), D_MODEL ** -0.5),
        'conv_w': nrm((N_EVEN, CONV_W, MIX_A), CONV_W ** -0.5),
        'lambda_q1': nrm((N_EVEN, DH_B), 0.1),
        'lambda_k1': nrm((N_EVEN, DH_B), 0.1),
        'lambda_q2': nrm((N_EVEN, DH_B), 0.1),
        'lambda_k2': nrm((N_EVEN, DH_B), 0.1),
        'subln': gain((N_EVEN, 2 * DH_B)),
        'w_in_odd': nrm((N_ODD, D_MODEL, IN_ODD), D_MODEL ** -0.5),
        'rpb': nrm((N_ODD, H_C, 2 * NA_WIN_R - 1, 2 * NA_WIN_C - 1), 0.1),
        'sink': nrm((N_ODD, H_D), 0.5),
        'w_out': nrm((DEPTH, MIX_WIDTH, D_MODEL), MIX_WIDTH ** -0.5),
        'mlp_w1': nrm((DEPTH, D_MODEL, D_FF), D_MODEL ** -0.5),
        'mlp_w2': nrm((DEPTH, D_FF, D_MODEL), D_FF ** -0.5),
    }


def reference(x_prompt, x_sample, cache_diff_k, cache_diff_v, cache_na_k, cache_na_v, cache_swa_k, cache_swa_v, c, c_ctx, mod_w, mod_b, norm_mix_pre, norm_mix_post, norm_mlp_pre, norm_mlp_post, w_in_even, conv_w, lambda_q1, lambda_k1, lambda_q2, lambda_k2, subln, w_in_odd, rpb, sink, w_out, mlp_w1, mlp_w2):
    def layer(x, cond, li, ctx_kv):
        mod = (jax.nn.silu(cond) @ mod_w[li] + mod_b[li])[:, None, :]
        sh1, sc1, g1, sh2, sc2, g2 = jnp.split(mod, 6, axis=-1)
        h = rmsnorm(x, norm_mix_pre[li]) * (1.0 + sc1) + sh1
        if li % 2 == 0:
            e = li // 2
            lam_init = 0.8 - 0.6 * math.exp(-0.3 * li)
            lam = (jnp.exp(jnp.sum(lambda_q1[e].astype(jnp.float32) * lambda_k1[e].astype(jnp.float32)))
                   - jnp.exp(jnp.sum(lambda_q2[e].astype(jnp.float32) * lambda_k2[e].astype(jnp.float32))) + lam_init)
            y, kv = even_mixer(h, w_in_even[e], conv_w[e], lam, lam_init, subln[e], w_out[li], ctx_kv)
        else:
            o = li // 2
            y, kv = odd_mixer(h, w_in_odd[o], rpb[o], sink[o], w_out[li], ctx_kv)
        x = x + g1 * rmsnorm(y, norm_mix_post[li])
        h = rmsnorm(x, norm_mlp_pre[li]) * (1.0 + sc2) + sh2
        y = jnp.square(jax.nn.relu(h @ mlp_w1[li])) @ mlp_w2[li]
        x = x + g2 * rmsnorm(y, norm_mlp_post[li])
        return x, kv

    xp = x_prompt
    diff_k, diff_v, na_k, na_v, swa_k, swa_v = [], [], [], [], [], []
    for li in range(DEPTH):
        xp, kv = layer(xp, c_ctx[None, :], li, None)
        if li % 2 == 0:
            diff_k.append(kv[0])
            diff_v.append(kv[1])
        else:
            na_k.append(kv[0])
            na_v.append(kv[1])
            swa_k.append(kv[2])
            swa_v.append(kv[3])
    y_prompt = xp
    new_diff_k = jnp.stack(diff_k, axis=1)
    new_diff_v = jnp.stack(diff_v, axis=1)
    new_na_k = jnp.stack(na_k, axis=1)
    new_na_v = jnp.stack(na_v, axis=1)
    new_swa_k = jnp.stack(swa_k, axis=1)
    new_swa_v = jnp.stack(swa_v, axis=1)

    xs = x_sample
    for li in range(DEPTH):
        if li % 2 == 0:
            e = li // 2
            ctx = (cache_diff_k[:, e], cache_diff_v[:, e])
        else:
            o = li // 2
            ctx = (cache_na_k[:, o], cache_na_v[:, o], cache_swa_k[:, o], cache_swa_v[:, o])
        xs, _ = layer(xs, c, li, ctx)
    y_sample = xs

    return (y_prompt, y_sample, new_diff_k, new_diff_v, new_na_k, new_na_v, new_swa_k, new_swa_v)
```

```python
import math
import os
SKIP = set(os.environ.get('DBG_SKIP', '').split(','))
DBG_CORES = int(os.environ.get('DBG_CORES', '8'))
from contextlib import ExitStack
import numpy as np
import concourse.bass as bass
import concourse.mybir as mybir
from concourse.bass_utils import run_bass_kernel_spmd

F32 = mybir.dt.float32
BF16 = mybir.dt.bfloat16
AF = mybir.ActivationFunctionType
ALU = mybir.AluOpType
AX = mybir.AxisListType
EPS = 1e-6
NCORES = 8


class Prog:
    def __init__(self, nc):
        self.nc = nc
        self.ops = []
        self.last_w = {}
        self.readers = {}
        self.rot = {}

    def nxt(self, name, n, base=0):
        v = self.rot.get(name, 0)
        self.rot[name] = v + 1
        return base + (v % n)

    def op(self, eng, fn, reads=(), writes=(), dma=False, semkey=None):
        idx = len(self.ops)
        raw = set()
        oth = set()
        for k in reads:
            w = self.last_w.get(k)
            if w is not None:
                raw.add(w)
            if isinstance(k, tuple) and k[0] == 'ps':
                for r in self.readers.get(k, ()):
                    oth.add(r)
        for k in writes:
            w = self.last_w.get(k)
            if w is not None:
                oth.add(w)
            for r in self.readers.get(k, ()):
                oth.add(r)
        for k in reads:
            self.readers.setdefault(k, []).append(idx)
        for k in writes:
            self.last_w[k] = idx
            self.readers[k] = []
        raw.discard(idx)
        oth.discard(idx)
        self.ops.append(dict(eng=eng, fn=fn, raw=raw, oth=oth - raw, dma=dma, semkey=semkey, sig=False))
        return idx

    def emit(self, final_wait_ops=()):
        nc = self.nc
        ops = self.ops
        for i, o in enumerate(ops):
            need = set(o['raw'])
            for d in o['oth']:
                p = ops[d]
                if p['eng'] == o['eng'] and not p['dma'] and not o['dma']:
                    continue
                need.add(d)
            o['need'] = need
            for d in need:
                ops[d]['sig'] = True
        for d in final_wait_ops:
            ops[d]['sig'] = True
        for o in ops:
            if o['dma']:
                o['sig'] = True
        stack = ExitStack()
        sems = {}
        cnt = {}
        rot = {}
        ROT = 12
        for i, o in enumerate(ops):
            if not o['sig']:
                continue
            if o['dma']:
                if o['semkey'] is not None:
                    name = 'dk_%s' % (o['semkey'],)
                else:
                    r = rot.get(o['eng'], 0)
                    rot[o['eng']] = r + 1
                    name = 'dr_%s_%d' % (o['eng'], r % ROT)
                cnt[name] = cnt.get(name, 0) + 16
            else:
                name = 'c_%s' % o['eng']
                cnt[name] = cnt.get(name, 0) + 1
            o['sem'] = name
            o['val'] = cnt[name]
            if name not in sems:
                sems[name] = stack.enter_context(nc.semaphore(name.replace(' ', '').replace("'", '').replace(',', '_').replace('(', '').replace(')', '')))
        per_eng = {e: [] for e in ['pe', 'act', 'dve', 'pool', 'sp']}
        for i, o in enumerate(ops):
            per_eng[o['eng']].append(i)
        final_wait_ops = list(final_wait_ops)

        def emit_eng(engname, engobj):
            waited = {}
            for i in per_eng[engname]:
                o = ops[i]
                w = {}
                for d in o['need']:
                    p = ops[d]
                    nm, v = p['sem'], p['val']
                    if waited.get(nm, 0) >= v:
                        continue
                    if w.get(nm, 0) < v:
                        w[nm] = v
                for nm, v in w.items():
                    engobj.wait_ge(sems[nm], v)
                    waited[nm] = v
                if o['dma'] and o['sig']:
                    pv = o['val'] - 16
                    if pv > 0 and waited.get(o['sem'], 0) < pv:
                        engobj.wait_ge(sems[o['sem']], pv)
                        waited[o['sem']] = pv
                ins = o['fn'](engobj)
                if o['sig']:
                    ins.then_inc(sems[o['sem']], 16 if o['dma'] else 1)
            if engname == 'sp':
                w = {}
                for d in final_wait_ops:
                    p = ops[d]
                    nm, v = p['sem'], p['val']
                    if w.get(nm, 0) < v:
                        w[nm] = v
                for nm, v in w.items():
                    if waited.get(nm, 0) < v:
                        engobj.wait_ge(sems[nm], v)

        with stack:
            with nc.Block() as block:
                @block.tensor
                def _(e):
                    emit_eng('pe', e)

                @block.scalar
                def _(e):
                    emit_eng('act', e)

                @block.vector
                def _(e):
                    emit_eng('dve', e)

                @block.gpsimd
                def _(e):
                    emit_eng('pool', e)

                @block.sync
                def _(e):
                    emit_eng('sp', e)


class StopBuild(Exception):
    pass


class TT:
    def __init__(self, h, free):
        self.h = h
        self.F = free

    def ap(self, off, dims, p0=0, pn=128):
        return bass.AP(self.h, p0 * self.F + off, [[self.F, pn]] + [list(d) for d in dims])


def build(stop_after=None):
    nc = bass.Bass("TRN2", target_bir_lowering=False)
    D = {}

    def din(name, shape):
        D[name] = nc.dram_tensor(name, list(shape), F32, kind="ExternalInput")

    def dout(name, shape):
        D[name] = nc.dram_tensor(name, list(shape), F32, kind="ExternalOutput")

    din('xP', [1024, 1024]); din('xS', [1024, 1024])
    din('cdk', [8, 256, 64]); din('cdv', [4, 256, 128])
    din('cnk', [8, 256, 64]); din('cnv', [8, 256, 64])
    din('csk', [2, 256, 64]); din('csv', [2, 256, 64])
    din('cvec', [2, 1024])
    din('mod_w', [2048, 6144]); din('mod_b', [2, 6144])
    din('gains', [4, 2048])
    din('w_in_even', [1024, 3072]); din('conv_w', [3, 512]); din('lams', [4, 64]); din('subln', [128, 1])
    din('w_in_odd', [1024, 2304]); din('rpbT', [128, 8 * 14 * 64]); din('sink', [1, 8])
    din('w_out', [2048, 1024]); din('mlp_w1', [2048, 4096]); din('mlp_w2', [8192, 1024])
    din('c_ident', [128, 128]); din('c_R2', [128, 128]); din('c_cos', [128, 1024]); din('c_sin', [128, 1024])
    din('c_mprev', [128, 128]); din('c_mnext', [128, 128]); din('c_colok', [128, 64]); din('c_colneg', [128, 64])
    dout('yP', [1024, 1024]); dout('yS', [1024, 1024])
    dout('ndk', [4, 8, 256, 64]); dout('ndv', [4, 4, 256, 128])
    dout('nnk', [4, 8, 256, 64]); dout('nnv', [4, 8, 256, 64])
    dout('nsk', [4, 2, 256, 64]); dout('nsv', [4, 2, 256, 64])

    es = ExitStack()

    def sb(name, free, dt):
        return TT(es.enter_context(nc.sbuf_tensor('s_' + name, [128, free], dt)), free)

    with es:
        xT = sb('xT', 8192, F32)
        hT = sb('hT', 8192, BF16)
        yb = sb('yb', 8192, BF16)
        wr = sb('wr', 2 * 8192, BF16)
        ar = sb('ar', 32768, BF16)
        bank = sb('bank', 7168, BF16)
        tmp = sb('tmp', 6 * 512, F32)
        Et = sb('Et', 4 * 512, BF16)
        sqt = sb('sqt', 4 * 512, BF16)
        rstd = sb('rstd', 2 * 512, F32)
        ident = sb('ident', 128, F32)
        ones = sb('ones', 128, BF16)
        R2 = sb('R2', 128, BF16)
        cosT = sb('cosT', 1024, BF16)
        sinT = sb('sinT', 1024, BF16)
        mprev = sb('mprev', 128, BF16)
        mnext = sb('mnext', 128, BF16)
        colok = sb('colok', 64, F32)
        colneg = sb('colneg', 64, F32)
        identb = sb('identb', 128, BF16)
        cT = sb('cT', 16, F32)
        sT = sb('sT', 16, BF16)
        modb = sb('modb', 96, F32)
        modv = sb('modv', 192, F32)
        gains = sb('gains', 64, F32)
        coef = sb('coef', 128, F32)
        cw = sb('cw', 12, F32)
        esink = sb('esink', 8, F32)
        lamt = sb('lamt', 256, F32)
        lams = sb('lams', 8, F32)
        subln = sb('subln', 2, F32)
        ps_all = TT(es.enter_context(nc.psum_tensor('ps_all', [128, 4096], F32)), 4096)

        P = Prog(nc)

        def MM(out, lhsT, rhs, start, stop, reads, writes, skip=False):
            if skip:
                P.op('pe', lambda e: e.matmul(out, lhsT, rhs, start=start, stop=stop, skip_group_check=True), reads=reads, writes=writes)
            else:
                P.op('pe', lambda e: e.matmul(out, lhsT, rhs, start=start, stop=stop), reads=reads, writes=writes)

        def TR(out, in_, reads, writes):
            P.op('pe', lambda e: e.transpose(out, in_, ident.ap(0, [[1, 128]])), reads=list(reads) + ['ident'], writes=writes)

        def ACTV(out, in_, func, reads, writes, bias=None, scale=None):
            kw = {}
            if bias is not None:
                kw['bias'] = bias
            if scale is not None:
                kw['scale'] = scale
            P.op('act', lambda e: e.activation(out, in_, func, **kw), reads=reads, writes=writes)

        def TTO(eng, out, in0, in1, op, reads, writes):
            P.op(eng, lambda e: e.tensor_tensor(out=out, in0=in0, in1=in1, op=op), reads=reads, writes=writes)

        def STT(eng, out, in0, scalar, in1, op0, op1, reads, writes):
            P.op(eng, lambda e: e.scalar_tensor_tensor(out=out, in0=in0, scalar=scalar, in1=in1, op0=op0, op1=op1), reads=reads, writes=writes)

        def TS(eng, out, in0, s1, s2, op0, op1, reads, writes):
            if s2 is None:
                P.op(eng, lambda e: e.tensor_scalar(out=out, in0=in0, scalar1=s1, scalar2=None, op0=op0), reads=reads, writes=writes)
            else:
                P.op(eng, lambda e: e.tensor_scalar(out=out, in0=in0, scalar1=s1, scalar2=s2, op0=op0, op1=op1), reads=reads, writes=writes)

        def CP(eng, out, in_, reads, writes):
            if eng == 'act':
                P.op('act', lambda e: e.activation(out, in_, AF.Copy), reads=reads, writes=writes)
            else:
                P.op(eng, lambda e: e.tensor_copy(out, in_), reads=reads, writes=writes)

        def RECIP(out, in_, reads, writes):
            P.op('dve', lambda e: e.reciprocal(out, in_), reads=reads, writes=writes)

        def DMA(q, out, in_, reads, writes, semkey=None, slow=False):
            if slow:
                return P.op(q, lambda e: e.dma_start(out=out, in_=in_, allow_slow_non_contiguous=True), reads=reads, writes=writes, dma=True, semkey=semkey)
            return P.op(q, lambda e: e.dma_start(out=out, in_=in_), reads=reads, writes=writes, dma=True, semkey=semkey)

        def psap(b, off, dims, p0=0, pn=128):
            return ps_all.ap(b * 512 + off, dims, p0, pn)

        def bankA():
            return P.nxt('bA', 4, 0)

        def bankB():
            return P.nxt('bB', 4, 4)

        def bankSCpair():
            return P.nxt('bSCp', 2) * 2

        def bankACC():
            return P.nxt('bACC', 4, 4)

        def tmpi():
            return P.nxt('tmp', 6)

        def Ei():
            return P.nxt('E', 4)

        def sqi():
            return P.nxt('sq', 4)

        def evq():
            return ['act', 'dve'][P.nxt('evq', 2)]

        out_ops = []
        cmap = {}
        marks = []

        def mark(name):
            marks.append((name, sum(1 for o in P.ops if o['eng'] == 'pe')))

        def setup():
            DMA('sp', ident.ap(0, [[1, 128]]), D['c_ident'].ap(), [], ['ident'])
            P.op('dve', lambda e: e.memset(ones.ap(0, [[1, 128]]), 1.0), writes=['ones'])
            DMA('pool', R2.ap(0, [[1, 128]]), D['c_R2'].ap(), [], ['R2'])
            DMA('pool', cosT.ap(0, [[1, 1024]]), D['c_cos'].ap(), [], ['cos'])
            DMA('pool', sinT.ap(0, [[1, 1024]]), D['c_sin'].ap(), [], ['sin'])
            DMA('pool', mprev.ap(0, [[1, 128]]), D['c_mprev'].ap(), [], ['mprev'])
            DMA('pool', mnext.ap(0, [[1, 128]]), D['c_mnext'].ap(), [], ['mnext'])
            DMA('sp', colok.ap(0, [[1, 64]]), D['c_colok'].ap(), [], ['colok'])
            DMA('sp', colneg.ap(0, [[1, 64]]), D['c_colneg'].ap(), [], ['colneg'])
            CP('dve', identb.ap(0, [[1, 128]]), ident.ap(0, [[1, 128]]), ['ident'], ['identb'])
            for j in range(2):
                DMA('sp', cT.ap(j, [[2, 8]]), bass.AP(D['cvec'], j * 1024, [[1, 128], [128, 8]]), [], ['cT'], slow=True)
                DMA('sp', modb.ap(j * 48, [[1, 48]]), bass.AP(D['mod_b'], j * 6144, [[1, 128], [128, 48]]), [], ['modb'], slow=True)
            for w in range(4):
                DMA('sp', gains.ap(w * 16, [[1, 16]]), bass.AP(D['gains'], w * 2048, [[1, 128], [128, 16]]), [], ['gains'], slow=True)
            for j in range(3):
                DMA('sp', cw.ap(j, [[3, 4]]), bass.AP(D['conv_w'], j * 512, [[1, 128], [128, 4]]), [], ['cw'], slow=True)
            DMA('sp', subln.ap(0, [[1, 1]]), D['subln'].ap(), [], ['subln0'])
            DMA('sp', esink.ap(0, [[1, 8]]), bass.AP(D['sink'], 0, [[0, 128], [1, 8]]), [], ['esink0'])
            DMA('sp', lamt.ap(0, [[1, 256]]), bass.AP(D['lams'], 0, [[0, 128], [1, 256]]), [], ['lamt'])
            ACTV(esink.ap(0, [[1, 8]]), esink.ap(0, [[1, 8]]), AF.Exp, ['esink0'], ['esink'])
            ACTV(sT.ap(0, [[1, 16]]), cT.ap(0, [[1, 16]]), AF.Silu, ['cT'], ['sT'])
            TTO('dve', lamt.ap(0, [[1, 64]]), lamt.ap(0, [[1, 64]]), lamt.ap(64, [[1, 64]]), ALU.mult, ['lamt'], ['lamt1'])
            TTO('dve', lamt.ap(128, [[1, 64]]), lamt.ap(128, [[1, 64]]), lamt.ap(192, [[1, 64]]), ALU.mult, ['lamt'], ['lamt2'])
            P.op('dve', lambda e: e.reduce_sum(lams.ap(0, [[1, 1]]), lamt.ap(0, [[1, 64]]), axis=AX.X), reads=['lamt1'], writes=['lams0'])
            P.op('dve', lambda e: e.reduce_sum(lams.ap(1, [[1, 1]]), lamt.ap(128, [[1, 64]]), axis=AX.X), reads=['lamt2'], writes=['lams1'])
            ACTV(lams.ap(2, [[1, 2]]), lams.ap(0, [[1, 2]]), AF.Exp, ['lams0', 'lams1'], ['lams2'])
            TTO('dve', lams.ap(4, [[1, 1]]), lams.ap(3, [[1, 1]]), lams.ap(2, [[1, 1]]), ALU.subtract, ['lams2'], ['lams4'])
            TS('dve', lams.ap(4, [[1, 1]]), lams.ap(4, [[1, 1]]), -0.2, None, ALU.add, None, ['lams4'], ['neglam'])
            TS('dve', subln.ap(1, [[1, 1]]), subln.ap(0, [[1, 1]]), 0.8, None, ALU.mult, None, ['subln0'], ['subln'])
            for hp in range(4):
                DMA('sp', tmp.ap(0, [[1, 1792]]), D['rpbT'].ap()[:, hp * 1792:(hp + 1) * 1792], [], [('tmp', i) for i in range(4)])
                ACTV(tmp.ap(0, [[1, 1792]]), tmp.ap(0, [[1, 1792]]), AF.Copy, [('tmp', i) for i in range(4)], [('tmp', i) for i in range(4)], scale=8.0)
                TTO('dve', tmp.ap(0, [[64, 28], [1, 64]]), tmp.ap(0, [[64, 28], [1, 64]]), colok.ap(0, [[0, 28], [1, 64]]), ALU.mult,
                    [('tmp', i) for i in range(4)] + ['colok'], [('tmp', i) for i in range(4)])
                TTO('dve', bank.ap(hp * 1792, [[64, 28], [1, 64]]), tmp.ap(0, [[64, 28], [1, 64]]), colneg.ap(0, [[0, 28], [1, 64]]), ALU.add,
                    [('tmp', i) for i in range(4)] + ['colneg'], ['bank'])

        class WV:
            def __init__(self, slot, off, stride):
                self.slot = slot
                self.base = slot * 8192 + off
                self.stride = stride
                self.key = ('w', slot)

        def wload(src_ap, dims, extra=None, stride=512):
            sl = P.nxt('wslot', 2)
            DMA('pool', wr.ap(sl * 8192, dims), src_ap, [], [('w', sl)], semkey='w%d' % sl)
            if extra:
                for (off, dims2, src2) in extra:
                    DMA('pool', wr.ap(sl * 8192 + off, dims2), src2, [], [('w', sl)], semkey='w%d' % sl)
            return WV(sl, 0, stride)

        wcache = {}

        def wpiece(name, row0, col0):
            c1 = (col0 // 1024) * 1024
            key = (name, row0, c1)
            if wcache.get(name, (None, None))[0] != key:
                src = D[name].ap()[row0:row0 + 1024, c1:c1 + 1024].rearrange("(kc p) c -> p kc c", p=128)
                wcache[name] = (key, wload(src, [[1024, 8], [1, 1024]], stride=1024))
            wv = wcache[name][1]
            return WV(wv.slot, col0 - c1, 1024)

        def mod_piece(li, pc):
            b = bankA()
            s = wpiece('mod_w', li * 1024, pc * 512)
            for mc in range(4):
                for kc in range(8):
                    MM(psap(b, mc * 2, [[1, 2]]), wr.ap(s.base + kc * s.stride + mc * 128, [[1, 128]]), sT.ap(kc * 2, [[1, 2]]),
                       kc == 0, kc == 7, [s.key, 'sT'], [('ps', b)])
            TTO('dve', modv.ap(li * 96 + pc * 8, [[2, 4], [1, 2]]), psap(b, 0, [[2, 4], [1, 2]]), modb.ap(li * 48 + pc * 4, [[1, 4], [0, 2]]), ALU.add,
                [('ps', b), 'modb'], [('modv', li)])
            if pc == 3:
                mod_finish(li, 0)
            if pc == 11:
                mod_finish(li, 1)

        def compute_mod(li):
            mark('mod l%d' % li)
            for pc in range(4):
                mod_piece(li, pc)

        modq = []

        def mod_more(n=1):
            for _ in range(n):
                if modq:
                    mod_piece(0, modq.pop(0))

        def mod_finish(li, part):
            def mv(c0):
                return modv.ap(li * 96 + c0 * 2, [[2, 8], [1, 2]])

            def gn(w):
                return gains.ap(w * 16 + li * 8, [[1, 8], [0, 2]])

            def cf(w):
                return coef.ap(li * 64 + w * 16, [[2, 8], [1, 2]])
            if part == 0:
                STT('dve', cf(0), mv(8), 1.0, gn(0), ALU.add, ALU.mult, [('modv', li), 'gains'], [('coef', li)])
                return
            TTO('dve', cf(1), mv(16), gn(1), ALU.mult, [('modv', li), 'gains'], [('coef', li)])
            STT('dve', cf(2), mv(32), 1.0, gn(2), ALU.add, ALU.mult, [('modv', li), 'gains'], [('coef', li)])
            TTO('dve', cf(3), mv(40), gn(3), ALU.mult, [('modv', li), 'gains'], [('coef', li)])

        def coefap(li, w, kc, j):
            return coef.ap(li * 64 + w * 16 + kc * 2 + j, [[1, 1]])

        def shap(li, which, kc, j):
            c0 = 0 if which == 1 else 24
            return modv.ap(li * 96 + (c0 + kc) * 2 + j, [[1, 1]])

        def load_x(u):
            mark('load_x u%d' % u)
            xd = D['xP'] if u == 0 else D['xS']
            for t in range(8):
                g = t // 4
                for hf in range(2):
                    ti = tmpi()
                    DMA('sp', tmp.ap(ti * 512, [[1, 512]]), xd.ap()[t * 128:(t + 1) * 128, hf * 512:(hf + 1) * 512], [], [('tmp', ti)])
                    b = bankA()
                    for c in range(4):
                        TR(psap(b, c * 128, [[1, 128]]), tmp.ap(ti * 512 + c * 128, [[1, 128]]), [('tmp', ti)], [('ps', b)])
                    CP(['dve', 'act'][g], xT.ap((hf * 4) * 1024 + t * 128, [[1024, 4], [1, 128]]), psap(b, 0, [[128, 4], [1, 128]]), [('ps', b)],
                       [('x', hf * 4 + c, g) for c in range(4)])

        def store_x(u):
            mark('store_x u%d' % u)
            run_tail()
            yd = D['yP'] if u == 0 else D['yS']
            for t in range(8):
                g = t // 4
                for hf in range(2):
                    b = bankA()
                    for c in range(4):
                        TR(psap(b, c * 128, [[1, 128]]), xT.ap((hf * 4 + c) * 1024 + t * 128, [[1, 128]]), [('x', hf * 4 + c, g)], [('ps', b)])
                    ti = tmpi()
                    CP(evq(), tmp.ap(ti * 512, [[1, 512]]), psap(b, 0, [[1, 512]]), [('ps', b)], [('tmp', ti)])
                    out_ops.append(DMA('sp', yd.ap()[t * 128:(t + 1) * 128, hf * 512:(hf + 1) * 512], tmp.ap(ti * 512, [[1, 512]]), [('tmp', ti)], []))

        def rstd_from(b, ri, scale):
            ACTV(rstd.ap(ri * 512, [[1, 512]]), psap(b, 0, [[1, 512]]), AF.Ln, [('ps', b), 'epsb'], [('rstd', ri)], bias=cmap['eps'], scale=scale)
            ACTV(rstd.ap(ri * 512, [[1, 512]]), rstd.ap(ri * 512, [[1, 512]]), AF.Exp, [('rstd', ri)], [('rstd', ri)], scale=-0.5)

        tailq = []

        def run_tail():
            while tailq:
                tailq.pop(0)()

        def norm_mod(u, li, which):
            mark('norm u%d l%d w%d' % (u, li, which))
            j = u
            w = 0 if which == 1 else 2
            for g in range(2):
                b = bankB()
                for kc in range(8):
                    si = sqi()
                    ACTV(sqt.ap(si * 512, [[1, 512]]), xT.ap(kc * 1024 + g * 512, [[1, 512]]), AF.Square, [('x', kc, g)], [('sq', si)])
                    MM(psap(b, 0, [[1, 512]]), ones.ap(0, [[1, 128]]), sqt.ap(si * 512, [[1, 512]]), kc == 0, kc == 7, [('sq', si), 'ones'], [('ps', b)])
                ri = P.nxt('rstd', 2)
                rstd_from(b, ri, 1.0 / 1024)
                for kc in range(8):
                    ti = tmpi()
                    STT('dve', tmp.ap(ti * 512, [[1, 512]]), xT.ap(kc * 1024 + g * 512, [[1, 512]]), coefap(li, w, kc, j), rstd.ap(ri * 512, [[1, 512]]),
                        ALU.mult, ALU.mult, [('x', kc, g), ('coef', li), ('rstd', ri)], [('tmp', ti)])
                    ACTV(hT.ap(kc * 1024 + g * 512, [[1, 512]]), tmp.ap(ti * 512, [[1, 512]]), AF.Identity, [('tmp', ti), ('modv', li)], [('h', kc, g)],
                         bias=shap(li, which, kc, j), scale=1.0)
                if g == 0:
                    run_tail()

        def dense_fm(s, mc, g, evac, kcn=8, wstride=512, src=None, srckey='h'):
            b = bankA()
            for kc in range(kcn):
                MM(psap(b, 0, [[1, 512]]), wr.ap(s.base + kc * s.stride + mc * 128, [[1, 128]]),
                   (src or hT).ap(kc * 1024 + g * 512, [[1, 512]]), kc == 0, kc == kcn - 1, [s.key, (srckey, kc, g)], [('ps', b)])
            evac(b)

        def dense_tm(s, col0, ncols, t, evac, wstride=512):
            b = bankA()
            g = t // 4
            for kc in range(8):
                MM(psap(b, 0, [[1, ncols]]), hT.ap(kc * 1024 + t * 128, [[1, 128]]), wr.ap(s.base + kc * s.stride + col0, [[1, ncols]]),
                   kc == 0, kc == 7, [s.key, ('h', kc, g)], [('ps', b)])
            evac(b)

        QT, KT = 0, 4096
        VT0, AB, UU = 9216, 14336, 18432
        Q2, K2, VT1, VT2, VD = 9216, 13312, 15872, 20992, 24576

        def rope_evac(b, g, dst_off, dst_keys):
            if 'rope' in SKIP:
                CP(evq(), ar.ap(dst_off, [[1, 512]]), psap(b, 0, [[1, 512]]), [('ps', b)], dst_keys)
                return
            ei = Ei()
            CP('act', Et.ap(ei * 512, [[1, 512]]), psap(b, 0, [[1, 512]]), [('ps', b)], [('E', ei)])
            b2 = bankA()
            MM(psap(b2, 0, [[1, 512]]), R2.ap(0, [[1, 128]]), Et.ap(ei * 512, [[1, 512]]), True, True, [('E', ei), 'R2'], [('ps', b2)])
            t1 = tmpi()
            TTO('dve', tmp.ap(t1 * 512, [[1, 512]]), psap(b, 0, [[1, 512]]), cosT.ap(g * 512, [[1, 512]]), ALU.mult, [('ps', b), 'cos'], [('tmp', t1)])
            t2 = tmpi()
            TTO('dve', tmp.ap(t2 * 512, [[1, 512]]), psap(b2, 0, [[1, 512]]), sinT.ap(g * 512, [[1, 512]]), ALU.mult, [('ps', b2), 'sin'], [('tmp', t2)])
            TTO('dve', ar.ap(dst_off, [[1, 512]]), tmp.ap(t1 * 512, [[1, 512]]), tmp.ap(t2 * 512, [[1, 512]]), ALU.add, [('tmp', t1), ('tmp', t2)], dst_keys)

        def load_ctx_k(name, nh, dst_off, dst_stride, nchunk, keyname, dup=False):
            for tt in range(2):
                ti = tmpi()
                if not dup:
                    DMA('sp', tmp.ap(ti * 512, [[64, nh], [1, 64]]), bass.AP(D[name], tt * 8192, [[64, 128], [16384, nh], [1, 64]]), [], [('tmp', ti)])
                else:
                    for dpl in range(2):
                        DMA('sp', tmp.ap(ti * 512 + dpl * 64, [[128, nh], [1, 64]]), bass.AP(D[name], tt * 8192, [[64, 128], [16384, nh], [1, 64]]), [], [('tmp', ti)])
                b = bankA()
                for c in range(nchunk):
                    TR(psap(b, c * 128, [[1, 128]]), tmp.ap(ti * 512 + c * 128, [[1, 128]]), [('tmp', ti)], [('ps', b)])
                CP('dve', ar.ap(dst_off + 1024 + tt * 128, [[dst_stride, nchunk], [1, 128]]), psap(b, 0, [[128, nchunk], [1, 128]]), [('ps', b)],
                   [(keyname, c, 'ctx') for c in range(nchunk)])

        class Pipe:
            DEPTH = 2

            def __init__(self):
                self.round = []
                self.pending = []
                self.late_prev = []
                self.late_new = []

            def add(self, item):
                if self.round and (self.round[0].get('sb') is None) != (item.get('sb') is None):
                    self.emit_round()
                self.round.append(item)
                if len(self.round) == 2:
                    self.emit_round()

            def emit_round(self):
                items = self.round
                self.round = []
                if not items:
                    return
                er = P.nxt('Er', 6)
                ebase = er * 1024
                ekey = [('yb', er // 4, (er % 4) * 2), ('yb', er // 4, (er % 4) * 2 + 1)]
                paired = items[0].get('sb') is not None
                bx = bankSCpair()
                by = bx + 1
                if paired:
                    offs = []
                    o = 0
                    for it in items:
                        offs.append(o)
                        o += it['nq']
                    tot = o
                    for k, (it, of) in enumerate(zip(items, offs)):
                        it['sa'](bx, of, k == 0)
                    for k, (it, of) in enumerate(zip(items, offs)):
                        it['sb'](by, of, k == 0)
                    for it, of in zip(items, offs):
                        if it.get('bias'):
                            it['bias'](bx, by, of)
                    ACTV(yb.ap(ebase, [[512, 2], [1, tot]]), psap(bx, 0, [[512, 2], [1, tot]]), AF.Exp, [('ps', bx), ('ps', by)], ekey, scale=0.125)
                    for it, of in zip(items, offs):
                        it['ea'] = ebase + of
                        it['eb'] = ebase + 512 + of
                else:
                    banks = [bx, by]
                    for k, it in enumerate(items):
                        it['sa'](banks[k], 0, True)
                    if len(items) == 2 and items[0]['nq'] == 512 and items[1]['nq'] == 512:
                        ACTV(yb.ap(ebase, [[1, 1024]]), psap(bx, 0, [[1, 1024]]), AF.Exp, [('ps', bx), ('ps', by)], ekey, scale=0.125)
                    else:
                        for k, it in enumerate(items):
                            ACTV(yb.ap(ebase + k * 512, [[1, it['nq']]]), psap(banks[k], 0, [[1, it['nq']]]), AF.Exp, [('ps', banks[k])], ekey, scale=0.125)
                    for k, it in enumerate(items):
                        it['ea'] = ebase + k * 512
                        it['eb'] = None

                def pend(items=items, ekey=ekey):
                    for it in items:
                        it['post'](it['ea'], it['eb'], ekey)
                self.pending.append(pend)
                while len(self.pending) > self.DEPTH:
                    self.pending.pop(0)()
                for f in self.late_prev:
                    f()
                self.late_prev = self.late_new
                self.late_new = []

            def flush(self):
                self.emit_round()
                while self.pending:
                    self.pending.pop(0)()
                    for f in self.late_prev:
                        f()
                    self.late_prev = self.late_new
                    self.late_new = []
                for f in self.late_prev + self.late_new:
                    f()
                self.late_prev = []
                self.late_new = []

        pipe = Pipe()

        def attn_pair(ktiles, qa, qb, qkeys, nq, dst_off, dst_keys, sink_h=None):
            bo = bankACC()
            bd = bankACC()
            nk = len(ktiles)

            def final():
                ti = tmpi()
                if sink_h is not None:
                    ACTV(tmp.ap(ti * 512, [[1, nq]]), psap(bd, 0, [[1, nq]]), AF.Ln, [('ps', bd), 'esink'], [('tmp', ti)], bias=esink.ap(sink_h, [[1, 1]]), scale=1.0)
                    ACTV(tmp.ap(ti * 512 + nq, [[1, nq]]), psap(bd, nq, [[1, nq]]), AF.Ln, [('ps', bd), 'esink'], [('tmp', ti)], bias=esink.ap(sink_h + 1, [[1, 1]]), scale=1.0)
                else:
                    ACTV(tmp.ap(ti * 512, [[1, 2 * nq]]), psap(bd, 0, [[1, 2 * nq]]), AF.Ln, [('ps', bd)], [('tmp', ti)])
                ACTV(tmp.ap(ti * 512, [[1, 2 * nq]]), tmp.ap(ti * 512, [[1, 2 * nq]]), AF.Exp, [('tmp', ti)], [('tmp', ti)], scale=-1.0)
                TTO('dve', hT.ap(dst_off, [[1, nq]], 0, 64), psap(bo, 0, [[1, nq]], 0, 64), tmp.ap(ti * 512, [[1, nq]], 0, 64), ALU.mult, [('ps', bo), ('tmp', ti)], dst_keys)
                TTO('dve', hT.ap(dst_off, [[1, nq]], 64, 64), psap(bo, nq, [[1, nq]], 64, 64), tmp.ap(ti * 512 + nq, [[1, nq]], 64, 64), ALU.mult, [('ps', bo), ('tmp', ti)], dst_keys)

            for i, kt in enumerate(ktiles):
                hasb = kt.get('ma') is not None

                def sa(b, of, first, kt=kt, hasb=hasb):
                    MM(psap(b, of, [[1, nq]]), kt['ka'], qa, first, not hasb, kt['kkeys'] + qkeys, [('ps', b)], skip=True)

                def sb(b, of, first, kt=kt, hasb=hasb):
                    MM(psap(b, of, [[1, nq]]), kt['kb'], qb, first, not hasb, kt['kkeys'] + qkeys, [('ps', b)], skip=True)

                bias = None
                if hasb:
                    def bias(bx, by, of, kt=kt):
                        MM(psap(bx, of, [[1, nq]]), identb.ap(0, [[1, 128]]), kt['ma'], False, True, kt['mkeys'] + ['identb'], [('ps', bx)], skip=True)
                        MM(psap(by, of, [[1, nq]]), identb.ap(0, [[1, 128]]), kt['mb'], False, True, kt['mkeys'] + ['identb'], [('ps', by)], skip=True)

                def post(ea, eb, ekey, i=i, kt=kt):
                    MM(psap(bo, 0, [[1, nq]]), kt['v'], yb.ap(ea, [[1, nq]]), i == 0, False, ekey + kt['vkeys'], [('ps', bo)], skip=True)
                    MM(psap(bo, nq, [[1, nq]]), kt['v'], yb.ap(eb, [[1, nq]]), False, i == nk - 1, ekey + kt['vkeys'], [('ps', bo)], skip=True)
                    MM(psap(bd, 0, [[nq, 2], [1, nq]]), ones.ap(0, [[1, 128]]), yb.ap(ea, [[eb - ea, 2], [1, nq]]), i == 0, i == nk - 1, ekey + ['ones'], [('ps', bd)])
                    if i == nk - 1:
                        final()
                pipe.add(dict(sa=sa, sb=sb, bias=bias, post=post, nq=nq))

        def diff_attn(u):
            mark('diff_attn u%d' % u)
            if u == 0:
                blocks = [(s * 256, 256, [s * 2, s * 2 + 1]) for s in range(4)]
            else:
                blocks = [(0, 512, list(range(10))), (512, 512, list(range(10)))]
            nblk = 0
            for h in range(4):
                for (q0, nq, kts) in blocks:
                    if u == 0:
                        if modq:
                            mod_more()
                        elif nblk - 4 < 12:
                            mod_piece(1, nblk - 4)
                    nblk += 1
                    g = q0 // 512
                    packed = (nq == 256)
                    ncols = 2 * nq if packed else nq
                    accs = []

                    def sublnfin(accs=accs, h=h, q0=q0, nq=nq, g=g, packed=packed):
                        tdk = P.nxt('tdslot', 2)
                        tdoff = (6 + tdk) * 1024
                        tdkey = [('yb', 1, 4 + 2 * tdk), ('yb', 1, 5 + 2 * tdk)]
                        if packed:
                            o0 = tmp.ap(accs[0] * 512, [[1, nq]]); o1 = tmp.ap(accs[0] * 512 + nq, [[1, nq]]); rk = [('tmp', accs[0])]
                        else:
                            o0 = tmp.ap(accs[0] * 512, [[1, nq]]); o1 = tmp.ap(accs[1] * 512, [[1, nq]]); rk = [('tmp', accs[0]), ('tmp', accs[1])]
                        STT('dve', yb.ap(tdoff, [[1, nq]]), o1, lams.ap(4, [[1, 1]]), o0, ALU.mult, ALU.add, rk + ['neglam'], tdkey)
                        si = sqi()
                        ACTV(sqt.ap(si * 512, [[1, nq]]), yb.ap(tdoff, [[1, nq]]), AF.Square, tdkey, [('sq', si)])

                        def stage2(si=si, tdoff=tdoff, tdkey=tdkey):
                            bs = bankACC()
                            MM(psap(bs, 0, [[1, nq]]), ones.ap(0, [[1, 128]]), sqt.ap(si * 512, [[1, nq]]), True, True, [('sq', si), 'ones'], [('ps', bs)])
                            ri = P.nxt('rstd', 2)
                            ACTV(rstd.ap(ri * 512, [[1, nq]]), psap(bs, 0, [[1, nq]]), AF.Ln, [('ps', bs), 'epsb'], [('rstd', ri)], bias=cmap['eps'], scale=1.0 / 128)
                            ACTV(rstd.ap(ri * 512, [[1, nq]]), rstd.ap(ri * 512, [[1, nq]]), AF.Exp, [('rstd', ri)], [('rstd', ri)], scale=-0.5)
                            STT('dve', hT.ap((4 + h) * 1024 + q0, [[1, nq]]), yb.ap(tdoff, [[1, nq]]), subln.ap(1, [[1, 1]]), rstd.ap(ri * 512, [[1, nq]]), ALU.mult, ALU.mult,
                                tdkey + [('rstd', ri), 'subln'], [('h', 4 + h, g)])
                        pipe.late_new.append(stage2)

                    maps = [None] if packed else [0, 1]
                    for m in maps:
                        bo = bankACC()
                        bd = bankACC()
                        nk = len(kts)
                        for i, kt in enumerate(kts):
                            kcol = kt * 128 if kt < 8 else 1024 + (kt - 8) * 128
                            kkey = [('k', h, kt // 4 if kt < 8 else 'ctx')]
                            vkey = [('v', kt)]

                            def smm(b, of, mm_, kcol=kcol, kkey=kkey, h=h, q0=q0, nq=nq, g=g):
                                MM(psap(b, of, [[1, nq]]), ar.ap(KT + h * 1280 + kcol, [[1, 128]], mm_ * 64, 64), ar.ap(QT + h * 1024 + q0, [[1, nq]], mm_ * 64, 64),
                                   True, True, kkey + [('q', h, g)], [('ps', b)])
                            if packed:
                                sa = lambda b, of, first, smm=smm: smm(b, of, 0)
                                sb = lambda b, of, first, smm=smm: smm(b, of, 1)
                            else:
                                sa = lambda b, of, first, smm=smm, m=m: smm(b, of, m)
                                sb = None

                            def post(ea, eb, ekey, i=i, kt=kt, vkey=vkey, bo=bo, bd=bd, nk=nk, ncols=ncols, nq=nq, h=h, m=m, maps=maps, accs=accs, sublnfin=sublnfin, packed=packed):
                                if packed:
                                    rhs = yb.ap(ea, [[eb - ea, 2], [1, nq]])
                                    oap = lambda bk: psap(bk, 0, [[nq, 2], [1, nq]])
                                else:
                                    rhs = yb.ap(ea, [[1, nq]])
                                    oap = lambda bk: psap(bk, 0, [[1, nq]])
                                MM(oap(bo), ar.ap(VT0 + kt * 512 + h * 128, [[1, 128]]), rhs, i == 0, i == nk - 1, ekey + vkey, [('ps', bo)])
                                MM(oap(bd), ones.ap(0, [[1, 128]]), rhs, i == 0, i == nk - 1, ekey + ['ones'], [('ps', bd)])
                                if i == nk - 1:
                                    tr = tmpi()
                                    ACTV(tmp.ap(tr * 512, [[1, ncols]]), psap(bd, 0, [[1, ncols]]), AF.Ln, [('ps', bd)], [('tmp', tr)])
                                    ACTV(tmp.ap(tr * 512, [[1, ncols]]), tmp.ap(tr * 512, [[1, ncols]]), AF.Exp, [('tmp', tr)], [('tmp', tr)], scale=-1.0)
                                    to = tmpi()
                                    TTO('dve', tmp.ap(to * 512, [[1, ncols]]), psap(bo, 0, [[1, ncols]]), tmp.ap(tr * 512, [[1, ncols]]), ALU.mult, [('ps', bo), ('tmp', tr)], [('tmp', to)])
                                    accs.append(to)
                                    if m == maps[-1]:
                                        sublnfin()
                            pipe.add(dict(sa=sa, sb=sb, post=post, nq=nq))

        deferred_pe = []

        def flush_deferred():
            while deferred_pe:
                deferred_pe.pop(0)()

        def post_evac(b, g, mo, bs):
            CP('act', yb.ap(g * 4096 + mo * 512, [[1, 512]]), psap(b, 0, [[1, 512]]), [('ps', b)], [('yb', g, mo)])
            si = sqi()
            ACTV(sqt.ap(si * 512, [[1, 512]]), psap(b, 0, [[1, 512]]), AF.Square, [('ps', b)], [('sq', si)])
            flush_deferred()
            deferred_pe.append(lambda: MM(psap(bs, 0, [[1, 512]]), ones.ap(0, [[1, 128]]), sqt.ap(si * 512, [[1, 512]]), mo == 0, mo == 7, [('sq', si), 'ones'], [('ps', bs)]))

        def post_update(u, li, w, g, bs):
            flush_deferred()
            ri = P.nxt('rstd', 2)
            ACTV(rstd.ap(ri * 512, [[1, 512]]), psap(bs, 0, [[1, 512]]), AF.Ln, [('ps', bs), 'epsb'], [('rstd', ri)], bias=cmap['eps'], scale=1.0 / 1024)
            ACTV(Et.ap(3 * 512, [[1, 512]]), rstd.ap(ri * 512, [[1, 512]]), AF.Exp, [('rstd', ri)], [('E', 3)], scale=-0.5)
            for mo in range(8):
                ti = P.nxt('Et3', 3)
                TTO('dve', Et.ap(ti * 512, [[1, 512]]), yb.ap(g * 4096 + mo * 512, [[1, 512]]), Et.ap(3 * 512, [[1, 512]]), ALU.mult, [('yb', g, mo), ('E', 3)], [('E', ti)])
                STT('dve', xT.ap(mo * 1024 + g * 512, [[1, 512]]), Et.ap(ti * 512, [[1, 512]]), coefap(li, w, mo, u), xT.ap(mo * 1024 + g * 512, [[1, 512]]), ALU.mult, ALU.add,
                    [('E', ti), ('coef', li), ('x', mo, g)], [('x', mo, g)])

        def wout_phase(u, li):
            mark('wout u%d l%d' % (u, li))
            s0 = wpiece('w_out', li * 1024, 0)
            s1 = wpiece('w_out', li * 1024, 512)
            for g in range(2):
                bs = bankB()
                for mo in range(8):
                    s = s0 if mo < 4 else s1
                    dense_fm(s, mo % 4, g, lambda b, g=g, mo=mo, bs=bs: post_evac(b, g, mo, bs))
                if g == 0:
                    post_update(u, li, 1, g, bs)
                else:
                    flush_deferred()
                    tailq.append(lambda bs=bs: post_update(u, li, 1, 1, bs))

        def mlp_phase(u, li):
            norm_mod(u, li, 2)
            mark('mlp1 u%d l%d' % (u, li))
            for pc in range(8):
                s = wpiece('mlp_w1', li * 1024, pc * 512)
                for g in range(2):
                    for mc in range(4):
                        m = pc * 4 + mc

                        def ev(b, m=m, g=g):
                            ei = Ei()
                            ACTV(Et.ap(ei * 512, [[1, 512]]), psap(b, 0, [[1, 512]]), AF.Relu, [('ps', b)], [('E', ei)])
                            TTO('dve', ar.ap(m * 1024 + g * 512, [[1, 512]]), Et.ap(ei * 512, [[1, 512]]), Et.ap(ei * 512, [[1, 512]]), ALU.mult, [('E', ei)], [('aT', m, g)])
                        dense_fm(s, mc, g, ev)
            mark('mlp2 u%d l%d' % (u, li))
            bss = [bankB(), bankB()]
            for ld in range(4):
                src = D['mlp_w2'].ap()[li * 4096:(li + 1) * 4096, ld * 256:(ld + 1) * 256].rearrange("(kc p) c -> p kc c", p=128)
                s2 = wload(src, [[256, 32], [1, 256]], stride=256)
                for g in range(2):
                    for mo in (2 * ld, 2 * ld + 1):
                        s = WV(s2.slot, (mo % 2) * 128, 256)
                        dense_fm(s, 0, g, lambda b, g=g, mo=mo: post_evac(b, g, mo, bss[g]), kcn=32, wstride=128, src=ar, srckey='aT')
                    if ld == 3 and g == 0:
                        post_update(u, li, 3, 0, bss[0])
            flush_deferred()
            tailq.append(lambda: post_update(u, li, 3, 1, bss[1]))

        def layer0(u):
            norm_mod(u, 0, 1)
            fence = [('h', 0, 0)]
            if u == 1 and 'ctx' not in SKIP:
                load_ctx_k('cdk', 8, KT, 1280, 4, 'k')
            s = wpiece('w_in_even', 0, 0)
            for c in range(4):
                for g in range(2):
                    dense_fm(s, c, g, lambda b, c=c, g=g: CP('act', ar.ap(AB + c * 1024 + g * 512, [[1, 512]]), psap(b, 0, [[1, 512]]), [('ps', b)], [('ab', c, g)]))
            mod_more()
            s = wpiece('w_in_even', 0, 512)
            for c in range(4):
                for g in range(2):
                    dense_fm(s, c, g, lambda b, c=c, g=g: CP('act', ar.ap(UU + c * 1024 + g * 512, [[1, 512]]), psap(b, 0, [[1, 512]]), [('ps', b)], [('u', c, g)]))
            mod_more()
            s = wpiece('w_in_even', 0, 1024)
            for c in range(4):
                for g in range(2):
                    dense_fm(s, c, g, lambda b, c=c, g=g: TTO('dve', ar.ap(UU + c * 1024 + g * 512, [[1, 512]]), psap(b, 0, [[1, 512]]), ar.ap(UU + c * 1024 + g * 512, [[1, 512]]),
                                                               ALU.mult, [('ps', b), ('u', c, g)], [('u', c, g)]))
            mod_more()
            s = wpiece('w_in_even', 0, 1536)
            for c in range(4):
                for g in range(2):
                    if u == 0:
                        dense_fm(s, c, g, lambda b, c=c, g=g: CP(evq(), ar.ap(QT + c * 1024 + g * 512, [[1, 512]]), psap(b, 0, [[1, 512]]), [('ps', b)], [('q', c, g)]))
                    else:
                        dense_fm(s, c, g, lambda b, c=c, g=g: rope_evac(b, g, QT + c * 1024 + g * 512, [('q', c, g)]))
            mod_more()
            s = wpiece('w_in_even', 0, 2048)
            for c in range(4):
                for g in range(2):
                    if u == 0:
                        dense_fm(s, c, g, lambda b, c=c, g=g: CP(evq(), ar.ap(KT + c * 1280 + g * 512, [[1, 512]]), psap(b, 0, [[1, 512]]), [('ps', b)], [('k', c, g)]))
                    else:
                        dense_fm(s, c, g, lambda b, c=c, g=g: rope_evac(b, g, KT + c * 1280 + g * 512, [('k', c, g)]))
            if u == 0 and 'ktm' not in SKIP:
                for t in range(8):
                    def ev(b, t=t):
                        ti = tmpi()
                        CP('dve', tmp.ap(ti * 512, [[1, 512]]), psap(b, 0, [[1, 512]]), [('ps', b)], [('tmp', ti)])
                        sq_, tt = t // 2, t % 2
                        out_ops.append(DMA('sp', bass.AP(D['ndk'], sq_ * 8 * 16384 + tt * 8192, [[64, 128], [16384, 8], [1, 64]]), tmp.ap(ti * 512, [[64, 8], [1, 64]]), [('tmp', ti)], []))
                    dense_tm(s, 0, 512, t, ev)
            s = wpiece('w_in_even', 0, 2560)
            for t in range(8):
                def ev(b, t=t):
                    CP('act', ar.ap(VT0 + t * 512, [[1, 512]]), psap(b, 0, [[1, 512]]), [('ps', b)], [('v', t)])
                    if u == 0 and 'vout' not in SKIP:
                        ti = tmpi()
                        CP('dve', tmp.ap(ti * 512, [[1, 512]]), psap(b, 0, [[1, 512]]), [('ps', b)], [('tmp', ti)])
                        sq_, tt = t // 2, t % 2
                        out_ops.append(DMA('sp', bass.AP(D['ndv'], sq_ * 4 * 32768 + tt * 16384, [[128, 128], [32768, 4], [1, 128]]), tmp.ap(ti * 512, [[128, 4], [1, 128]]), [('tmp', ti)], []))
                dense_tm(s, 0, 512, t, ev)
            mark('conv u%d' % u)
            nseq, sl = (4, 256) if u == 0 else (1, 1024)
            for c in range(4 if 'conv' not in SKIP else 0):
                ukeys = [('u', c, 0), ('u', c, 1)]
                t1 = tmpi(); t2 = tmpi()
                acc_keys = [('tmp', t1), ('tmp', t2)]
                acc = lambda off, n: tmp.ap(t1 * 512 + off, [[1, n]])
                if t2 != t1 + 1:
                    t1 = tmpi(); t2 = tmpi()
                    acc_keys = [('tmp', t1), ('tmp', t2)]
                base = t1 * 512
                TS('dve', tmp.ap(base, [[1, 1024]]), ar.ap(UU + c * 1024, [[1, 1024]]), cw.ap(c * 3 + 1, [[1, 1]]), None, ALU.mult, None, ukeys + ['cw'], acc_keys)
                STT('dve', tmp.ap(base + 1, [[sl, nseq], [1, sl - 1]]), ar.ap(UU + c * 1024, [[sl, nseq], [1, sl - 1]]), cw.ap(c * 3 + 0, [[1, 1]]),
                    tmp.ap(base + 1, [[sl, nseq], [1, sl - 1]]), ALU.mult, ALU.add, ukeys + ['cw'] + acc_keys, acc_keys)
                STT('dve', tmp.ap(base, [[sl, nseq], [1, sl - 1]]), ar.ap(UU + c * 1024 + 1, [[sl, nseq], [1, sl - 1]]), cw.ap(c * 3 + 2, [[1, 1]]),
                    tmp.ap(base, [[sl, nseq], [1, sl - 1]]), ALU.mult, ALU.add, ukeys + ['cw'] + acc_keys, acc_keys)
                TTO('dve', hT.ap(c * 1024, [[1, 1024]]), tmp.ap(base, [[1, 1024]]), ar.ap(AB + c * 1024, [[1, 1024]]), ALU.mult,
                    acc_keys + [('ab', c, 0), ('ab', c, 1)], [('h', c, 0), ('h', c, 1)])
            if u == 1 and 'ctx' not in SKIP:
                for tt in range(2):
                    DMA('pool', ar.ap(VT0 + (8 + tt) * 512, [[128, 4], [1, 128]]), bass.AP(D['cdv'], tt * 16384, [[128, 128], [32768, 4], [1, 128]]), fence, [('v', 8 + tt)])
            if stop_after == ('A', u):
                raise StopBuild('h')
            diff_attn(u)
            if stop_after == ('B', u):
                pipe.flush()
                raise StopBuild('h')
            pipe.flush()
            wout_phase(u, 0)
            if stop_after == ('C', u):
                raise StopBuild('x')
            mlp_phase(u, 0)

        def layer1(u):
            norm_mod(u, 1, 1)
            fence = [('h', 0, 0)]
            if u == 1:
                load_ctx_k('cnk', 8, KT, 1280, 4, 'k')
                load_ctx_k('csk', 2, K2, 1280, 2, 'k2', dup=True)
            s = wpiece('w_in_odd', 0, 0)
            for c in range(4):
                for g in range(2):
                    dense_fm(s, c, g, lambda b, c=c, g=g: CP(evq(), ar.ap(QT + c * 1024 + g * 512, [[1, 512]]), psap(b, 0, [[1, 512]]), [('ps', b)], [('q', c, g)]))
            s = wpiece('w_in_odd', 0, 512)
            for c in range(4):
                for g in range(2):
                    dense_fm(s, c, g, lambda b, c=c, g=g: CP(evq(), ar.ap(KT + c * 1280 + g * 512, [[1, 512]]), psap(b, 0, [[1, 512]]), [('ps', b)], [('k', c, g)]))
            if u == 0:
                for t in range(8):
                    def ev(b, t=t):
                        ti = tmpi()
                        CP('dve', tmp.ap(ti * 512, [[1, 512]]), psap(b, 0, [[1, 512]]), [('ps', b)], [('tmp', ti)])
                        sq_, tt = t // 2, t % 2
                        out_ops.append(DMA('sp', bass.AP(D['nnk'], sq_ * 8 * 16384 + tt * 8192, [[64, 128], [16384, 8], [1, 64]]), tmp.ap(ti * 512, [[64, 8], [1, 64]]), [('tmp', ti)], []))
                    dense_tm(s, 0, 512, t, ev)
            s = wpiece('w_in_odd', 0, 1024)
            for t in range(8):
                def ev(b, t=t):
                    CP('act', ar.ap(VT1 + t * 512, [[1, 512]]), psap(b, 0, [[1, 512]]), [('ps', b)], [('v', t)])
                    if u == 0:
                        ti = tmpi()
                        CP('dve', tmp.ap(ti * 512, [[1, 512]]), psap(b, 0, [[1, 512]]), [('ps', b)], [('tmp', ti)])
                        sq_, tt = t // 2, t % 2
                        out_ops.append(DMA('sp', bass.AP(D['nnv'], sq_ * 8 * 16384 + tt * 8192, [[64, 128], [16384, 8], [1, 64]]), tmp.ap(ti * 512, [[64, 8], [1, 64]]), [('tmp', ti)], []))
                dense_tm(s, 0, 512, t, ev)
            if u == 1:
                for i in range(7):
                    b = bankA()
                    for kc in range(8):
                        gk = [('h', kc, 0), ('h', kc, 1)]
                        MM(psap(b, 0, [[1, 512]]), hT.ap(kc * 1024 + 64 + i * 128, [[1, 128]]), wr.ap(s.base + kc * s.stride, [[1, 512]]), kc == 0, kc == 7, [s.key] + gk, [('ps', b)])
                    CP(evq(), ar.ap(VT2 + i * 512, [[1, 512]]), psap(b, 0, [[1, 512]]), [('ps', b)], [('v2', i)])
            s = wpiece('w_in_odd', 0, 1536)
            for c in range(4):
                for g in range(2):
                    if u == 0:
                        dense_fm(s, c, g, lambda b, c=c, g=g: CP(evq(), ar.ap(Q2 + c * 1024 + g * 512, [[1, 512]]), psap(b, 0, [[1, 512]]), [('ps', b)], [('q2', c, g)]))
                    else:
                        dense_fm(s, c, g, lambda b, c=c, g=g: rope_evac(b, g, Q2 + c * 1024 + g * 512, [('q2', c, g)]))
            srcA = bass.AP(D['w_in_odd'], 2048, [[2304, 128], [2304 * 128, 8], [1, 256]])
            extra = []
            for g_ in range(2):
                for dpl in range(2):
                    extra.append((2048 + g_ * 128 + dpl * 64, [[256, 8], [1, 64]], bass.AP(D['w_in_odd'], 2048 + g_ * 64, [[2304, 128], [2304 * 128, 8], [1, 64]])))
            s = wload(srcA, [[256, 8], [1, 256]], extra=extra, stride=256)
            for gk_ in range(2):
                for g in range(2):
                    b = bankA()
                    for kc in range(8):
                        MM(psap(b, 0, [[1, 512]]), wr.ap(s.base + 2048 + kc * 256 + gk_ * 128, [[1, 128]]), hT.ap(kc * 1024 + g * 512, [[1, 512]]), kc == 0, kc == 7,
                           [s.key, ('h', kc, g)], [('ps', b)])
                    if u == 0:
                        CP(evq(), ar.ap(K2 + gk_ * 1280 + g * 512, [[1, 512]]), psap(b, 0, [[1, 512]]), [('ps', b)], [('k2', gk_, g)])
                    else:
                        rope_evac(b, g, K2 + gk_ * 1280 + g * 512, [('k2', gk_, g)])
            for t in range(8):
                def ev(b, t=t):
                    for dpl in range(2):
                        CP(['act', 'dve'][dpl], ar.ap(VD + t * 256 + dpl * 64, [[128, 2], [1, 64]]), psap(b, 128, [[64, 2], [1, 64]]), [('ps', b)], [('vd', t)])
                    if u == 0:
                        ti = tmpi()
                        CP('dve', tmp.ap(ti * 512, [[1, 256]]), psap(b, 0, [[1, 256]]), [('ps', b)], [('tmp', ti)])
                        sq_, tt = t // 2, t % 2
                        out_ops.append(DMA('sp', bass.AP(D['nsk'], sq_ * 2 * 16384 + tt * 8192, [[64, 128], [16384, 2], [1, 64]]), tmp.ap(ti * 512, [[64, 2], [1, 64]]), [('tmp', ti)], []))
                        out_ops.append(DMA('sp', bass.AP(D['nsv'], sq_ * 2 * 16384 + tt * 8192, [[64, 128], [16384, 2], [1, 64]]), tmp.ap(ti * 512 + 128, [[64, 2], [1, 64]]), [('tmp', ti)], []))
                dense_tm(s, 0, 256, t, ev, wstride=256)

            if u == 1:
                for tt in range(2):
                    DMA('pool', ar.ap(VT1 + (8 + tt) * 512, [[64, 8], [1, 64]]), bass.AP(D['cnv'], tt * 8192, [[64, 128], [16384, 8], [1, 64]]), fence, [('v', 8 + tt)])
                    for dpl in range(2):
                        DMA('pool', ar.ap(VD + (8 + tt) * 256 + dpl * 64, [[128, 2], [1, 64]]), bass.AP(D['csv'], tt * 8192, [[64, 128], [16384, 2], [1, 64]]), fence, [('vd', 8 + tt)])
            mark('attn1 u%d' % u)

            def kt_c(c, col, vtile_ap, vkeys, kkeys, ma=None, mb=None, mkeys=None):
                return dict(ka=ar.ap(KT + c * 1280 + col, [[1, 128]], 0, 64), kb=ar.ap(KT + c * 1280 + col, [[1, 128]], 64, 64), kkeys=kkeys,
                            v=vtile_ap, vkeys=vkeys, ma=ma, mb=mb, mkeys=mkeys or [])

            def kt_d(gk_, col, vt, kkeys, m=None, mkeys=None):
                return dict(ka=ar.ap(K2 + gk_ * 1280 + col, [[1, 128]], 0, 64), kb=ar.ap(K2 + gk_ * 1280 + col, [[1, 128]], 64, 64), kkeys=kkeys,
                            v=ar.ap(VD + vt * 256 + gk_ * 128, [[1, 128]]), vkeys=[('vd', vt)], ma=m, mb=m, mkeys=mkeys or [])

            if u == 0:
                for sq_ in range(4):
                    g = sq_ // 2
                    q0 = sq_ * 256
                    for c in range(4):
                        kts = [kt_c(c, q0 + kt * 128, ar.ap(VT1 + (sq_ * 2 + kt) * 512 + c * 128, [[1, 128]]), [('v', sq_ * 2 + kt)], [('k', c, g)]) for kt in range(2)]
                        attn_pair(kts, ar.ap(QT + c * 1024 + q0, [[1, 256]], 0, 64), ar.ap(QT + c * 1024 + q0, [[1, 256]], 64, 64), [('q', c, g)], 256,
                                  c * 1024 + q0, [('h', c, g)])
                    for c in range(4):
                        gk_ = c // 2
                        kts = [kt_d(gk_, q0 + kt * 128, sq_ * 2 + kt, [('k2', gk_, g)]) for kt in range(2)]
                        attn_pair(kts, ar.ap(Q2 + c * 1024 + q0, [[1, 256]], 0, 64), ar.ap(Q2 + c * 1024 + q0, [[1, 256]], 64, 64), [('q2', c, g)], 256,
                                  (4 + c) * 1024 + q0, [('h', 4 + c, g)], sink_h=2 * c)
            else:
                groups = [(0, 4, 0), (4, 1, 0)] + [(r, 1, r - 4) for r in range(5, 12)] + [(12, 4, 8)]
                for c in range(4):
                    for (r0, nr, rs) in groups:
                        nq = nr * 64
                        q0 = r0 * 64
                        g = q0 // 512
                        kts = []
                        for jj in range(4):
                            ka_ = rs + 2 * jj
                            col = ka_ * 64
                            if ka_ % 2 == 0:
                                vap = ar.ap(VT1 + (ka_ // 2) * 512 + c * 128, [[1, 128]]); vk = [('v', ka_ // 2)]
                            else:
                                vap = ar.ap(VT2 + ((ka_ - 1) // 2) * 512 + c * 128, [[1, 128]]); vk = [('v2', (ka_ - 1) // 2)]
                            i0 = 6 - (ka_ - r0)
                            assert 0 <= i0 and i0 + nr <= 14
                            ma = bank.ap((2 * c) * 896 + i0 * 64, [[1, nq]])
                            mb = bank.ap((2 * c + 1) * 896 + i0 * 64, [[1, nq]])
                            kk = [('k', c, (col // 512)), ('k', c, ((col + 127) // 512))]
                            kts.append(kt_c(c, col, vap, vk, kk, ma, mb, ['bank']))
                        for tt in range(2):
                            kts.append(kt_c(c, 1024 + tt * 128, ar.ap(VT1 + (8 + tt) * 512 + c * 128, [[1, 128]]), [('v', 8 + tt)], [('k', c, 'ctx')]))
                        attn_pair(kts, ar.ap(QT + c * 1024 + q0, [[1, nq]], 0, 64), ar.ap(QT + c * 1024 + q0, [[1, nq]], 64, 64), [('q', c, g)], nq,
                                  c * 1024 + q0, [('h', c, g)])
                mark('attn1D u%d' % u)
                for c in range(4):
                    gk_ = c // 2
                    for n in range(8):
                        g = n // 4
                        q0 = n * 128
                        kts = []
                        if n > 0:
                            kts.append(kt_d(gk_, (n - 1) * 128, n - 1, [('k2', gk_, (n - 1) // 4)], mprev.ap(0, [[1, 128]]), ['mprev']))
                        kts.append(kt_d(gk_, n * 128, n, [('k2', gk_, g)]))
                        if n < 7:
                            kts.append(kt_d(gk_, (n + 1) * 128, n + 1, [('k2', gk_, (n + 1) // 4)], mnext.ap(0, [[1, 128]]), ['mnext']))
                        for tt in range(2):
                            kts.append(kt_d(gk_, 1024 + tt * 128, 8 + tt, [('k2', gk_, 'ctx')]))
                        attn_pair(kts, ar.ap(Q2 + c * 1024 + q0, [[1, 128]], 0, 64), ar.ap(Q2 + c * 1024 + q0, [[1, 128]], 64, 64), [('q2', c, g)], 128,
                                  (4 + c) * 1024 + q0, [('h', 4 + c, g)], sink_h=2 * c)
            pipe.flush()
            wout_phase(u, 1)
            mlp_phase(u, 1)

        epsb = sb('epsb', 1, F32)
        P.op('dve', lambda e: e.memset(epsb.ap(0, [[1, 1]]), EPS), writes=['epsb'])
        cmap['eps'] = epsb.ap(0, [[1, 1]])
        setup()
        load_x(0)
        compute_mod(0)
        modq.extend(range(4, 12))
        done = False
        for u in range(2):
            if u == 1:
                load_x(u)
            if stop_after == ('X', u):
                store_x(u)
                break
            if stop_after == ('N', u):
                norm_mod(u, 0, 1)
                for kc in range(8):
                    for g in range(2):
                        CP('dve', xT.ap(kc * 1024 + g * 512, [[1, 512]]), hT.ap(kc * 1024 + g * 512, [[1, 512]]), [('h', kc, g)], [('x', kc, g)])
                store_x(u)
                break
            try:
                layer0(u)
            except StopBuild as ex:
                if str(ex) == 'h':
                    for kc in range(8):
                        for g in range(2):
                            CP('dve', xT.ap(kc * 1024 + g * 512, [[1, 512]]), hT.ap(kc * 1024 + g * 512, [[1, 512]]), [('h', kc, g)], [('x', kc, g)])
                store_x(u)
                break
            if stop_after == ('L0', u):
                store_x(u)
                done = True
                break
            layer1(u)
            store_x(u)
            if stop_after == ('L1', u):
                done = True
                break
        mark('end')
        if os.environ.get('DBG_MARKS'):
            import json
            json.dump(marks, open(os.environ['DBG_MARKS'], 'w'))
        P.emit(final_wait_ops=out_ops)
        print("ops:", len(P.ops))
    return nc


def host_consts():
    c = {}
    c['c_ident'] = np.eye(128, dtype=np.float32)
    Rm = np.zeros((64, 64), np.float32)
    for i in range(2):
        for f in range(16):
            a, b = i * 32 + f, i * 32 + 16 + f
            Rm[a, b] = -1.0
            Rm[b, a] = 1.0
    R2 = np.zeros((128, 128), np.float32)
    R2[:64, :64] = Rm.T
    R2[64:, 64:] = Rm.T
    c['c_R2'] = R2
    t = np.arange(1024)
    rows = (t // 64).astype(np.float32)
    cols = (t % 64).astype(np.float32)
    inv = (1.0 / (np.float32(10000.0) ** (np.arange(16, dtype=np.float32) / np.float32(16)))).astype(np.float32)
    cos = np.zeros((128, 1024), np.float32)
    sin = np.zeros((128, 1024), np.float32)
    for p in range(128):
        d = p % 64
        i, f = d // 32, d % 16
        pos = rows if i == 0 else cols
        ang = (pos * inv[f]).astype(np.float32)
        cos[p] = np.cos(ang)
        sin[p] = np.sin(ang)
    c['c_cos'] = cos
    c['c_sin'] = sin
    j = np.arange(128)[:, None]
    i = np.arange(128)[None, :]
    c['c_mprev'] = ((j >= i).astype(np.float32) - 1.0) * 30000.0
    c['c_mnext'] = ((j <= i).astype(np.float32) - 1.0) * 30000.0
    qc = np.arange(64)
    cs = np.clip(qc - 8, 0, 48)
    kc = np.arange(64)
    ok = (kc[:, None] >= cs[None, :]) & (kc[:, None] < cs[None, :] + 16)
    c['c_colok'] = np.concatenate([ok, ok], axis=0).astype(np.float32)
    c['c_colneg'] = (c['c_colok'] - 1.0) * 30000.0
    return c


def rpb_layout(rpb):
    kc = np.arange(64)[:, None]
    qc = np.arange(64)[None, :]
    dc = np.clip(kc - qc + 15, 0, 30)
    out = np.zeros((128, 8, 14, 64), np.float32)
    for i in range(14):
        out[:64, :, i, :] = np.transpose(rpb[:, (6 - i) + 7][:, dc], (1, 0, 2))
        out[64:, :, i, :] = np.transpose(rpb[:, (7 - i) + 7][:, dc], (1, 0, 2))
    return np.ascontiguousarray(out.reshape(128, 8 * 14 * 64))


_NC_CACHE = {}


def kernel(x_prompt, x_sample, cache_diff_k, cache_diff_v, cache_na_k, cache_na_v, cache_swa_k, cache_swa_v, c, c_ctx, mod_w, mod_b,
           norm_mix_pre, norm_mix_post, norm_mlp_pre, norm_mlp_post, w_in_even, conv_w, lambda_q1, lambda_k1, lambda_q2, lambda_k2,
           subln, w_in_odd, rpb, sink, w_out, mlp_w1, mlp_w2, _stop_after=None):
    f = lambda a: np.ascontiguousarray(np.asarray(a, dtype=np.float32))
    if _stop_after not in _NC_CACHE:
        _NC_CACHE[_stop_after] = build(_stop_after)
    nc = _NC_CACHE[_stop_after]
    consts = host_consts()
    shared = dict(
        mod_w=f(mod_w).reshape(2048, 6144), mod_b=f(mod_b),
        gains=np.stack([f(norm_mix_pre).reshape(-1), f(norm_mix_post).reshape(-1), f(norm_mlp_pre).reshape(-1), f(norm_mlp_post).reshape(-1)]),
        w_in_even=f(w_in_even)[0], conv_w=f(conv_w)[0],
        lams=np.stack([f(lambda_q1)[0], f(lambda_k1)[0], f(lambda_q2)[0], f(lambda_k2)[0]]),
        subln=f(subln)[0].reshape(128, 1), w_in_odd=f(w_in_odd)[0], rpbT=rpb_layout(f(rpb)[0]), sink=f(sink)[0].reshape(1, 8),
        w_out=f(w_out).reshape(2048, 1024), mlp_w1=f(mlp_w1).reshape(2048, 4096), mlp_w2=f(mlp_w2).reshape(8192, 1024),
    )
    shared.update(consts)
    xp = f(x_prompt); xs = f(x_sample)
    in_maps = []
    for k in range(NCORES):
        m = dict(shared)
        m['xP'] = xp[4 * k:4 * k + 4].reshape(1024, 1024)
        m['xS'] = xs[k]
        m['cdk'] = f(cache_diff_k)[k, 0].reshape(8, 256, 64)
        m['cdv'] = f(cache_diff_v)[k, 0]
        m['cnk'] = f(cache_na_k)[k, 0]
        m['cnv'] = f(cache_na_v)[k, 0]
        m['csk'] = f(cache_swa_k)[k, 0]
        m['csv'] = f(cache_swa_v)[k, 0]
        m['cvec'] = np.stack([f(c_ctx), f(c)[k]])
        in_maps.append(m)
    in_maps = in_maps[:DBG_CORES]
    res = run_bass_kernel_spmd(nc, in_maps, core_ids=list(range(DBG_CORES)))
    R = list(res.results) + [res.results[0]] * (NCORES - DBG_CORES)
    y_prompt = np.concatenate([r['yP'].reshape(4, 256, 1024) for r in R], axis=0)
    y_sample = np.stack([r['yS'] for r in R], axis=0)
    ndk = np.concatenate([r['ndk'].reshape(4, 1, 4, 2, 256, 64) for r in R], axis=0)
    ndv = np.concatenate([r['ndv'].reshape(4, 1, 4, 256, 128) for r in R], axis=0)
    nnk = np.concatenate([r['nnk'].reshape(4, 1, 8, 256, 64) for r in R], axis=0)
    nnv = np.concatenate([r['nnv'].reshape(4, 1, 8, 256, 64) for r in R], axis=0)
    nsk = np.concatenate([r['nsk'].reshape(4, 1, 2, 256, 64) for r in R], axis=0)
    nsv = np.concatenate([r['nsv'].reshape(4, 1, 2, 256, 64) for r in R], axis=0)
    return (y_prompt, y_sample, ndk, ndv, nnk, nnv, nsk, nsv)
```

```python
import math
import os
SKIP = set(os.environ.get('DBG_SKIP', '').split(','))
DBG_CORES = int(os.environ.get('DBG_CORES', '8'))
from contextlib import ExitStack
import numpy as np
import concourse.bass as bass
import concourse.mybir as mybir
from concourse.bass_utils import run_bass_kernel_spmd

F32 = mybir.dt.float32
BF16 = mybir.dt.bfloat16
AF = mybir.ActivationFunctionType
ALU = mybir.AluOpType
AX = mybir.AxisListType
EPS = 1e-6
NCORES = 8


class Prog:
    def __init__(self, nc):
        self.nc = nc
        self.ops = []
        self.last_w = {}
        self.readers = {}
        self.rot = {}

    def nxt(self, name, n, base=0):
        v = self.rot.get(name, 0)
        self.rot[name] = v + 1
        return base + (v % n)

    def op(self, eng, fn, reads=(), writes=(), dma=False, semkey=None):
        idx = len(self.ops)
        raw = set()
        oth = set()
        for k in reads:
            w = self.last_w.get(k)
            if w is not None:
                raw.add(w)
            if isinstance(k, tuple) and k[0] == 'ps':
                for r in self.readers.get(k, ()):
                    oth.add(r)
        for k in writes:
            w = self.last_w.get(k)
            if w is not None:
                oth.add(w)
            for r in self.readers.get(k, ()):
                oth.add(r)
        for k in reads:
            self.readers.setdefault(k, []).append(idx)
        for k in writes:
            self.last_w[k] = idx
            self.readers[k] = []
        raw.discard(idx)
        oth.discard(idx)
        self.ops.append(dict(eng=eng, fn=fn, raw=raw, oth=oth - raw, dma=dma, semkey=semkey, sig=False))
        return idx

    def emit(self, final_wait_ops=()):
        nc = self.nc
        ops = self.ops
        for i, o in enumerate(ops):
            need = set(o['raw'])
            for d in o['oth']:
                p = ops[d]
                if p['eng'] == o['eng'] and not p['dma'] and not o['dma']:
                    continue
                need.add(d)
            o['need'] = need
            for d in need:
                ops[d]['sig'] = True
        for d in final_wait_ops:
            ops[d]['sig'] = True
        for o in ops:
            if o['dma']:
                o['sig'] = True
        stack = ExitStack()
        sems = {}
        cnt = {}
        rot = {}
        ROT = 12
        for i, o in enumerate(ops):
            if not o['sig']:
                continue
            if o['dma']:
                if o['semkey'] is not None:
                    name = 'dk_%s' % (o['semkey'],)
                else:
                    r = rot.get(o['eng'], 0)
                    rot[o['eng']] = r + 1
                    name = 'dr_%s_%d' % (o['eng'], r % ROT)
                cnt[name] = cnt.get(name, 0) + 16
            else:
                name = 'c_%s' % o['eng']
                cnt[name] = cnt.get(name, 0) + 1
            o['sem'] = name
            o['val'] = cnt[name]
            if name not in sems:
                sems[name] = stack.enter_context(nc.semaphore(name.replace(' ', '').replace("'", '').replace(',', '_').replace('(', '').replace(')', '')))
        per_eng = {e: [] for e in ['pe', 'act', 'dve', 'pool', 'sp']}
        for i, o in enumerate(ops):
            per_eng[o['eng']].append(i)
        final_wait_ops = list(final_wait_ops)

        def emit_eng(engname, engobj):
            waited = {}
            for i in per_eng[engname]:
                o = ops[i]
                w = {}
                for d in o['need']:
                    p = ops[d]
                    nm, v = p['sem'], p['val']
                    if waited.get(nm, 0) >= v:
                        continue
                    if w.get(nm, 0) < v:
                        w[nm] = v
                for nm, v in w.items():
                    engobj.wait_ge(sems[nm], v)
                    waited[nm] = v
                if o['dma'] and o['sig']:
                    pv = o['val'] - 16
                    if pv > 0 and waited.get(o['sem'], 0) < pv:
                        engobj.wait_ge(sems[o['sem']], pv)
                        waited[o['sem']] = pv
                ins = o['fn'](engobj)
                if o['sig']:
                    ins.then_inc(sems[o['sem']], 16 if o['dma'] else 1)
            if engname == 'sp':
                w = {}
                for d in final_wait_ops:
                    p = ops[d]
                    nm, v = p['sem'], p['val']
                    if w.get(nm, 0) < v:
                        w[nm] = v
                for nm, v in w.items():
                    if waited.get(nm, 0) < v:
                        engobj.wait_ge(sems[nm], v)

        with stack:
            with nc.Block() as block:
                @block.tensor
                def _(e):
                    emit_eng('pe', e)

                @block.scalar
                def _(e):
                    emit_eng('act', e)

                @block.vector
                def _(e):
                    emit_eng('dve', e)

                @block.gpsimd
                def _(e):
                    emit_eng('pool', e)

                @block.sync
                def _(e):
                    emit_eng('sp', e)


class StopBuild(Exception):
    pass


class TT:
    def __init__(self, h, free):
        self.h = h
        self.F = free

    def ap(self, off, dims, p0=0, pn=128):
        return bass.AP(self.h, p0 * self.F + off, [[self.F, pn]] + [list(d) for d in dims])


def build(stop_after=None):
    nc = bass.Bass("TRN2", target_bir_lowering=False)
    D = {}

    def din(name, shape):
        D[name] = nc.dram_tensor(name, list(shape), F32, kind="ExternalInput")

    def dout(name, shape):
        D[name] = nc.dram_tensor(name, list(shape), F32, kind="ExternalOutput")

    din('xP', [1024, 1024]); din('xS', [1024, 1024])
    din('cdk', [8, 256, 64]); din('cdv', [4, 256, 128])
    din('cnk', [8, 256, 64]); din('cnv', [8, 256, 64])
    din('csk', [2, 256, 64]); din('csv', [2, 256, 64])
    din('cvec', [2, 1024])
    din('mod_w', [2048, 6144]); din('mod_b', [2, 6144])
    din('gains', [4, 2048])
    din('w_in_even', [1024, 3072]); din('conv_w', [3, 512]); din('lams', [4, 64]); din('subln', [128, 1])
    din('w_in_odd', [1024, 2304]); din('rpbT', [128, 8 * 14 * 64]); din('sink', [1, 8])
    din('w_out', [2048, 1024]); din('mlp_w1', [2048, 4096]); din('mlp_w2', [8192, 1024])
    din('c_ident', [128, 128]); din('c_R2', [128, 128]); din('c_cos', [128, 1024]); din('c_sin', [128, 1024])
    din('c_mprev', [128, 128]); din('c_mnext', [128, 128]); din('c_colok', [128, 64]); din('c_colneg', [128, 64])
    dout('yP', [1024, 1024]); dout('yS', [1024, 1024])
    dout('ndk', [4, 8, 256, 64]); dout('ndv', [4, 4, 256, 128])
    dout('nnk', [4, 8, 256, 64]); dout('nnv', [4, 8, 256, 64])
    dout('nsk', [4, 2, 256, 64]); dout('nsv', [4, 2, 256, 64])

    es = ExitStack()

    def sb(name, free, dt):
        return TT(es.enter_context(nc.sbuf_tensor('s_' + name, [128, free], dt)), free)

    with es:
        xT = sb('xT', 8192, F32)
        hT = sb('hT', 8192, BF16)
        yb = sb('yb', 8192, BF16)
        wr = sb('wr', 2 * 8192, BF16)
        ar = sb('ar', 32768, BF16)
        bank = sb('bank', 7168, BF16)
        tmp = sb('tmp', 6 * 512, F32)
        Et = sb('Et', 4 * 512, BF16)
        sqt = sb('sqt', 4 * 512, BF16)
        rstd = sb('rstd', 2 * 512, F32)
        ident = sb('ident', 128, F32)
        ones = sb('ones', 128, BF16)
        R2 = sb('R2', 128, BF16)
        cosT = sb('cosT', 1024, BF16)
        sinT = sb('sinT', 1024, BF16)
        mprev = sb('mprev', 128, BF16)
        mnext = sb('mnext', 128, BF16)
        colok = sb('colok', 64, F32)
        colneg = sb('colneg', 64, F32)
        identb = sb('identb', 128, BF16)
        cT = sb('cT', 16, F32)
        sT = sb('sT', 16, BF16)
        modb = sb('modb', 96, F32)
        modv = sb('modv', 192, F32)
        gains = sb('gains', 64, F32)
        coef = sb('coef', 128, F32)
        cw = sb('cw', 12, F32)
        esink = sb('esink', 8, F32)
        lamt = sb('lamt', 256, F32)
        lams = sb('lams', 8, F32)
        subln = sb('subln', 2, F32)
        ps_all = TT(es.enter_context(nc.psum_tensor('ps_all', [128, 4096], F32)), 4096)

        P = Prog(nc)

        def MM(out, lhsT, rhs, start, stop, reads, writes, skip=False):
            if skip:
                P.op('pe', lambda e: e.matmul(out, lhsT, rhs, start=start, stop=stop, skip_group_check=True), reads=reads, writes=writes)
            else:
                P.op('pe', lambda e: e.matmul(out, lhsT, rhs, start=start, stop=stop), reads=reads, writes=writes)

        def TR(out, in_, reads, writes):
            P.op('pe', lambda e: e.transpose(out, in_, ident.ap(0, [[1, 128]])), reads=list(reads) + ['ident'], writes=writes)

        def ACTV(out, in_, func, reads, writes, bias=None, scale=None):
            kw = {}
            if bias is not None:
                kw['bias'] = bias
            if scale is not None:
                kw['scale'] = scale
            P.op('act', lambda e: e.activation(out, in_, func, **kw), reads=reads, writes=writes)

        def TTO(eng, out, in0, in1, op, reads, writes):
            P.op(eng, lambda e: e.tensor_tensor(out=out, in0=in0, in1=in1, op=op), reads=reads, writes=writes)

        def STT(eng, out, in0, scalar, in1, op0, op1, reads, writes):
            P.op(eng, lambda e: e.scalar_tensor_tensor(out=out, in0=in0, scalar=scalar, in1=in1, op0=op0, op1=op1), reads=reads, writes=writes)

        def TS(eng, out, in0, s1, s2, op0, op1, reads, writes):
            if s2 is None:
                P.op(eng, lambda e: e.tensor_scalar(out=out, in0=in0, scalar1=s1, scalar2=None, op0=op0), reads=reads, writes=writes)
            else:
                P.op(eng, lambda e: e.tensor_scalar(out=out, in0=in0, scalar1=s1, scalar2=s2, op0=op0, op1=op1), reads=reads, writes=writes)

        def CP(eng, out, in_, reads, writes):
            if eng == 'act':
                P.op('act', lambda e: e.activation(out, in_, AF.Copy), reads=reads, writes=writes)
            else:
                P.op(eng, lambda e: e.tensor_copy(out, in_), reads=reads, writes=writes)

        def RECIP(out, in_, reads, writes):
            P.op('dve', lambda e: e.reciprocal(out, in_), reads=reads, writes=writes)

        def DMA(q, out, in_, reads, writes, semkey=None, slow=False):
            if slow:
                return P.op(q, lambda e: e.dma_start(out=out, in_=in_, allow_slow_non_contiguous=True), reads=reads, writes=writes, dma=True, semkey=semkey)
            return P.op(q, lambda e: e.dma_start(out=out, in_=in_), reads=reads, writes=writes, dma=True, semkey=semkey)

        def psap(b, off, dims, p0=0, pn=128):
            return ps_all.ap(b * 512 + off, dims, p0, pn)

        def bankA():
            return P.nxt('bA', 4, 0)

        def bankB():
            return P.nxt('bB', 4, 4)

        def bankSCpair():
            return P.nxt('bSCp', 2) * 2

        def bankACC():
            return P.nxt('bACC', 4, 4)

        def tmpi():
            return P.nxt('tmp', 6)

        def Ei():
            return P.nxt('E', 4)

        def sqi():
            return P.nxt('sq', 4)

        def evq():
            return ['act', 'dve'][P.nxt('evq', 2)]

        out_ops = []
        cmap = {}
        marks = []

        def mark(name):
            marks.append((name, sum(1 for o in P.ops if o['eng'] == 'pe')))

        def setup():
            DMA('sp', ident.ap(0, [[1, 128]]), D['c_ident'].ap(), [], ['ident'])
            P.op('dve', lambda e: e.memset(ones.ap(0, [[1, 128]]), 1.0), writes=['ones'])
            DMA('pool', R2.ap(0, [[1, 128]]), D['c_R2'].ap(), [], ['R2'])
            DMA('pool', cosT.ap(0, [[1, 1024]]), D['c_cos'].ap(), [], ['cos'])
            DMA('pool', sinT.ap(0, [[1, 1024]]), D['c_sin'].ap(), [], ['sin'])
            DMA('pool', mprev.ap(0, [[1, 128]]), D['c_mprev'].ap(), [], ['mprev'])
            DMA('pool', mnext.ap(0, [[1, 128]]), D['c_mnext'].ap(), [], ['mnext'])
            DMA('sp', colok.ap(0, [[1, 64]]), D['c_colok'].ap(), [], ['colok'])
            DMA('sp', colneg.ap(0, [[1, 64]]), D['c_colneg'].ap(), [], ['colneg'])
            CP('dve', identb.ap(0, [[1, 128]]), ident.ap(0, [[1, 128]]), ['ident'], ['identb'])
            for j in range(2):
                DMA('sp', cT.ap(j, [[2, 8]]), bass.AP(D['cvec'], j * 1024, [[1, 128], [128, 8]]), [], ['cT'], slow=True)
                DMA('sp', modb.ap(j * 48, [[1, 48]]), bass.AP(D['mod_b'], j * 6144, [[1, 128], [128, 48]]), [], ['modb'], slow=True)
            for w in range(4):
                DMA('sp', gains.ap(w * 16, [[1, 16]]), bass.AP(D['gains'], w * 2048, [[1, 128], [128, 16]]), [], ['gains'], slow=True)
            for j in range(3):
                DMA('sp', cw.ap(j, [[3, 4]]), bass.AP(D['conv_w'], j * 512, [[1, 128], [128, 4]]), [], ['cw'], slow=True)
            DMA('sp', subln.ap(0, [[1, 1]]), D['subln'].ap(), [], ['subln0'])
            DMA('sp', esink.ap(0, [[1, 8]]), bass.AP(D['sink'], 0, [[0, 128], [1, 8]]), [], ['esink0'])
            DMA('sp', lamt.ap(0, [[1, 256]]), bass.AP(D['lams'], 0, [[0, 128], [1, 256]]), [], ['lamt'])
            ACTV(esink.ap(0, [[1, 8]]), esink.ap(0, [[1, 8]]), AF.Exp, ['esink0'], ['esink'])
            ACTV(sT.ap(0, [[1, 16]]), cT.ap(0, [[1, 16]]), AF.Silu, ['cT'], ['sT'])
            TTO('dve', lamt.ap(0, [[1, 64]]), lamt.ap(0, [[1, 64]]), lamt.ap(64, [[1, 64]]), ALU.mult, ['lamt'], ['lamt1'])
            TTO('dve', lamt.ap(128, [[1, 64]]), lamt.ap(128, [[1, 64]]), lamt.ap(192, [[1, 64]]), ALU.mult, ['lamt'], ['lamt2'])
            P.op('dve', lambda e: e.reduce_sum(lams.ap(0, [[1, 1]]), lamt.ap(0, [[1, 64]]), axis=AX.X), reads=['lamt1'], writes=['lams0'])
            P.op('dve', lambda e: e.reduce_sum(lams.ap(1, [[1, 1]]), lamt.ap(128, [[1, 64]]), axis=AX.X), reads=['lamt2'], writes=['lams1'])
            ACTV(lams.ap(2, [[1, 2]]), lams.ap(0, [[1, 2]]), AF.Exp, ['lams0', 'lams1'], ['lams2'])
            TTO('dve', lams.ap(4, [[1, 1]]), lams.ap(3, [[1, 1]]), lams.ap(2, [[1, 1]]), ALU.subtract, ['lams2'], ['lams4'])
            TS('dve', lams.ap(4, [[1, 1]]), lams.ap(4, [[1, 1]]), -0.2, None, ALU.add, None, ['lams4'], ['neglam'])
            TS('dve', subln.ap(1, [[1, 1]]), subln.ap(0, [[1, 1]]), 0.8, None, ALU.mult, None, ['subln0'], ['subln'])
            for hp in range(4):
                DMA('sp', tmp.ap(0, [[1, 1792]]), D['rpbT'].ap()[:, hp * 1792:(hp + 1) * 1792], [], [('tmp', i) for i in range(4)])
                ACTV(tmp.ap(0, [[1, 1792]]), tmp.ap(0, [[1, 1792]]), AF.Copy, [('tmp', i) for i in range(4)], [('tmp', i) for i in range(4)], scale=8.0)
                TTO('dve', tmp.ap(0, [[64, 28], [1, 64]]), tmp.ap(0, [[64, 28], [1, 64]]), colok.ap(0, [[0, 28], [1, 64]]), ALU.mult,
                    [('tmp', i) for i in range(4)] + ['colok'], [('tmp', i) for i in range(4)])
                TTO('dve', bank.ap(hp * 1792, [[64, 28], [1, 64]]), tmp.ap(0, [[64, 28], [1, 64]]), colneg.ap(0, [[0, 28], [1, 64]]), ALU.add,
                    [('tmp', i) for i in range(4)] + ['colneg'], ['bank'])

        class WV:
            def __init__(self, slot, off, stride):
                self.slot = slot
                self.base = slot * 8192 + off
                self.stride = stride
                self.key = ('w', slot)

        def wload(src_ap, dims, extra=None, stride=512):
            sl = P.nxt('wslot', 2)
            DMA('pool', wr.ap(sl * 8192, dims), src_ap, [], [('w', sl)], semkey='w%d' % sl)
            if extra:
                for (off, dims2, src2) in extra:
                    DMA('pool', wr.ap(sl * 8192 + off, dims2), src2, [], [('w', sl)], semkey='w%d' % sl)
            return WV(sl, 0, stride)

        wcache = {}

        def wpiece(name, row0, col0):
            c1 = (col0 // 1024) * 1024
            key = (name, row0, c1)
            if wcache.get(name, (None, None))[0] != key:
                src = D[name].ap()[row0:row0 + 1024, c1:c1 + 1024].rearrange("(kc p) c -> p kc c", p=128)
                wcache[name] = (key, wload(src, [[1024, 8], [1, 1024]], stride=1024))
            wv = wcache[name][1]
            return WV(wv.slot, col0 - c1, 1024)

        def mod_piece(li, pc):
            b = bankA()
            s = wpiece('mod_w', li * 1024, pc * 512)
            for mc in range(4):
                for kc in range(8):
                    MM(psap(b, mc * 2, [[1, 2]]), wr.ap(s.base + kc * s.stride + mc * 128, [[1, 128]]), sT.ap(kc * 2, [[1, 2]]),
                       kc == 0, kc == 7, [s.key, 'sT'], [('ps', b)])
            TTO('dve', modv.ap(li * 96 + pc * 8, [[2, 4], [1, 2]]), psap(b, 0, [[2, 4], [1, 2]]), modb.ap(li * 48 + pc * 4, [[1, 4], [0, 2]]), ALU.add,
                [('ps', b), 'modb'], [('modv', li)])
            if pc == 3:
                mod_finish(li, 0)
            if pc == 11:
                mod_finish(li, 1)

        def compute_mod(li):
            mark('mod l%d' % li)
            for pc in range(4):
                mod_piece(li, pc)

        modq = []

        def mod_more(n=1):
            for _ in range(n):
                if modq:
                    mod_piece(0, modq.pop(0))

        def mod_finish(li, part):
            def mv(c0):
                return modv.ap(li * 96 + c0 * 2, [[2, 8], [1, 2]])

            def gn(w):
                return gains.ap(w * 16 + li * 8, [[1, 8], [0, 2]])

            def cf(w):
                return coef.ap(li * 64 + w * 16, [[2, 8], [1, 2]])
            if part == 0:
                STT('dve', cf(0), mv(8), 1.0, gn(0), ALU.add, ALU.mult, [('modv', li), 'gains'], [('coef', li)])
                return
            TTO('dve', cf(1), mv(16), gn(1), ALU.mult, [('modv', li), 'gains'], [('coef', li)])
            STT('dve', cf(2), mv(32), 1.0, gn(2), ALU.add, ALU.mult, [('modv', li), 'gains'], [('coef', li)])
            TTO('dve', cf(3), mv(40), gn(3), ALU.mult, [('modv', li), 'gains'], [('coef', li)])

        def coefap(li, w, kc, j):
            return coef.ap(li * 64 + w * 16 + kc * 2 + j, [[1, 1]])

        def shap(li, which, kc, j):
            c0 = 0 if which == 1 else 24
            return modv.ap(li * 96 + (c0 + kc) * 2 + j, [[1, 1]])

        def load_x(u):
            mark('load_x u%d' % u)
            xd = D['xP'] if u == 0 else D['xS']
            for t in range(8):
                g = t // 4
                for hf in range(2):
                    ti = tmpi()
                    DMA('sp', tmp.ap(ti * 512, [[1, 512]]), xd.ap()[t * 128:(t + 1) * 128, hf * 512:(hf + 1) * 512], [], [('tmp', ti)])
                    b = bankA()
                    for c in range(4):
                        TR(psap(b, c * 128, [[1, 128]]), tmp.ap(ti * 512 + c * 128, [[1, 128]]), [('tmp', ti)], [('ps', b)])
                    CP(['dve', 'act'][g], xT.ap((hf * 4) * 1024 + t * 128, [[1024, 4], [1, 128]]), psap(b, 0, [[128, 4], [1, 128]]), [('ps', b)],
                       [('x', hf * 4 + c, g) for c in range(4)])

        def store_x(u):
            mark('store_x u%d' % u)
            run_tail()
            yd = D['yP'] if u == 0 else D['yS']
            for t in range(8):
                g = t // 4
                for hf in range(2):
                    b = bankA()
                    for c in range(4):
                        TR(psap(b, c * 128, [[1, 128]]), xT.ap((hf * 4 + c) * 1024 + t * 128, [[1, 128]]), [('x', hf * 4 + c, g)], [('ps', b)])
                    ti = tmpi()
                    CP(evq(), tmp.ap(ti * 512, [[1, 512]]), psap(b, 0, [[1, 512]]), [('ps', b)], [('tmp', ti)])
                    out_ops.append(DMA('sp', yd.ap()[t * 128:(t + 1) * 128, hf * 512:(hf + 1) * 512], tmp.ap(ti * 512, [[1, 512]]), [('tmp', ti)], []))

        def rstd_from(b, ri, scale):
            ACTV(rstd.ap(ri * 512, [[1, 512]]), psap(b, 0, [[1, 512]]), AF.Ln, [('ps', b), 'epsb'], [('rstd', ri)], bias=cmap['eps'], scale=scale)
            ACTV(rstd.ap(ri * 512, [[1, 512]]), rstd.ap(ri * 512, [[1, 512]]), AF.Exp, [('rstd', ri)], [('rstd', ri)], scale=-0.5)

        tailq = []

        def run_tail():
            while tailq:
                tailq.pop(0)()

        def norm_mod(u, li, which, between=None):
            mark('norm u%d l%d w%d' % (u, li, which))
            j = u
            w = 0 if which == 1 else 2
            for g in range(2):
                b = bankB()
                for kc in range(8):
                    si = sqi()
                    ACTV(sqt.ap(si * 512, [[1, 512]]), xT.ap(kc * 1024 + g * 512, [[1, 512]]), AF.Square, [('x', kc, g)], [('sq', si)])
                    MM(psap(b, 0, [[1, 512]]), ones.ap(0, [[1, 128]]), sqt.ap(si * 512, [[1, 512]]), kc == 0, kc == 7, [('sq', si), 'ones'], [('ps', b)])
                ri = P.nxt('rstd', 2)
                rstd_from(b, ri, 1.0 / 1024)
                for kc in range(8):
                    ti = tmpi()
                    STT('dve', tmp.ap(ti * 512, [[1, 512]]), xT.ap(kc * 1024 + g * 512, [[1, 512]]), coefap(li, w, kc, j), rstd.ap(ri * 512, [[1, 512]]),
                        ALU.mult, ALU.mult, [('x', kc, g), ('coef', li), ('rstd', ri)], [('tmp', ti)])
                    ACTV(hT.ap(kc * 1024 + g * 512, [[1, 512]]), tmp.ap(ti * 512, [[1, 512]]), AF.Identity, [('tmp', ti), ('modv', li)], [('h', kc, g)],
                         bias=shap(li, which, kc, j), scale=1.0)
                if g == 0:
                    run_tail()
                    if between is not None:
                        between()

        def dense_fm(s, mc, g, evac, kcn=8, wstride=512, src=None, srckey='h'):
            b = bankA()
            for kc in range(kcn):
                MM(psap(b, 0, [[1, 512]]), wr.ap(s.base + kc * s.stride + mc * 128, [[1, 128]]),
                   (src or hT).ap(kc * 1024 + g * 512, [[1, 512]]), kc == 0, kc == kcn - 1, [s.key, (srckey, kc, g)], [('ps', b)])
            evac(b)

        def dense_tm(s, col0, ncols, t, evac, wstride=512):
            b = bankA()
            g = t // 4
            for kc in range(8):
                MM(psap(b, 0, [[1, ncols]]), hT.ap(kc * 1024 + t * 128, [[1, 128]]), wr.ap(s.base + kc * s.stride + col0, [[1, ncols]]),
                   kc == 0, kc == 7, [s.key, ('h', kc, g)], [('ps', b)])
            evac(b)

        QT, KT = 0, 4096
        VT0, AB, UU = 9216, 14336, 18432
        Q2, K2, VT1, VT2, VD = 9216, 13312, 15872, 20992, 24576

        def rope_evac(b, g, dst_off, dst_keys):
            if 'rope' in SKIP:
                CP(evq(), ar.ap(dst_off, [[1, 512]]), psap(b, 0, [[1, 512]]), [('ps', b)], dst_keys)
                return
            ei = Ei()
            CP('act', Et.ap(ei * 512, [[1, 512]]), psap(b, 0, [[1, 512]]), [('ps', b)], [('E', ei)])
            b2 = bankA()
            MM(psap(b2, 0, [[1, 512]]), R2.ap(0, [[1, 128]]), Et.ap(ei * 512, [[1, 512]]), True, True, [('E', ei), 'R2'], [('ps', b2)])
            t1 = tmpi()
            TTO('dve', tmp.ap(t1 * 512, [[1, 512]]), psap(b, 0, [[1, 512]]), cosT.ap(g * 512, [[1, 512]]), ALU.mult, [('ps', b), 'cos'], [('tmp', t1)])
            t2 = tmpi()
            TTO('dve', tmp.ap(t2 * 512, [[1, 512]]), psap(b2, 0, [[1, 512]]), sinT.ap(g * 512, [[1, 512]]), ALU.mult, [('ps', b2), 'sin'], [('tmp', t2)])
            TTO('dve', ar.ap(dst_off, [[1, 512]]), tmp.ap(t1 * 512, [[1, 512]]), tmp.ap(t2 * 512, [[1, 512]]), ALU.add, [('tmp', t1), ('tmp', t2)], dst_keys)

        def load_ctx_k(name, nh, dst_off, dst_stride, nchunk, keyname, dup=False):
            for tt in range(2):
                ti = tmpi()
                if not dup:
                    DMA('sp', tmp.ap(ti * 512, [[64, nh], [1, 64]]), bass.AP(D[name], tt * 8192, [[64, 128], [16384, nh], [1, 64]]), [], [('tmp', ti)])
                else:
                    for dpl in range(2):
                        DMA('sp', tmp.ap(ti * 512 + dpl * 64, [[128, nh], [1, 64]]), bass.AP(D[name], tt * 8192, [[64, 128], [16384, nh], [1, 64]]), [], [('tmp', ti)])
                b = bankA()
                for c in range(nchunk):
                    TR(psap(b, c * 128, [[1, 128]]), tmp.ap(ti * 512 + c * 128, [[1, 128]]), [('tmp', ti)], [('ps', b)])
                CP('dve', ar.ap(dst_off + 1024 + tt * 128, [[dst_stride, nchunk], [1, 128]]), psap(b, 0, [[128, nchunk], [1, 128]]), [('ps', b)],
                   [(keyname, c, 'ctx') for c in range(nchunk)])

        class Pipe:
            DEPTH = 2

            def __init__(self):
                self.round = []
                self.pending = []
                self.late_prev = []
                self.late_new = []

            def add(self, item):
                if self.round and (self.round[0].get('sb') is None) != (item.get('sb') is None):
                    self.emit_round()
                self.round.append(item)
                if len(self.round) == 2:
                    self.emit_round()

            def emit_round(self):
                items = self.round
                self.round = []
                if not items:
                    return
                er = P.nxt('Er', 6)
                ebase = er * 1024
                ekey = [('yb', er // 4, (er % 4) * 2), ('yb', er // 4, (er % 4) * 2 + 1)]
                paired = items[0].get('sb') is not None
                bx = bankSCpair()
                by = bx + 1
                if paired:
                    offs = []
                    o = 0
                    for it in items:
                        offs.append(o)
                        o += it['nq']
                    tot = o
                    for k, (it, of) in enumerate(zip(items, offs)):
                        it['sa'](bx, of, k == 0)
                    for k, (it, of) in enumerate(zip(items, offs)):
                        it['sb'](by, of, k == 0)
                    for it, of in zip(items, offs):
                        if it.get('bias'):
                            it['bias'](bx, by, of)
                    ACTV(yb.ap(ebase, [[512, 2], [1, tot]]), psap(bx, 0, [[512, 2], [1, tot]]), AF.Exp, [('ps', bx), ('ps', by)], ekey, scale=0.125)
                    for it, of in zip(items, offs):
                        it['ea'] = ebase + of
                        it['eb'] = ebase + 512 + of
                else:
                    banks = [bx, by]
                    for k, it in enumerate(items):
                        it['sa'](banks[k], 0, True)
                    if len(items) == 2 and items[0]['nq'] == 512 and items[1]['nq'] == 512:
                        ACTV(yb.ap(ebase, [[1, 1024]]), psap(bx, 0, [[1, 1024]]), AF.Exp, [('ps', bx), ('ps', by)], ekey, scale=0.125)
                    else:
                        for k, it in enumerate(items):
                            ACTV(yb.ap(ebase + k * 512, [[1, it['nq']]]), psap(banks[k], 0, [[1, it['nq']]]), AF.Exp, [('ps', banks[k])], ekey, scale=0.125)
                    for k, it in enumerate(items):
                        it['ea'] = ebase + k * 512
                        it['eb'] = None

                def pend(items=items, ekey=ekey):
                    for it in items:
                        it['post'](it['ea'], it['eb'], ekey)
                self.pending.append(pend)
                while len(self.pending) > self.DEPTH:
                    self.pending.pop(0)()
                for f in self.late_prev:
                    f()
                self.late_prev = self.late_new
                self.late_new = []

            def flush(self):
                self.emit_round()
                while self.pending:
                    self.pending.pop(0)()
                    for f in self.late_prev:
                        f()
                    self.late_prev = self.late_new
                    self.late_new = []
                for f in self.late_prev + self.late_new:
                    f()
                self.late_prev = []
                self.late_new = []

        pipe = Pipe()

        def attn_pair(ktiles, qa, qb, qkeys, nq, dst_off, dst_keys, sink_h=None):
            bo = bankACC()
            bd = bankACC()
            nk = len(ktiles)

            def final():
                ti = tmpi()
                if sink_h is not None:
                    ACTV(tmp.ap(ti * 512, [[1, nq]]), psap(bd, 0, [[1, nq]]), AF.Ln, [('ps', bd), 'esink'], [('tmp', ti)], bias=esink.ap(sink_h, [[1, 1]]), scale=1.0)
                    ACTV(tmp.ap(ti * 512 + nq, [[1, nq]]), psap(bd, nq, [[1, nq]]), AF.Ln, [('ps', bd), 'esink'], [('tmp', ti)], bias=esink.ap(sink_h + 1, [[1, 1]]), scale=1.0)
                else:
                    ACTV(tmp.ap(ti * 512, [[1, 2 * nq]]), psap(bd, 0, [[1, 2 * nq]]), AF.Ln, [('ps', bd)], [('tmp', ti)])
                ACTV(tmp.ap(ti * 512, [[1, 2 * nq]]), tmp.ap(ti * 512, [[1, 2 * nq]]), AF.Exp, [('tmp', ti)], [('tmp', ti)], scale=-1.0)
                TTO('dve', hT.ap(dst_off, [[1, nq]], 0, 64), psap(bo, 0, [[1, nq]], 0, 64), tmp.ap(ti * 512, [[1, nq]], 0, 64), ALU.mult, [('ps', bo), ('tmp', ti)], dst_keys)
                TTO('dve', hT.ap(dst_off, [[1, nq]], 64, 64), psap(bo, nq, [[1, nq]], 64, 64), tmp.ap(ti * 512 + nq, [[1, nq]], 64, 64), ALU.mult, [('ps', bo), ('tmp', ti)], dst_keys)

            for i, kt in enumerate(ktiles):
                hasb = kt.get('ma') is not None

                def sa(b, of, first, kt=kt, hasb=hasb):
                    MM(psap(b, of, [[1, nq]]), kt['ka'], qa, first, not hasb, kt['kkeys'] + qkeys, [('ps', b)], skip=True)

                def sb(b, of, first, kt=kt, hasb=hasb):
                    MM(psap(b, of, [[1, nq]]), kt['kb'], qb, first, not hasb, kt['kkeys'] + qkeys, [('ps', b)], skip=True)

                bias = None
                if hasb:
                    def bias(bx, by, of, kt=kt):
                        MM(psap(bx, of, [[1, nq]]), identb.ap(0, [[1, 128]]), kt['ma'], False, True, kt['mkeys'] + ['identb'], [('ps', bx)], skip=True)
                        MM(psap(by, of, [[1, nq]]), identb.ap(0, [[1, 128]]), kt['mb'], False, True, kt['mkeys'] + ['identb'], [('ps', by)], skip=True)

                def post(ea, eb, ekey, i=i, kt=kt):
                    MM(psap(bo, 0, [[1, nq]]), kt['v'], yb.ap(ea, [[1, nq]]), i == 0, False, ekey + kt['vkeys'], [('ps', bo)], skip=True)
                    MM(psap(bo, nq, [[1, nq]]), kt['v'], yb.ap(eb, [[1, nq]]), False, i == nk - 1, ekey + kt['vkeys'], [('ps', bo)], skip=True)
                    MM(psap(bd, 0, [[nq, 2], [1, nq]]), ones.ap(0, [[1, 128]]), yb.ap(ea, [[eb - ea, 2], [1, nq]]), i == 0, i == nk - 1, ekey + ['ones'], [('ps', bd)])
                    if i == nk - 1:
                        final()
                pipe.add(dict(sa=sa, sb=sb, bias=bias, post=post, nq=nq))

        def diff_attn(u):
            mark('diff_attn u%d' % u)
            if u == 0:
                blocks = [(s * 256, 256, [s * 2, s * 2 + 1]) for s in range(4)]
            else:
                blocks = [(0, 512, list(range(10))), (512, 512, list(range(10)))]
            nblk = 0
            for h in range(4):
                for (q0, nq, kts) in blocks:
                    if u == 0:
                        if modq:
                            mod_more()
                        elif nblk - 4 < 12:
                            mod_piece(1, nblk - 4)
                    nblk += 1
                    g = q0 // 512
                    packed = (nq == 256)
                    ncols = 2 * nq if packed else nq
                    accs = []

                    def sublnfin(accs=accs, h=h, q0=q0, nq=nq, g=g, packed=packed):
                        tdk = P.nxt('tdslot', 2)
                        tdoff = (6 + tdk) * 1024
                        tdkey = [('yb', 1, 4 + 2 * tdk), ('yb', 1, 5 + 2 * tdk)]
                        if packed:
                            o0 = tmp.ap(accs[0] * 512, [[1, nq]]); o1 = tmp.ap(accs[0] * 512 + nq, [[1, nq]]); rk = [('tmp', accs[0])]
                        else:
                            o0 = tmp.ap(accs[0] * 512, [[1, nq]]); o1 = tmp.ap(accs[1] * 512, [[1, nq]]); rk = [('tmp', accs[0]), ('tmp', accs[1])]
                        STT('dve', yb.ap(tdoff, [[1, nq]]), o1, lams.ap(4, [[1, 1]]), o0, ALU.mult, ALU.add, rk + ['neglam'], tdkey)
                        si = sqi()
                        ACTV(sqt.ap(si * 512, [[1, nq]]), yb.ap(tdoff, [[1, nq]]), AF.Square, tdkey, [('sq', si)])

                        def stage2(si=si, tdoff=tdoff, tdkey=tdkey):
                            bs = bankACC()
                            MM(psap(bs, 0, [[1, nq]]), ones.ap(0, [[1, 128]]), sqt.ap(si * 512, [[1, nq]]), True, True, [('sq', si), 'ones'], [('ps', bs)])
                            ri = P.nxt('rstd', 2)
                            ACTV(rstd.ap(ri * 512, [[1, nq]]), psap(bs, 0, [[1, nq]]), AF.Ln, [('ps', bs), 'epsb'], [('rstd', ri)], bias=cmap['eps'], scale=1.0 / 128)
                            ACTV(rstd.ap(ri * 512, [[1, nq]]), rstd.ap(ri * 512, [[1, nq]]), AF.Exp, [('rstd', ri)], [('rstd', ri)], scale=-0.5)
                            STT('dve', hT.ap((4 + h) * 1024 + q0, [[1, nq]]), yb.ap(tdoff, [[1, nq]]), subln.ap(1, [[1, 1]]), rstd.ap(ri * 512, [[1, nq]]), ALU.mult, ALU.mult,
                                tdkey + [('rstd', ri), 'subln'], [('h', 4 + h, g)])
                        pipe.late_new.append(stage2)

                    maps = [None] if packed else [0, 1]
                    for m in maps:
                        bo = bankACC()
                        bd = bankACC()
                        nk = len(kts)
                        for i, kt in enumerate(kts):
                            kcol = kt * 128 if kt < 8 else 1024 + (kt - 8) * 128
                            kkey = [('k', h, kt // 4 if kt < 8 else 'ctx')]
                            vkey = [('v', kt)]

                            def smm(b, of, mm_, kcol=kcol, kkey=kkey, h=h, q0=q0, nq=nq, g=g):
                                MM(psap(b, of, [[1, nq]]), ar.ap(KT + h * 1280 + kcol, [[1, 128]], mm_ * 64, 64), ar.ap(QT + h * 1024 + q0, [[1, nq]], mm_ * 64, 64),
                                   True, True, kkey + [('q', h, g)], [('ps', b)])
                            if packed:
                                sa = lambda b, of, first, smm=smm: smm(b, of, 0)
                                sb = lambda b, of, first, smm=smm: smm(b, of, 1)
                            else:
                                sa = lambda b, of, first, smm=smm, m=m: smm(b, of, m)
                                sb = None

                            def post(ea, eb, ekey, i=i, kt=kt, vkey=vkey, bo=bo, bd=bd, nk=nk, ncols=ncols, nq=nq, h=h, m=m, maps=maps, accs=accs, sublnfin=sublnfin, packed=packed):
                                if packed:
                                    rhs = yb.ap(ea, [[eb - ea, 2], [1, nq]])
                                    oap = lambda bk: psap(bk, 0, [[nq, 2], [1, nq]])
                                else:
                                    rhs = yb.ap(ea, [[1, nq]])
                                    oap = lambda bk: psap(bk, 0, [[1, nq]])
                                MM(oap(bo), ar.ap(VT0 + kt * 512 + h * 128, [[1, 128]]), rhs, i == 0, i == nk - 1, ekey + vkey, [('ps', bo)])
                                MM(oap(bd), ones.ap(0, [[1, 128]]), rhs, i == 0, i == nk - 1, ekey + ['ones'], [('ps', bd)])
                                if i == nk - 1:
                                    tr = tmpi()
                                    ACTV(tmp.ap(tr * 512, [[1, ncols]]), psap(bd, 0, [[1, ncols]]), AF.Ln, [('ps', bd)], [('tmp', tr)])
                                    ACTV(tmp.ap(tr * 512, [[1, ncols]]), tmp.ap(tr * 512, [[1, ncols]]), AF.Exp, [('tmp', tr)], [('tmp', tr)], scale=-1.0)
                                    to = tmpi()
                                    TTO('dve', tmp.ap(to * 512, [[1, ncols]]), psap(bo, 0, [[1, ncols]]), tmp.ap(tr * 512, [[1, ncols]]), ALU.mult, [('ps', bo), ('tmp', tr)], [('tmp', to)])
                                    accs.append(to)
                                    if m == maps[-1]:
                                        sublnfin()
                            pipe.add(dict(sa=sa, sb=sb, post=post, nq=nq))

        deferred_pe = []

        def flush_deferred():
            while deferred_pe:
                deferred_pe.pop(0)()

        def post_evac(b, g, mo, bs):
            CP('act', yb.ap(g * 4096 + mo * 512, [[1, 512]]), psap(b, 0, [[1, 512]]), [('ps', b)], [('yb', g, mo)])
            si = sqi()
            ACTV(sqt.ap(si * 512, [[1, 512]]), psap(b, 0, [[1, 512]]), AF.Square, [('ps', b)], [('sq', si)])
            flush_deferred()
            deferred_pe.append(lambda: MM(psap(bs, 0, [[1, 512]]), ones.ap(0, [[1, 128]]), sqt.ap(si * 512, [[1, 512]]), mo == 0, mo == 7, [('sq', si), 'ones'], [('ps', bs)]))

        def post_update(u, li, w, g, bs):
            flush_deferred()
            ri = P.nxt('rstd', 2)
            ACTV(rstd.ap(ri * 512, [[1, 512]]), psap(bs, 0, [[1, 512]]), AF.Ln, [('ps', bs), 'epsb'], [('rstd', ri)], bias=cmap['eps'], scale=1.0 / 1024)
            ACTV(Et.ap(3 * 512, [[1, 512]]), rstd.ap(ri * 512, [[1, 512]]), AF.Exp, [('rstd', ri)], [('E', 3)], scale=-0.5)
            for mo in range(8):
                ti = P.nxt('Et3', 3)
                TTO('dve', Et.ap(ti * 512, [[1, 512]]), yb.ap(g * 4096 + mo * 512, [[1, 512]]), Et.ap(3 * 512, [[1, 512]]), ALU.mult, [('yb', g, mo), ('E', 3)], [('E', ti)])
                STT('dve', xT.ap(mo * 1024 + g * 512, [[1, 512]]), Et.ap(ti * 512, [[1, 512]]), coefap(li, w, mo, u), xT.ap(mo * 1024 + g * 512, [[1, 512]]), ALU.mult, ALU.add,
                    [('E', ti), ('coef', li), ('x', mo, g)], [('x', mo, g)])

        def wout_phase(u, li):
            mark('wout u%d l%d' % (u, li))
            s0 = wpiece('w_out', li * 1024, 0)
            s1 = wpiece('w_out', li * 1024, 512)
            for g in range(2):
                bs = bankB()
                for mo in range(8):
                    s = s0 if mo < 4 else s1
                    dense_fm(s, mo % 4, g, lambda b, g=g, mo=mo, bs=bs: post_evac(b, g, mo, bs))
                if g == 0:
                    post_update(u, li, 1, g, bs)
                else:
                    flush_deferred()
                    tailq.append(lambda bs=bs: post_update(u, li, 1, 1, bs))

        def mlp_phase(u, li):
            def mlp1_piece(pc, g):
                s = wpiece('mlp_w1', li * 1024, pc * 512)
                for mc in range(4):
                    m = pc * 4 + mc

                    def ev(b, m=m, g=g):
                        ei = Ei()
                        ACTV(Et.ap(ei * 512, [[1, 512]]), psap(b, 0, [[1, 512]]), AF.Relu, [('ps', b)], [('E', ei)])
                        TTO('dve', ar.ap(m * 1024 + g * 512, [[1, 512]]), Et.ap(ei * 512, [[1, 512]]), Et.ap(ei * 512, [[1, 512]]), ALU.mult, [('E', ei)], [('aT', m, g)])
                    dense_fm(s, mc, g, ev)

            def first_g0():
                mlp1_piece(0, 0)
                mlp1_piece(1, 0)
            norm_mod(u, li, 2, between=first_g0)
            mark('mlp1 u%d l%d' % (u, li))
            mlp1_piece(0, 1)
            mlp1_piece(1, 1)
            for pc in range(2, 8):
                for g in range(2):
                    mlp1_piece(pc, g)
            mark('mlp2 u%d l%d' % (u, li))
            bss = [bankB(), bankB()]
            for ld in range(4):
                src = D['mlp_w2'].ap()[li * 4096:(li + 1) * 4096, ld * 256:(ld + 1) * 256].rearrange("(kc p) c -> p kc c", p=128)
                s2 = wload(src, [[256, 32], [1, 256]], stride=256)
                for g in range(2):
                    for mo in (2 * ld, 2 * ld + 1):
                        s = WV(s2.slot, (mo % 2) * 128, 256)
                        dense_fm(s, 0, g, lambda b, g=g, mo=mo: post_evac(b, g, mo, bss[g]), kcn=32, wstride=128, src=ar, srckey='aT')
                    if ld == 3 and g == 0:
                        post_update(u, li, 3, 0, bss[0])
            flush_deferred()
            tailq.append(lambda: post_update(u, li, 3, 1, bss[1]))

        def layer0(u):
            def piece_ab(g):
                s = wpiece('w_in_even', 0, 0)
                for c in range(4):
                    dense_fm(s, c, g, lambda b, c=c, g=g: CP('act', ar.ap(AB + c * 1024 + g * 512, [[1, 512]]), psap(b, 0, [[1, 512]]), [('ps', b)], [('ab', c, g)]))

            def piece_ac(g):
                s = wpiece('w_in_even', 0, 512)
                for c in range(4):
                    dense_fm(s, c, g, lambda b, c=c, g=g: CP('act', ar.ap(UU + c * 1024 + g * 512, [[1, 512]]), psap(b, 0, [[1, 512]]), [('ps', b)], [('u', c, g)]))

            def first_g0():
                piece_ab(0)
                piece_ac(0)
            norm_mod(u, 0, 1, between=first_g0)
            fence = [('h', 0, 0)]
            piece_ab(1)
            mod_more()
            piece_ac(1)
            mod_more()
            s = wpiece('w_in_even', 0, 1024)
            for c in range(4):
                for g in range(2):
                    dense_fm(s, c, g, lambda b, c=c, g=g: TTO('dve', ar.ap(UU + c * 1024 + g * 512, [[1, 512]]), psap(b, 0, [[1, 512]]), ar.ap(UU + c * 1024 + g * 512, [[1, 512]]),
                                                               ALU.mult, [('ps', b), ('u', c, g)], [('u', c, g)]))
            mod_more()
            s = wpiece('w_in_even', 0, 1536)
            for c in range(4):
                for g in range(2):
                    if u == 0:
                        dense_fm(s, c, g, lambda b, c=c, g=g: CP(evq(), ar.ap(QT + c * 1024 + g * 512, [[1, 512]]), psap(b, 0, [[1, 512]]), [('ps', b)], [('q', c, g)]))
                    else:
                        dense_fm(s, c, g, lambda b, c=c, g=g: rope_evac(b, g, QT + c * 1024 + g * 512, [('q', c, g)]))
            mod_more()
            if u == 1 and 'ctx' not in SKIP:
                load_ctx_k('cdk', 8, KT, 1280, 4, 'k')
            s = wpiece('w_in_even', 0, 2048)
            for c in range(4):
                for g in range(2):
                    if u == 0:
                        dense_fm(s, c, g, lambda b, c=c, g=g: CP(evq(), ar.ap(KT + c * 1280 + g * 512, [[1, 512]]), psap(b, 0, [[1, 512]]), [('ps', b)], [('k', c, g)]))
                    else:
                        dense_fm(s, c, g, lambda b, c=c, g=g: rope_evac(b, g, KT + c * 1280 + g * 512, [('k', c, g)]))
            if u == 0 and 'ktm' not in SKIP:
                for t in range(8):
                    def ev(b, t=t):
                        ti = tmpi()
                        CP('dve', tmp.ap(ti * 512, [[1, 512]]), psap(b, 0, [[1, 512]]), [('ps', b)], [('tmp', ti)])
                        sq_, tt = t // 2, t % 2
                        out_ops.append(DMA('sp', bass.AP(D['ndk'], sq_ * 8 * 16384 + tt * 8192, [[64, 128], [16384, 8], [1, 64]]), tmp.ap(ti * 512, [[64, 8], [1, 64]]), [('tmp', ti)], []))
                    dense_tm(s, 0, 512, t, ev)
            s = wpiece('w_in_even', 0, 2560)
            for t in range(8):
                def ev(b, t=t):
                    CP('act', ar.ap(VT0 + t * 512, [[1, 512]]), psap(b, 0, [[1, 512]]), [('ps', b)], [('v', t)])
                    if u == 0 and 'vout' not in SKIP:
                        ti = tmpi()
                        CP('dve', tmp.ap(ti * 512, [[1, 512]]), psap(b, 0, [[1, 512]]), [('ps', b)], [('tmp', ti)])
                        sq_, tt = t // 2, t % 2
                        out_ops.append(DMA('sp', bass.AP(D['ndv'], sq_ * 4 * 32768 + tt * 16384, [[128, 128], [32768, 4], [1, 128]]), tmp.ap(ti * 512, [[128, 4], [1, 128]]), [('tmp', ti)], []))
                dense_tm(s, 0, 512, t, ev)
            mark('conv u%d' % u)
            nseq, sl = (4, 256) if u == 0 else (1, 1024)
            for c in range(4 if 'conv' not in SKIP else 0):
                ukeys = [('u', c, 0), ('u', c, 1)]
                t1 = tmpi(); t2 = tmpi()
                acc_keys = [('tmp', t1), ('tmp', t2)]
                acc = lambda off, n: tmp.ap(t1 * 512 + off, [[1, n]])
                if t2 != t1 + 1:
                    t1 = tmpi(); t2 = tmpi()
                    acc_keys = [('tmp', t1), ('tmp', t2)]
                base = t1 * 512
                TS('dve', tmp.ap(base, [[1, 1024]]), ar.ap(UU + c * 1024, [[1, 1024]]), cw.ap(c * 3 + 1, [[1, 1]]), None, ALU.mult, None, ukeys + ['cw'], acc_keys)
                STT('dve', tmp.ap(base + 1, [[sl, nseq], [1, sl - 1]]), ar.ap(UU + c * 1024, [[sl, nseq], [1, sl - 1]]), cw.ap(c * 3 + 0, [[1, 1]]),
                    tmp.ap(base + 1, [[sl, nseq], [1, sl - 1]]), ALU.mult, ALU.add, ukeys + ['cw'] + acc_keys, acc_keys)
                STT('dve', tmp.ap(base, [[sl, nseq], [1, sl - 1]]), ar.ap(UU + c * 1024 + 1, [[sl, nseq], [1, sl - 1]]), cw.ap(c * 3 + 2, [[1, 1]]),
                    tmp.ap(base, [[sl, nseq], [1, sl - 1]]), ALU.mult, ALU.add, ukeys + ['cw'] + acc_keys, acc_keys)
                TTO('dve', hT.ap(c * 1024, [[1, 1024]]), tmp.ap(base, [[1, 1024]]), ar.ap(AB + c * 1024, [[1, 1024]]), ALU.mult,
                    acc_keys + [('ab', c, 0), ('ab', c, 1)], [('h', c, 0), ('h', c, 1)])
            if u == 1 and 'ctx' not in SKIP:
                for tt in range(2):
                    DMA('pool', ar.ap(VT0 + (8 + tt) * 512, [[128, 4], [1, 128]]), bass.AP(D['cdv'], tt * 16384, [[128, 128], [32768, 4], [1, 128]]), fence, [('v', 8 + tt)])
            if stop_after == ('A', u):
                raise StopBuild('h')
            diff_attn(u)
            if stop_after == ('B', u):
                pipe.flush()
                raise StopBuild('h')
            pipe.flush()
            wout_phase(u, 0)
            if stop_after == ('C', u):
                raise StopBuild('x')
            mlp_phase(u, 0)

        def layer1(u):
            def piece_cq(g):
                s = wpiece('w_in_odd', 0, 0)
                for c in range(4):
                    dense_fm(s, c, g, lambda b, c=c, g=g: CP(evq(), ar.ap(QT + c * 1024 + g * 512, [[1, 512]]), psap(b, 0, [[1, 512]]), [('ps', b)], [('q', c, g)]))

            def piece_ck(g):
                s = wpiece('w_in_odd', 0, 512)
                for c in range(4):
                    dense_fm(s, c, g, lambda b, c=c, g=g: CP(evq(), ar.ap(KT + c * 1280 + g * 512, [[1, 512]]), psap(b, 0, [[1, 512]]), [('ps', b)], [('k', c, g)]))

            def first_g0():
                piece_cq(0)
                piece_ck(0)
            norm_mod(u, 1, 1, between=first_g0)
            fence = [('h', 0, 0)]
            piece_cq(1)
            piece_ck(1)
            s = wpiece('w_in_odd', 0, 512)
            if u == 1:
                load_ctx_k('cnk', 8, KT, 1280, 4, 'k')
                load_ctx_k('csk', 2, K2, 1280, 2, 'k2', dup=True)
            if u == 0:
                for t in range(8):
                    def ev(b, t=t):
                        ti = tmpi()
                        CP('dve', tmp.ap(ti * 512, [[1, 512]]), psap(b, 0, [[1, 512]]), [('ps', b)], [('tmp', ti)])
                        sq_, tt = t // 2, t % 2
                        out_ops.append(DMA('sp', bass.AP(D['nnk'], sq_ * 8 * 16384 + tt * 8192, [[64, 128], [16384, 8], [1, 64]]), tmp.ap(ti * 512, [[64, 8], [1, 64]]), [('tmp', ti)], []))
                    dense_tm(s, 0, 512, t, ev)
            s = wpiece('w_in_odd', 0, 1024)
            for t in range(8):
                def ev(b, t=t):
                    CP('act', ar.ap(VT1 + t * 512, [[1, 512]]), psap(b, 0, [[1, 512]]), [('ps', b)], [('v', t)])
                    if u == 0:
                        ti = tmpi()
                        CP('dve', tmp.ap(ti * 512, [[1, 512]]), psap(b, 0, [[1, 512]]), [('ps', b)], [('tmp', ti)])
                        sq_, tt = t // 2, t % 2
                        out_ops.append(DMA('sp', bass.AP(D['nnv'], sq_ * 8 * 16384 + tt * 8192, [[64, 128], [16384, 8], [1, 64]]), tmp.ap(ti * 512, [[64, 8], [1, 64]]), [('tmp', ti)], []))
                dense_tm(s, 0, 512, t, ev)
            if u == 1:
                for i in range(7):
                    b = bankA()
                    for kc in range(8):
                        gk = [('h', kc, 0), ('h', kc, 1)]
                        MM(psap(b, 0, [[1, 512]]), hT.ap(kc * 1024 + 64 + i * 128, [[1, 128]]), wr.ap(s.base + kc * s.stride, [[1, 512]]), kc == 0, kc == 7, [s.key] + gk, [('ps', b)])
                    CP(evq(), ar.ap(VT2 + i * 512, [[1, 512]]), psap(b, 0, [[1, 512]]), [('ps', b)], [('v2', i)])
            s = wpiece('w_in_odd', 0, 1536)
            for c in range(4):
                for g in range(2):
                    if u == 0:
                        dense_fm(s, c, g, lambda b, c=c, g=g: CP(evq(), ar.ap(Q2 + c * 1024 + g * 512, [[1, 512]]), psap(b, 0, [[1, 512]]), [('ps', b)], [('q2', c, g)]))
                    else:
                        dense_fm(s, c, g, lambda b, c=c, g=g: rope_evac(b, g, Q2 + c * 1024 + g * 512, [('q2', c, g)]))
            srcA = bass.AP(D['w_in_odd'], 2048, [[2304, 128], [2304 * 128, 8], [1, 256]])
            extra = []
            for g_ in range(2):
                for dpl in range(2):
                    extra.append((2048 + g_ * 128 + dpl * 64, [[256, 8], [1, 64]], bass.AP(D['w_in_odd'], 2048 + g_ * 64, [[2304, 128], [2304 * 128, 8], [1, 64]])))
            s = wload(srcA, [[256, 8], [1, 256]], extra=extra, stride=256)
            for gk_ in range(2):
                for g in range(2):
                    b = bankA()
                    for kc in range(8):
                        MM(psap(b, 0, [[1, 512]]), wr.ap(s.base + 2048 + kc * 256 + gk_ * 128, [[1, 128]]), hT.ap(kc * 1024 + g * 512, [[1, 512]]), kc == 0, kc == 7,
                           [s.key, ('h', kc, g)], [('ps', b)])
                    if u == 0:
                        CP(evq(), ar.ap(K2 + gk_ * 1280 + g * 512, [[1, 512]]), psap(b, 0, [[1, 512]]), [('ps', b)], [('k2', gk_, g)])
                    else:
                        rope_evac(b, g, K2 + gk_ * 1280 + g * 512, [('k2', gk_, g)])
            for t in range(8):
                def ev(b, t=t):
                    for dpl in range(2):
                        CP(['act', 'dve'][dpl], ar.ap(VD + t * 256 + dpl * 64, [[128, 2], [1, 64]]), psap(b, 128, [[64, 2], [1, 64]]), [('ps', b)], [('vd', t)])
                    if u == 0:
                        ti = tmpi()
                        CP('dve', tmp.ap(ti * 512, [[1, 256]]), psap(b, 0, [[1, 256]]), [('ps', b)], [('tmp', ti)])
                        sq_, tt = t // 2, t % 2
                        out_ops.append(DMA('sp', bass.AP(D['nsk'], sq_ * 2 * 16384 + tt * 8192, [[64, 128], [16384, 2], [1, 64]]), tmp.ap(ti * 512, [[64, 2], [1, 64]]), [('tmp', ti)], []))
                        out_ops.append(DMA('sp', bass.AP(D['nsv'], sq_ * 2 * 16384 + tt * 8192, [[64, 128], [16384, 2], [1, 64]]), tmp.ap(ti * 512 + 128, [[64, 2], [1, 64]]), [('tmp', ti)], []))
                dense_tm(s, 0, 256, t, ev, wstride=256)

            if u == 1:
                for tt in range(2):
                    DMA('pool', ar.ap(VT1 + (8 + tt) * 512, [[64, 8], [1, 64]]), bass.AP(D['cnv'], tt * 8192, [[64, 128], [16384, 8], [1, 64]]), fence, [('v', 8 + tt)])
                    for dpl in range(2):
                        DMA('pool', ar.ap(VD + (8 + tt) * 256 + dpl * 64, [[128, 2], [1, 64]]), bass.AP(D['csv'], tt * 8192, [[64, 128], [16384, 2], [1, 64]]), fence, [('vd', 8 + tt)])
            mark('attn1 u%d' % u)

            def kt_c(c, col, vtile_ap, vkeys, kkeys, ma=None, mb=None, mkeys=None):
                return dict(ka=ar.ap(KT + c * 1280 + col, [[1, 128]], 0, 64), kb=ar.ap(KT + c * 1280 + col, [[1, 128]], 64, 64), kkeys=kkeys,
                            v=vtile_ap, vkeys=vkeys, ma=ma, mb=mb, mkeys=mkeys or [])

            def kt_d(gk_, col, vt, kkeys, m=None, mkeys=None):
                return dict(ka=ar.ap(K2 + gk_ * 1280 + col, [[1, 128]], 0, 64), kb=ar.ap(K2 + gk_ * 1280 + col, [[1, 128]], 64, 64), kkeys=kkeys,
                            v=ar.ap(VD + vt * 256 + gk_ * 128, [[1, 128]]), vkeys=[('vd', vt)], ma=m, mb=m, mkeys=mkeys or [])

            if u == 0:
                for sq_ in range(4):
                    g = sq_ // 2
                    q0 = sq_ * 256
                    for c in range(4):
                        kts = [kt_c(c, q0 + kt * 128, ar.ap(VT1 + (sq_ * 2 + kt) * 512 + c * 128, [[1, 128]]), [('v', sq_ * 2 + kt)], [('k', c, g)]) for kt in range(2)]
                        attn_pair(kts, ar.ap(QT + c * 1024 + q0, [[1, 256]], 0, 64), ar.ap(QT + c * 1024 + q0, [[1, 256]], 64, 64), [('q', c, g)], 256,
                                  c * 1024 + q0, [('h', c, g)])
                    for c in range(4):
                        gk_ = c // 2
                        kts = [kt_d(gk_, q0 + kt * 128, sq_ * 2 + kt, [('k2', gk_, g)]) for kt in range(2)]
                        attn_pair(kts, ar.ap(Q2 + c * 1024 + q0, [[1, 256]], 0, 64), ar.ap(Q2 + c * 1024 + q0, [[1, 256]], 64, 64), [('q2', c, g)], 256,
                                  (4 + c) * 1024 + q0, [('h', 4 + c, g)], sink_h=2 * c)
            else:
                groups = [(0, 4, 0), (4, 1, 0)] + [(r, 1, r - 4) for r in range(5, 12)] + [(12, 4, 8)]
                for c in range(4):
                    for (r0, nr, rs) in groups:
                        nq = nr * 64
                        q0 = r0 * 64
                        g = q0 // 512
                        kts = []
                        for jj in range(4):
                            ka_ = rs + 2 * jj
                            col = ka_ * 64
                            if ka_ % 2 == 0:
                                vap = ar.ap(VT1 + (ka_ // 2) * 512 + c * 128, [[1, 128]]); vk = [('v', ka_ // 2)]
                            else:
                                vap = ar.ap(VT2 + ((ka_ - 1) // 2) * 512 + c * 128, [[1, 128]]); vk = [('v2', (ka_ - 1) // 2)]
                            i0 = 6 - (ka_ - r0)
                            assert 0 <= i0 and i0 + nr <= 14
                            ma = bank.ap((2 * c) * 896 + i0 * 64, [[1, nq]])
                            mb = bank.ap((2 * c + 1) * 896 + i0 * 64, [[1, nq]])
                            kk = [('k', c, (col // 512)), ('k', c, ((col + 127) // 512))]
                            kts.append(kt_c(c, col, vap, vk, kk, ma, mb, ['bank']))
                        for tt in range(2):
                            kts.append(kt_c(c, 1024 + tt * 128, ar.ap(VT1 + (8 + tt) * 512 + c * 128, [[1, 128]]), [('v', 8 + tt)], [('k', c, 'ctx')]))
                        attn_pair(kts, ar.ap(QT + c * 1024 + q0, [[1, nq]], 0, 64), ar.ap(QT + c * 1024 + q0, [[1, nq]], 64, 64), [('q', c, g)], nq,
                                  c * 1024 + q0, [('h', c, g)])
                mark('attn1D u%d' % u)
                for c in range(4):
                    gk_ = c // 2
                    for n in range(8):
                        g = n // 4
                        q0 = n * 128
                        kts = []
                        if n > 0:
                            kts.append(kt_d(gk_, (n - 1) * 128, n - 1, [('k2', gk_, (n - 1) // 4)], mprev.ap(0, [[1, 128]]), ['mprev']))
                        kts.append(kt_d(gk_, n * 128, n, [('k2', gk_, g)]))
                        if n < 7:
                            kts.append(kt_d(gk_, (n + 1) * 128, n + 1, [('k2', gk_, (n + 1) // 4)], mnext.ap(0, [[1, 128]]), ['mnext']))
                        for tt in range(2):
                            kts.append(kt_d(gk_, 1024 + tt * 128, 8 + tt, [('k2', gk_, 'ctx')]))
                        attn_pair(kts, ar.ap(Q2 + c * 1024 + q0, [[1, 128]], 0, 64), ar.ap(Q2 + c * 1024 + q0, [[1, 128]], 64, 64), [('q2', c, g)], 128,
                                  (4 + c) * 1024 + q0, [('h', 4 + c, g)], sink_h=2 * c)
            pipe.flush()
            wout_phase(u, 1)
            mlp_phase(u, 1)

        epsb = sb('epsb', 1, F32)
        P.op('dve', lambda e: e.memset(epsb.ap(0, [[1, 1]]), EPS), writes=['epsb'])
        cmap['eps'] = epsb.ap(0, [[1, 1]])
        setup()
        load_x(0)
        compute_mod(0)
        modq.extend(range(4, 12))
        done = False
        for u in range(2):
            if u == 1:
                load_x(u)
            if stop_after == ('X', u):
                store_x(u)
                break
            if stop_after == ('N', u):
                norm_mod(u, 0, 1)
                for kc in range(8):
                    for g in range(2):
                        CP('dve', xT.ap(kc * 1024 + g * 512, [[1, 512]]), hT.ap(kc * 1024 + g * 512, [[1, 512]]), [('h', kc, g)], [('x', kc, g)])
                store_x(u)
                break
            try:
                layer0(u)
            except StopBuild as ex:
                if str(ex) == 'h':
                    for kc in range(8):
                        for g in range(2):
                            CP('dve', xT.ap(kc * 1024 + g * 512, [[1, 512]]), hT.ap(kc * 1024 + g * 512, [[1, 512]]), [('h', kc, g)], [('x', kc, g)])
                store_x(u)
                break
            if stop_after == ('L0', u):
                store_x(u)
                done = True
                break
            layer1(u)
            store_x(u)
            if stop_after == ('L1', u):
                done = True
                break
        mark('end')
        if os.environ.get('DBG_MARKS'):
            import json
            json.dump(marks, open(os.environ['DBG_MARKS'], 'w'))
        P.emit(final_wait_ops=out_ops)
        print("ops:", len(P.ops))
    return nc


def host_consts():
    c = {}
    c['c_ident'] = np.eye(128, dtype=np.float32)
    Rm = np.zeros((64, 64), np.float32)
    for i in range(2):
        for f in range(16):
            a, b = i * 32 + f, i * 32 + 16 + f
            Rm[a, b] = -1.0
            Rm[b, a] = 1.0
    R2 = np.zeros((128, 128), np.float32)
    R2[:64, :64] = Rm.T
    R2[64:, 64:] = Rm.T
    c['c_R2'] = R2
    t = np.arange(1024)
    rows = (t // 64).astype(np.float32)
    cols = (t % 64).astype(np.float32)
    inv = (1.0 / (np.float32(10000.0) ** (np.arange(16, dtype=np.float32) / np.float32(16)))).astype(np.float32)
    cos = np.zeros((128, 1024), np.float32)
    sin = np.zeros((128, 1024), np.float32)
    for p in range(128):
        d = p % 64
        i, f = d // 32, d % 16
        pos = rows if i == 0 else cols
        ang = (pos * inv[f]).astype(np.float32)
        cos[p] = np.cos(ang)
        sin[p] = np.sin(ang)
    c['c_cos'] = cos
    c['c_sin'] = sin
    j = np.arange(128)[:, None]
    i = np.arange(128)[None, :]
    c['c_mprev'] = ((j >= i).astype(np.float32) - 1.0) * 30000.0
    c['c_mnext'] = ((j <= i).astype(np.float32) - 1.0) * 30000.0
    qc = np.arange(64)
    cs = np.clip(qc - 8, 0, 48)
    kc = np.arange(64)
    ok = (kc[:, None] >= cs[None, :]) & (kc[:, None] < cs[None, :] + 16)
    c['c_colok'] = np.concatenate([ok, ok], axis=0).astype(np.float32)
    c['c_colneg'] = (c['c_colok'] - 1.0) * 30000.0
    return c


def rpb_layout(rpb):
    kc = np.arange(64)[:, None]
    qc = np.arange(64)[None, :]
    dc = np.clip(kc - qc + 15, 0, 30)
    out = np.zeros((128, 8, 14, 64), np.float32)
    for i in range(14):
        out[:64, :, i, :] = np.transpose(rpb[:, (6 - i) + 7][:, dc], (1, 0, 2))
        out[64:, :, i, :] = np.transpose(rpb[:, (7 - i) + 7][:, dc], (1, 0, 2))
    return np.ascontiguousarray(out.reshape(128, 8 * 14 * 64))


_NC_CACHE = {}


def kernel(x_prompt, x_sample, cache_diff_k, cache_diff_v, cache_na_k, cache_na_v, cache_swa_k, cache_swa_v, c, c_ctx, mod_w, mod_b,
           norm_mix_pre, norm_mix_post, norm_mlp_pre, norm_mlp_post, w_in_even, conv_w, lambda_q1, lambda_k1, lambda_q2, lambda_k2,
           subln, w_in_odd, rpb, sink, w_out, mlp_w1, mlp_w2, _stop_after=None):
    f = lambda a: np.ascontiguousarray(np.asarray(a, dtype=np.float32))
    if _stop_after not in _NC_CACHE:
        _NC_CACHE[_stop_after] = build(_stop_after)
    nc = _NC_CACHE[_stop_after]
    consts = host_consts()
    shared = dict(
        mod_w=f(mod_w).reshape(2048, 6144), mod_b=f(mod_b),
        gains=np.stack([f(norm_mix_pre).reshape(-1), f(norm_mix_post).reshape(-1), f(norm_mlp_pre).reshape(-1), f(norm_mlp_post).reshape(-1)]),
        w_in_even=f(w_in_even)[0], conv_w=f(conv_w)[0],
        lams=np.stack([f(lambda_q1)[0], f(lambda_k1)[0], f(lambda_q2)[0], f(lambda_k2)[0]]),
        subln=f(subln)[0].reshape(128, 1), w_in_odd=f(w_in_odd)[0], rpbT=rpb_layout(f(rpb)[0]), sink=f(sink)[0].reshape(1, 8),
        w_out=f(w_out).reshape(2048, 1024), mlp_w1=f(mlp_w1).reshape(2048, 4096), mlp_w2=f(mlp_w2).reshape(8192, 1024),
    )
    shared.update(consts)
    xp = f(x_prompt); xs = f(x_sample)
    in_maps = []
    for k in range(NCORES):
        m = dict(shared)
        m['xP'] = xp[4 * k:4 * k + 4].reshape(1024, 1024)
        m['xS'] = xs[k]
        m['cdk'] = f(cache_diff_k)[k, 0].reshape(8, 256, 64)
        m['cdv'] = f(cache_diff_v)[k, 0]
        m['cnk'] = f(cache_na_k)[k, 0]
        m['cnv'] = f(cache_na_v)[k, 0]
        m['csk'] = f(cache_swa_k)[k, 0]
        m['csv'] = f(cache_swa_v)[k, 0]
        m['cvec'] = np.stack([f(c_ctx), f(c)[k]])
        in_maps.append(m)
    in_maps = in_maps[:DBG_CORES]
    res = run_bass_kernel_spmd(nc, in_maps, core_ids=list(range(DBG_CORES)))
    R = list(res.results) + [res.results[0]] * (NCORES - DBG_CORES)
    y_prompt = np.concatenate([r['yP'].reshape(4, 256, 1024) for r in R], axis=0)
    y_sample = np.stack([r['yS'] for r in R], axis=0)
    ndk = np.concatenate([r['ndk'].reshape(4, 1, 4, 2, 256, 64) for r in R], axis=0)
    ndv = np.concatenate([r['ndv'].reshape(4, 1, 4, 256, 128) for r in R], axis=0)
    nnk = np.concatenate([r['nnk'].reshape(4, 1, 8, 256, 64) for r in R], axis=0)
    nnv = np.concatenate([r['nnv'].reshape(4, 1, 8, 256, 64) for r in R], axis=0)
    nsk = np.concatenate([r['nsk'].reshape(4, 1, 2, 256, 64) for r in R], axis=0)
    nsv = np.concatenate([r['nsv'].reshape(4, 1, 2, 256, 64) for r in R], axis=0)
    return (y_prompt, y_sample, ndk, ndv, nnk, nnv, nsk, nsv)
```

```python
import math
import os
SKIP = set(os.environ.get('DBG_SKIP', '').split(','))
DBG_CORES = int(os.environ.get('DBG_CORES', '8'))
from contextlib import ExitStack
import numpy as np
import concourse.bass as bass
import concourse.mybir as mybir
from concourse.bass_utils import run_bass_kernel_spmd

F32 = mybir.dt.float32
BF16 = mybir.dt.bfloat16
AF = mybir.ActivationFunctionType
ALU = mybir.AluOpType
AX = mybir.AxisListType
EPS = 1e-6
NCORES = 8


class Prog:
    def __init__(self, nc):
        self.nc = nc
        self.ops = []
        self.last_w = {}
        self.readers = {}
        self.rot = {}

    def nxt(self, name, n, base=0):
        v = self.rot.get(name, 0)
        self.rot[name] = v + 1
        return base + (v % n)

    def op(self, eng, fn, reads=(), writes=(), dma=False, semkey=None):
        idx = len(self.ops)
        raw = set()
        oth = set()
        for k in reads:
            w = self.last_w.get(k)
            if w is not None:
                raw.add(w)
            if isinstance(k, tuple) and k[0] == 'ps':
                for r in self.readers.get(k, ()):
                    oth.add(r)
        for k in writes:
            w = self.last_w.get(k)
            if w is not None:
                oth.add(w)
            for r in self.readers.get(k, ()):
                oth.add(r)
        for k in reads:
            self.readers.setdefault(k, []).append(idx)
        for k in writes:
            self.last_w[k] = idx
            self.readers[k] = []
        raw.discard(idx)
        oth.discard(idx)
        self.ops.append(dict(eng=eng, fn=fn, raw=raw, oth=oth - raw, dma=dma, semkey=semkey, sig=False))
        return idx

    def emit(self, final_wait_ops=()):
        nc = self.nc
        ops = self.ops
        for i, o in enumerate(ops):
            need = set(o['raw'])
            for d in o['oth']:
                p = ops[d]
                if p['eng'] == o['eng'] and not p['dma'] and not o['dma']:
                    continue
                need.add(d)
            o['need'] = need
            for d in need:
                ops[d]['sig'] = True
        for d in final_wait_ops:
            ops[d]['sig'] = True
        for o in ops:
            if o['dma']:
                o['sig'] = True
        stack = ExitStack()
        sems = {}
        cnt = {}
        rot = {}
        ROT = 12
        for i, o in enumerate(ops):
            if not o['sig']:
                continue
            if o['dma']:
                if o['semkey'] is not None:
                    name = 'dk_%s' % (o['semkey'],)
                else:
                    r = rot.get(o['eng'], 0)
                    rot[o['eng']] = r + 1
                    name = 'dr_%s_%d' % (o['eng'], r % ROT)
                cnt[name] = cnt.get(name, 0) + 16
            else:
                name = 'c_%s' % o['eng']
                cnt[name] = cnt.get(name, 0) + 1
            o['sem'] = name
            o['val'] = cnt[name]
            if name not in sems:
                sems[name] = stack.enter_context(nc.semaphore(name.replace(' ', '').replace("'", '').replace(',', '_').replace('(', '').replace(')', '')))
        per_eng = {e: [] for e in ['pe', 'act', 'dve', 'pool', 'sp']}
        for i, o in enumerate(ops):
            per_eng[o['eng']].append(i)
        final_wait_ops = list(final_wait_ops)

        def emit_eng(engname, engobj):
            waited = {}
            for i in per_eng[engname]:
                o = ops[i]
                w = {}
                for d in o['need']:
                    p = ops[d]
                    nm, v = p['sem'], p['val']
                    if waited.get(nm, 0) >= v:
                        continue
                    if w.get(nm, 0) < v:
                        w[nm] = v
                for nm, v in w.items():
                    engobj.wait_ge(sems[nm], v)
                    waited[nm] = v
                if o['dma'] and o['sig']:
                    pv = o['val'] - 16
                    if pv > 0 and waited.get(o['sem'], 0) < pv:
                        engobj.wait_ge(sems[o['sem']], pv)
                        waited[o['sem']] = pv
                ins = o['fn'](engobj)
                if o['sig']:
                    ins.then_inc(sems[o['sem']], 16 if o['dma'] else 1)
            if engname == 'sp':
                w = {}
                for d in final_wait_ops:
                    p = ops[d]
                    nm, v = p['sem'], p['val']
                    if w.get(nm, 0) < v:
                        w[nm] = v
                for nm, v in w.items():
                    if waited.get(nm, 0) < v:
                        engobj.wait_ge(sems[nm], v)

        with stack:
            with nc.Block() as block:
                @block.tensor
                def _(e):
                    emit_eng('pe', e)

                @block.scalar
                def _(e):
                    emit_eng('act', e)

                @block.vector
                def _(e):
                    emit_eng('dve', e)

                @block.gpsimd
                def _(e):
                    emit_eng('pool', e)

                @block.sync
                def _(e):
                    emit_eng('sp', e)


class StopBuild(Exception):
    pass


class TT:
    def __init__(self, h, free):
        self.h = h
        self.F = free

    def ap(self, off, dims, p0=0, pn=128):
        return bass.AP(self.h, p0 * self.F + off, [[self.F, pn]] + [list(d) for d in dims])


def build(stop_after=None):
    nc = bass.Bass("TRN2", target_bir_lowering=False)
    D = {}

    def din(name, shape):
        D[name] = nc.dram_tensor(name, list(shape), F32, kind="ExternalInput")

    def dout(name, shape):
        D[name] = nc.dram_tensor(name, list(shape), F32, kind="ExternalOutput")

    din('xP', [1024, 1024]); din('xS', [1024, 1024])
    din('cdk', [8, 256, 64]); din('cdv', [4, 256, 128])
    din('cnk', [8, 256, 64]); din('cnv', [8, 256, 64])
    din('csk', [2, 256, 64]); din('csv', [2, 256, 64])
    din('cvec', [2, 1024])
    din('mod_w', [2048, 6144]); din('mod_b', [2, 6144])
    din('gains', [4, 2048])
    din('w_in_even', [1024, 3072]); din('conv_w', [3, 512]); din('lams', [4, 64]); din('subln', [128, 1])
    din('w_in_odd', [1024, 2304]); din('rpbT', [128, 8 * 14 * 64]); din('sink', [1, 8])
    din('w_out', [2048, 1024]); din('mlp_w1', [2048, 4096]); din('mlp_w2', [8192, 1024])
    din('c_ident', [128, 128]); din('c_R2', [128, 128]); din('c_cos', [128, 1024]); din('c_sin', [128, 1024])
    din('c_mprev', [128, 128]); din('c_mnext', [128, 128]); din('c_colok', [128, 64]); din('c_colneg', [128, 64])
    dout('yP', [1024, 1024]); dout('yS', [1024, 1024])
    dout('ndk', [4, 8, 256, 64]); dout('ndv', [4, 4, 256, 128])
    dout('nnk', [4, 8, 256, 64]); dout('nnv', [4, 8, 256, 64])
    dout('nsk', [4, 2, 256, 64]); dout('nsv', [4, 2, 256, 64])

    es = ExitStack()

    def sb(name, free, dt):
        return TT(es.enter_context(nc.sbuf_tensor('s_' + name, [128, free], dt)), free)

    with es:
        xT = sb('xT', 8192, F32)
        hT = sb('hT', 8192, BF16)
        yb = sb('yb', 8192, BF16)
        wr = sb('wr', 2 * 8192, BF16)
        ar = sb('ar', 32768, BF16)
        bank = sb('bank', 7168, BF16)
        tmp = sb('tmp', 6 * 512, F32)
        Et = sb('Et', 4 * 512, BF16)
        sqt = sb('sqt', 4 * 512, BF16)
        rstd = sb('rstd', 2 * 512, F32)
        ident = sb('ident', 128, F32)
        ones = sb('ones', 128, BF16)
        R2 = sb('R2', 128, BF16)
        cosT = sb('cosT', 1024, BF16)
        sinT = sb('sinT', 1024, BF16)
        mprev = sb('mprev', 128, BF16)
        mnext = sb('mnext', 128, BF16)
        colok = sb('colok', 64, F32)
        colneg = sb('colneg', 64, F32)
        identb = sb('identb', 128, BF16)
        cT = sb('cT', 16, F32)
        sT = sb('sT', 16, BF16)
        modb = sb('modb', 96, F32)
        modv = sb('modv', 192, F32)
        gains = sb('gains', 64, F32)
        coef = sb('coef', 128, F32)
        cw = sb('cw', 12, F32)
        esink = sb('esink', 8, F32)
        lamt = sb('lamt', 256, F32)
        lams = sb('lams', 8, F32)
        subln = sb('subln', 2, F32)
        ps_all = TT(es.enter_context(nc.psum_tensor('ps_all', [128, 4096], F32)), 4096)

        P = Prog(nc)

        def MM(out, lhsT, rhs, start, stop, reads, writes, skip=False):
            if skip:
                P.op('pe', lambda e: e.matmul(out, lhsT, rhs, start=start, stop=stop, skip_group_check=True), reads=reads, writes=writes)
            else:
                P.op('pe', lambda e: e.matmul(out, lhsT, rhs, start=start, stop=stop), reads=reads, writes=writes)

        def TR(out, in_, reads, writes):
            P.op('pe', lambda e: e.transpose(out, in_, ident.ap(0, [[1, 128]])), reads=list(reads) + ['ident'], writes=writes)

        def ACTV(out, in_, func, reads, writes, bias=None, scale=None):
            kw = {}
            if bias is not None:
                kw['bias'] = bias
            if scale is not None:
                kw['scale'] = scale
            P.op('act', lambda e: e.activation(out, in_, func, **kw), reads=reads, writes=writes)

        def TTO(eng, out, in0, in1, op, reads, writes):
            P.op(eng, lambda e: e.tensor_tensor(out=out, in0=in0, in1=in1, op=op), reads=reads, writes=writes)

        def STT(eng, out, in0, scalar, in1, op0, op1, reads, writes):
            P.op(eng, lambda e: e.scalar_tensor_tensor(out=out, in0=in0, scalar=scalar, in1=in1, op0=op0, op1=op1), reads=reads, writes=writes)

        def TS(eng, out, in0, s1, s2, op0, op1, reads, writes):
            if s2 is None:
                P.op(eng, lambda e: e.tensor_scalar(out=out, in0=in0, scalar1=s1, scalar2=None, op0=op0), reads=reads, writes=writes)
            else:
                P.op(eng, lambda e: e.tensor_scalar(out=out, in0=in0, scalar1=s1, scalar2=s2, op0=op0, op1=op1), reads=reads, writes=writes)

        def CP(eng, out, in_, reads, writes):
            if eng == 'act':
                P.op('act', lambda e: e.activation(out, in_, AF.Copy), reads=reads, writes=writes)
            else:
                P.op(eng, lambda e: e.tensor_copy(out, in_), reads=reads, writes=writes)

        def RECIP(out, in_, reads, writes):
            P.op('dve', lambda e: e.reciprocal(out, in_), reads=reads, writes=writes)

        def DMA(q, out, in_, reads, writes, semkey=None, slow=False):
            if slow:
                return P.op(q, lambda e: e.dma_start(out=out, in_=in_, allow_slow_non_contiguous=True), reads=reads, writes=writes, dma=True, semkey=semkey)
            return P.op(q, lambda e: e.dma_start(out=out, in_=in_), reads=reads, writes=writes, dma=True, semkey=semkey)

        def psap(b, off, dims, p0=0, pn=128):
            return ps_all.ap(b * 512 + off, dims, p0, pn)

        def bankA():
            return P.nxt('bA', 4, 0)

        def bankB():
            return P.nxt('bB', 4, 4)

        def bankSCpair():
            return P.nxt('bSCp', 2) * 2

        def bankACC():
            return P.nxt('bACC', 4, 4)

        def tmpi():
            return P.nxt('tmp', 6)

        def Ei():
            return P.nxt('E', 4)

        def sqi():
            return P.nxt('sq', 4)

        def evq():
            return ['act', 'dve'][P.nxt('evq', 2)]

        out_ops = []
        cmap = {}
        marks = []

        def mark(name):
            marks.append((name, sum(1 for o in P.ops if o['eng'] == 'pe')))

        def setup():
            DMA('sp', ident.ap(0, [[1, 128]]), D['c_ident'].ap(), [], ['ident'])
            P.op('dve', lambda e: e.memset(ones.ap(0, [[1, 128]]), 1.0), writes=['ones'])
            DMA('pool', R2.ap(0, [[1, 128]]), D['c_R2'].ap(), [], ['R2'])
            DMA('pool', cosT.ap(0, [[1, 1024]]), D['c_cos'].ap(), [], ['cos'])
            DMA('pool', sinT.ap(0, [[1, 1024]]), D['c_sin'].ap(), [], ['sin'])
            DMA('pool', mprev.ap(0, [[1, 128]]), D['c_mprev'].ap(), [], ['mprev'])
            DMA('pool', mnext.ap(0, [[1, 128]]), D['c_mnext'].ap(), [], ['mnext'])
            DMA('sp', colok.ap(0, [[1, 64]]), D['c_colok'].ap(), [], ['colok'])
            DMA('sp', colneg.ap(0, [[1, 64]]), D['c_colneg'].ap(), [], ['colneg'])
            CP('dve', identb.ap(0, [[1, 128]]), ident.ap(0, [[1, 128]]), ['ident'], ['identb'])
            for j in range(2):
                DMA('sp', cT.ap(j, [[2, 8]]), bass.AP(D['cvec'], j * 1024, [[1, 128], [128, 8]]), [], ['cT'], slow=True)
                DMA('sp', modb.ap(j * 48, [[1, 48]]), bass.AP(D['mod_b'], j * 6144, [[1, 128], [128, 48]]), [], ['modb'], slow=True)
            for w in range(4):
                DMA('sp', gains.ap(w * 16, [[1, 16]]), bass.AP(D['gains'], w * 2048, [[1, 128], [128, 16]]), [], ['gains'], slow=True)
            for j in range(3):
                DMA('sp', cw.ap(j, [[3, 4]]), bass.AP(D['conv_w'], j * 512, [[1, 128], [128, 4]]), [], ['cw'], slow=True)
            DMA('sp', subln.ap(0, [[1, 1]]), D['subln'].ap(), [], ['subln0'])
            DMA('sp', esink.ap(0, [[1, 8]]), bass.AP(D['sink'], 0, [[0, 128], [1, 8]]), [], ['esink0'])
            DMA('sp', lamt.ap(0, [[1, 256]]), bass.AP(D['lams'], 0, [[0, 128], [1, 256]]), [], ['lamt'])
            ACTV(esink.ap(0, [[1, 8]]), esink.ap(0, [[1, 8]]), AF.Exp, ['esink0'], ['esink'])
            ACTV(sT.ap(0, [[1, 16]]), cT.ap(0, [[1, 16]]), AF.Silu, ['cT'], ['sT'])
            TTO('dve', lamt.ap(0, [[1, 64]]), lamt.ap(0, [[1, 64]]), lamt.ap(64, [[1, 64]]), ALU.mult, ['lamt'], ['lamt1'])
            TTO('dve', lamt.ap(128, [[1, 64]]), lamt.ap(128, [[1, 64]]), lamt.ap(192, [[1, 64]]), ALU.mult, ['lamt'], ['lamt2'])
            P.op('dve', lambda e: e.reduce_sum(lams.ap(0, [[1, 1]]), lamt.ap(0, [[1, 64]]), axis=AX.X), reads=['lamt1'], writes=['lams0'])
            P.op('dve', lambda e: e.reduce_sum(lams.ap(1, [[1, 1]]), lamt.ap(128, [[1, 64]]), axis=AX.X), reads=['lamt2'], writes=['lams1'])
            ACTV(lams.ap(2, [[1, 2]]), lams.ap(0, [[1, 2]]), AF.Exp, ['lams0', 'lams1'], ['lams2'])
            TTO('dve', lams.ap(4, [[1, 1]]), lams.ap(3, [[1, 1]]), lams.ap(2, [[1, 1]]), ALU.subtract, ['lams2'], ['lams4'])
            TS('dve', lams.ap(4, [[1, 1]]), lams.ap(4, [[1, 1]]), -0.2, None, ALU.add, None, ['lams4'], ['neglam'])
            TS('dve', subln.ap(1, [[1, 1]]), subln.ap(0, [[1, 1]]), 0.8, None, ALU.mult, None, ['subln0'], ['subln'])
            for hp in range(4):
                DMA('sp', tmp.ap(0, [[1, 1792]]), D['rpbT'].ap()[:, hp * 1792:(hp + 1) * 1792], [], [('tmp', i) for i in range(4)])
                ACTV(tmp.ap(0, [[1, 1792]]), tmp.ap(0, [[1, 1792]]), AF.Copy, [('tmp', i) for i in range(4)], [('tmp', i) for i in range(4)], scale=8.0)
                TTO('dve', tmp.ap(0, [[64, 28], [1, 64]]), tmp.ap(0, [[64, 28], [1, 64]]), colok.ap(0, [[0, 28], [1, 64]]), ALU.mult,
                    [('tmp', i) for i in range(4)] + ['colok'], [('tmp', i) for i in range(4)])
                TTO('dve', bank.ap(hp * 1792, [[64, 28], [1, 64]]), tmp.ap(0, [[64, 28], [1, 64]]), colneg.ap(0, [[0, 28], [1, 64]]), ALU.add,
                    [('tmp', i) for i in range(4)] + ['colneg'], ['bank'])

        class WV:
            def __init__(self, slot, off, stride):
                self.slot = slot
                self.base = slot * 8192 + off
                self.stride = stride
                self.key = ('w', slot)

        def wload(src_ap, dims, extra=None, stride=512):
            sl = P.nxt('wslot', 2)
            DMA('pool', wr.ap(sl * 8192, dims), src_ap, [], [('w', sl)], semkey='w%d' % sl)
            if extra:
                for (off, dims2, src2) in extra:
                    DMA('pool', wr.ap(sl * 8192 + off, dims2), src2, [], [('w', sl)], semkey='w%d' % sl)
            return WV(sl, 0, stride)

        wcache = {}

        def wpiece(name, row0, col0):
            c1 = (col0 // 1024) * 1024
            key = (name, row0, c1)
            if wcache.get(name, (None, None))[0] != key:
                src = D[name].ap()[row0:row0 + 1024, c1:c1 + 1024].rearrange("(kc p) c -> p kc c", p=128)
                wcache[name] = (key, wload(src, [[1024, 8], [1, 1024]], stride=1024))
            wv = wcache[name][1]
            return WV(wv.slot, col0 - c1, 1024)

        def mod_piece(li, pc):
            b = bankA()
            s = wpiece('mod_w', li * 1024, pc * 512)
            for mc in range(4):
                for kc in range(8):
                    MM(psap(b, mc * 2, [[1, 2]]), wr.ap(s.base + kc * s.stride + mc * 128, [[1, 128]]), sT.ap(kc * 2, [[1, 2]]),
                       kc == 0, kc == 7, [s.key, 'sT'], [('ps', b)])
            TTO('dve', modv.ap(li * 96 + pc * 8, [[2, 4], [1, 2]]), psap(b, 0, [[2, 4], [1, 2]]), modb.ap(li * 48 + pc * 4, [[1, 4], [0, 2]]), ALU.add,
                [('ps', b), 'modb'], [('modv', li)])
            if pc == 3:
                mod_finish(li, 0)
            if pc == 11:
                mod_finish(li, 1)

        def compute_mod(li):
            mark('mod l%d' % li)
            for pc in range(4):
                mod_piece(li, pc)

        modq = []

        def mod_more(n=1):
            for _ in range(n):
                if modq:
                    mod_piece(0, modq.pop(0))

        def mod_finish(li, part):
            def mv(c0):
                return modv.ap(li * 96 + c0 * 2, [[2, 8], [1, 2]])

            def gn(w):
                return gains.ap(w * 16 + li * 8, [[1, 8], [0, 2]])

            def cf(w):
                return coef.ap(li * 64 + w * 16, [[2, 8], [1, 2]])
            if part == 0:
                STT('dve', cf(0), mv(8), 1.0, gn(0), ALU.add, ALU.mult, [('modv', li), 'gains'], [('coef', li)])
                return
            TTO('dve', cf(1), mv(16), gn(1), ALU.mult, [('modv', li), 'gains'], [('coef', li)])
            STT('dve', cf(2), mv(32), 1.0, gn(2), ALU.add, ALU.mult, [('modv', li), 'gains'], [('coef', li)])
            TTO('dve', cf(3), mv(40), gn(3), ALU.mult, [('modv', li), 'gains'], [('coef', li)])

        def coefap(li, w, kc, j):
            return coef.ap(li * 64 + w * 16 + kc * 2 + j, [[1, 1]])

        def shap(li, which, kc, j):
            c0 = 0 if which == 1 else 24
            return modv.ap(li * 96 + (c0 + kc) * 2 + j, [[1, 1]])

        def load_x(u):
            mark('load_x u%d' % u)
            xd = D['xP'] if u == 0 else D['xS']
            for t in range(8):
                g = t // 4
                for hf in range(2):
                    ti = tmpi()
                    DMA('sp', tmp.ap(ti * 512, [[1, 512]]), xd.ap()[t * 128:(t + 1) * 128, hf * 512:(hf + 1) * 512], [], [('tmp', ti)])
                    b = bankA()
                    for c in range(4):
                        TR(psap(b, c * 128, [[1, 128]]), tmp.ap(ti * 512 + c * 128, [[1, 128]]), [('tmp', ti)], [('ps', b)])
                    CP(['dve', 'act'][g], xT.ap((hf * 4) * 1024 + t * 128, [[1024, 4], [1, 128]]), psap(b, 0, [[128, 4], [1, 128]]), [('ps', b)],
                       [('x', hf * 4 + c, g) for c in range(4)])

        def store_x(u):
            mark('store_x u%d' % u)
            yd = D['yP'] if u == 0 else D['yS']
            for t in range(8):
                if t == 4:
                    run_tail()
                g = t // 4
                for hf in range(2):
                    b = bankA()
                    for c in range(4):
                        TR(psap(b, c * 128, [[1, 128]]), xT.ap((hf * 4 + c) * 1024 + t * 128, [[1, 128]]), [('x', hf * 4 + c, g)], [('ps', b)])
                    ti = tmpi()
                    CP(evq(), tmp.ap(ti * 512, [[1, 512]]), psap(b, 0, [[1, 512]]), [('ps', b)], [('tmp', ti)])
                    out_ops.append(DMA('sp', yd.ap()[t * 128:(t + 1) * 128, hf * 512:(hf + 1) * 512], tmp.ap(ti * 512, [[1, 512]]), [('tmp', ti)], []))

        def rstd_from(b, ri, scale):
            ACTV(rstd.ap(ri * 512, [[1, 512]]), psap(b, 0, [[1, 512]]), AF.Ln, [('ps', b), 'epsb'], [('rstd', ri)], bias=cmap['eps'], scale=scale)
            ACTV(rstd.ap(ri * 512, [[1, 512]]), rstd.ap(ri * 512, [[1, 512]]), AF.Exp, [('rstd', ri)], [('rstd', ri)], scale=-0.5)

        tailq = []

        def run_tail():
            while tailq:
                tailq.pop(0)()

        def norm_mod(u, li, which):
            mark('norm u%d l%d w%d' % (u, li, which))
            j = u
            w = 0 if which == 1 else 2
            for g in range(2):
                b = bankB()
                for kc in range(8):
                    si = sqi()
                    ACTV(sqt.ap(si * 512, [[1, 512]]), xT.ap(kc * 1024 + g * 512, [[1, 512]]), AF.Square, [('x', kc, g)], [('sq', si)])
                    MM(psap(b, 0, [[1, 512]]), ones.ap(0, [[1, 128]]), sqt.ap(si * 512, [[1, 512]]), kc == 0, kc == 7, [('sq', si), 'ones'], [('ps', b)])
                ri = P.nxt('rstd', 2)
                rstd_from(b, ri, 1.0 / 1024)
                for kc in range(8):
                    ti = tmpi()
                    STT('dve', tmp.ap(ti * 512, [[1, 512]]), xT.ap(kc * 1024 + g * 512, [[1, 512]]), coefap(li, w, kc, j), rstd.ap(ri * 512, [[1, 512]]),
                        ALU.mult, ALU.mult, [('x', kc, g), ('coef', li), ('rstd', ri)], [('tmp', ti)])
                    ACTV(hT.ap(kc * 1024 + g * 512, [[1, 512]]), tmp.ap(ti * 512, [[1, 512]]), AF.Identity, [('tmp', ti), ('modv', li)], [('h', kc, g)],
                         bias=shap(li, which, kc, j), scale=1.0)
                if g == 0:
                    run_tail()

        def dense_fm(s, mc, g, evac, kcn=8, wstride=512, src=None, srckey='h'):
            b = bankA()
            for kc in range(kcn):
                MM(psap(b, 0, [[1, 512]]), wr.ap(s.base + kc * s.stride + mc * 128, [[1, 128]]),
                   (src or hT).ap(kc * 1024 + g * 512, [[1, 512]]), kc == 0, kc == kcn - 1, [s.key, (srckey, kc, g)], [('ps', b)])
            evac(b)

        def dense_tm(s, col0, ncols, t, evac, wstride=512):
            b = bankA()
            g = t // 4
            for kc in range(8):
                MM(psap(b, 0, [[1, ncols]]), hT.ap(kc * 1024 + t * 128, [[1, 128]]), wr.ap(s.base + kc * s.stride + col0, [[1, ncols]]),
                   kc == 0, kc == 7, [s.key, ('h', kc, g)], [('ps', b)])
            evac(b)

        QT, KT = 0, 4096
        VT0, AB, UU = 9216, 14336, 18432
        Q2, K2, VT1, VT2, VD = 9216, 13312, 15872, 20992, 24576

        def rope_evac(b, g, dst_off, dst_keys):
            if 'rope' in SKIP:
                CP(evq(), ar.ap(dst_off, [[1, 512]]), psap(b, 0, [[1, 512]]), [('ps', b)], dst_keys)
                return
            ei = Ei()
            CP('act', Et.ap(ei * 512, [[1, 512]]), psap(b, 0, [[1, 512]]), [('ps', b)], [('E', ei)])
            b2 = bankA()
            MM(psap(b2, 0, [[1, 512]]), R2.ap(0, [[1, 128]]), Et.ap(ei * 512, [[1, 512]]), True, True, [('E', ei), 'R2'], [('ps', b2)])
            t1 = tmpi()
            TTO('dve', tmp.ap(t1 * 512, [[1, 512]]), psap(b, 0, [[1, 512]]), cosT.ap(g * 512, [[1, 512]]), ALU.mult, [('ps', b), 'cos'], [('tmp', t1)])
            t2 = tmpi()
            TTO('dve', tmp.ap(t2 * 512, [[1, 512]]), psap(b2, 0, [[1, 512]]), sinT.ap(g * 512, [[1, 512]]), ALU.mult, [('ps', b2), 'sin'], [('tmp', t2)])
            TTO('dve', ar.ap(dst_off, [[1, 512]]), tmp.ap(t1 * 512, [[1, 512]]), tmp.ap(t2 * 512, [[1, 512]]), ALU.add, [('tmp', t1), ('tmp', t2)], dst_keys)

        def load_ctx_k(name, nh, dst_off, dst_stride, nchunk, keyname, dup=False):
            for tt in range(2):
                ti = tmpi()
                if not dup:
                    DMA('sp', tmp.ap(ti * 512, [[64, nh], [1, 64]]), bass.AP(D[name], tt * 8192, [[64, 128], [16384, nh], [1, 64]]), [], [('tmp', ti)])
                else:
                    for dpl in range(2):
                        DMA('sp', tmp.ap(ti * 512 + dpl * 64, [[128, nh], [1, 64]]), bass.AP(D[name], tt * 8192, [[64, 128], [16384, nh], [1, 64]]), [], [('tmp', ti)])
                b = bankA()
                for c in range(nchunk):
                    TR(psap(b, c * 128, [[1, 128]]), tmp.ap(ti * 512 + c * 128, [[1, 128]]), [('tmp', ti)], [('ps', b)])
                CP('dve', ar.ap(dst_off + 1024 + tt * 128, [[dst_stride, nchunk], [1, 128]]), psap(b, 0, [[128, nchunk], [1, 128]]), [('ps', b)],
                   [(keyname, c, 'ctx') for c in range(nchunk)])

        class Pipe:
            DEPTH = 2

            def __init__(self):
                self.round = []
                self.pending = []
                self.late_prev = []
                self.late_new = []

            def add(self, item):
                if self.round and (self.round[0].get('sb') is None) != (item.get('sb') is None):
                    self.emit_round()
                self.round.append(item)
                if len(self.round) == 2:
                    self.emit_round()

            def emit_round(self):
                items = self.round
                self.round = []
                if not items:
                    return
                er = P.nxt('Er', 6)
                ebase = er * 1024
                ekey = [('yb', er // 4, (er % 4) * 2), ('yb', er // 4, (er % 4) * 2 + 1)]
                paired = items[0].get('sb') is not None
                bx = bankSCpair()
                by = bx + 1
                if paired:
                    offs = []
                    o = 0
                    for it in items:
                        offs.append(o)
                        o += it['nq']
                    tot = o
                    for k, (it, of) in enumerate(zip(items, offs)):
                        it['sa'](bx, of, k == 0)
                    for k, (it, of) in enumerate(zip(items, offs)):
                        it['sb'](by, of, k == 0)
                    for it, of in zip(items, offs):
                        if it.get('bias'):
                            it['bias'](bx, by, of)
                    ACTV(yb.ap(ebase, [[512, 2], [1, tot]]), psap(bx, 0, [[512, 2], [1, tot]]), AF.Exp, [('ps', bx), ('ps', by)], ekey, scale=0.125)
                    for it, of in zip(items, offs):
                        it['ea'] = ebase + of
                        it['eb'] = ebase + 512 + of
                else:
                    banks = [bx, by]
                    for k, it in enumerate(items):
                        it['sa'](banks[k], 0, True)
                    if len(items) == 2 and items[0]['nq'] == 512 and items[1]['nq'] == 512:
                        ACTV(yb.ap(ebase, [[1, 1024]]), psap(bx, 0, [[1, 1024]]), AF.Exp, [('ps', bx), ('ps', by)], ekey, scale=0.125)
                    else:
                        for k, it in enumerate(items):
                            ACTV(yb.ap(ebase + k * 512, [[1, it['nq']]]), psap(banks[k], 0, [[1, it['nq']]]), AF.Exp, [('ps', banks[k])], ekey, scale=0.125)
                    for k, it in enumerate(items):
                        it['ea'] = ebase + k * 512
                        it['eb'] = None

                def pend(items=items, ekey=ekey):
                    for it in items:
                        it['post'](it['ea'], it['eb'], ekey)
                self.pending.append(pend)
                while len(self.pending) > self.DEPTH:
                    self.pending.pop(0)()
                for f in self.late_prev:
                    f()
                self.late_prev = self.late_new
                self.late_new = []

            def flush(self):
                self.emit_round()
                while self.pending:
                    self.pending.pop(0)()
                    for f in self.late_prev:
                        f()
                    self.late_prev = self.late_new
                    self.late_new = []
                for f in self.late_prev + self.late_new:
                    f()
                self.late_prev = []
                self.late_new = []

        pipe = Pipe()

        def attn_pair(ktiles, qa, qb, qkeys, nq, dst_off, dst_keys, sink_h=None):
            bo = bankACC()
            bd = bankACC()
            nk = len(ktiles)

            def final():
                ti = tmpi()
                if sink_h is not None:
                    ACTV(tmp.ap(ti * 512, [[1, nq]]), psap(bd, 0, [[1, nq]]), AF.Ln, [('ps', bd), 'esink'], [('tmp', ti)], bias=esink.ap(sink_h, [[1, 1]]), scale=1.0)
                    ACTV(tmp.ap(ti * 512 + nq, [[1, nq]]), psap(bd, nq, [[1, nq]]), AF.Ln, [('ps', bd), 'esink'], [('tmp', ti)], bias=esink.ap(sink_h + 1, [[1, 1]]), scale=1.0)
                else:
                    ACTV(tmp.ap(ti * 512, [[1, 2 * nq]]), psap(bd, 0, [[1, 2 * nq]]), AF.Ln, [('ps', bd)], [('tmp', ti)])
                ACTV(tmp.ap(ti * 512, [[1, 2 * nq]]), tmp.ap(ti * 512, [[1, 2 * nq]]), AF.Exp, [('tmp', ti)], [('tmp', ti)], scale=-1.0)
                TTO('dve', hT.ap(dst_off, [[1, nq]], 0, 64), psap(bo, 0, [[1, nq]], 0, 64), tmp.ap(ti * 512, [[1, nq]], 0, 64), ALU.mult, [('ps', bo), ('tmp', ti)], dst_keys)
                TTO('dve', hT.ap(dst_off, [[1, nq]], 64, 64), psap(bo, nq, [[1, nq]], 64, 64), tmp.ap(ti * 512 + nq, [[1, nq]], 64, 64), ALU.mult, [('ps', bo), ('tmp', ti)], dst_keys)

            for i, kt in enumerate(ktiles):
                hasb = kt.get('ma') is not None

                def sa(b, of, first, kt=kt, hasb=hasb):
                    MM(psap(b, of, [[1, nq]]), kt['ka'], qa, first, not hasb, kt['kkeys'] + qkeys, [('ps', b)], skip=True)

                def sb(b, of, first, kt=kt, hasb=hasb):
                    MM(psap(b, of, [[1, nq]]), kt['kb'], qb, first, not hasb, kt['kkeys'] + qkeys, [('ps', b)], skip=True)

                bias = None
                if hasb:
                    def bias(bx, by, of, kt=kt):
                        MM(psap(bx, of, [[1, nq]]), identb.ap(0, [[1, 128]]), kt['ma'], False, True, kt['mkeys'] + ['identb'], [('ps', bx)], skip=True)
                        MM(psap(by, of, [[1, nq]]), identb.ap(0, [[1, 128]]), kt['mb'], False, True, kt['mkeys'] + ['identb'], [('ps', by)], skip=True)

                def post(ea, eb, ekey, i=i, kt=kt):
                    MM(psap(bo, 0, [[1, nq]]), kt['v'], yb.ap(ea, [[1, nq]]), i == 0, False, ekey + kt['vkeys'], [('ps', bo)], skip=True)
                    MM(psap(bo, nq, [[1, nq]]), kt['v'], yb.ap(eb, [[1, nq]]), False, i == nk - 1, ekey + kt['vkeys'], [('ps', bo)], skip=True)
                    MM(psap(bd, 0, [[nq, 2], [1, nq]]), ones.ap(0, [[1, 128]]), yb.ap(ea, [[eb - ea, 2], [1, nq]]), i == 0, i == nk - 1, ekey + ['ones'], [('ps', bd)])
                    if i == nk - 1:
                        final()
                pipe.add(dict(sa=sa, sb=sb, bias=bias, post=post, nq=nq))

        def diff_attn(u):
            mark('diff_attn u%d' % u)
            if u == 0:
                blocks = [(s * 256, 256, [s * 2, s * 2 + 1]) for s in range(4)]
            else:
                blocks = [(0, 512, list(range(10))), (512, 512, list(range(10)))]
            nblk = 0
            for h in range(4):
                for (q0, nq, kts) in blocks:
                    if u == 0:
                        if modq:
                            mod_more()
                        elif nblk - 4 < 12:
                            mod_piece(1, nblk - 4)
                    nblk += 1
                    g = q0 // 512
                    packed = (nq == 256)
                    ncols = 2 * nq if packed else nq
                    accs = []

                    def sublnfin(accs=accs, h=h, q0=q0, nq=nq, g=g, packed=packed):
                        tdk = P.nxt('tdslot', 2)
                        tdoff = (6 + tdk) * 1024
                        tdkey = [('yb', 1, 4 + 2 * tdk), ('yb', 1, 5 + 2 * tdk)]
                        if packed:
                            o0 = tmp.ap(accs[0] * 512, [[1, nq]]); o1 = tmp.ap(accs[0] * 512 + nq, [[1, nq]]); rk = [('tmp', accs[0])]
                        else:
                            o0 = tmp.ap(accs[0] * 512, [[1, nq]]); o1 = tmp.ap(accs[1] * 512, [[1, nq]]); rk = [('tmp', accs[0]), ('tmp', accs[1])]
                        STT('dve', yb.ap(tdoff, [[1, nq]]), o1, lams.ap(4, [[1, 1]]), o0, ALU.mult, ALU.add, rk + ['neglam'], tdkey)
                        si = sqi()
                        ACTV(sqt.ap(si * 512, [[1, nq]]), yb.ap(tdoff, [[1, nq]]), AF.Square, tdkey, [('sq', si)])

                        def stage2(si=si, tdoff=tdoff, tdkey=tdkey):
                            bs = bankACC()
                            MM(psap(bs, 0, [[1, nq]]), ones.ap(0, [[1, 128]]), sqt.ap(si * 512, [[1, nq]]), True, True, [('sq', si), 'ones'], [('ps', bs)])
                            ri = P.nxt('rstd', 2)
                            ACTV(rstd.ap(ri * 512, [[1, nq]]), psap(bs, 0, [[1, nq]]), AF.Ln, [('ps', bs), 'epsb'], [('rstd', ri)], bias=cmap['eps'], scale=1.0 / 128)
                            ACTV(rstd.ap(ri * 512, [[1, nq]]), rstd.ap(ri * 512, [[1, nq]]), AF.Exp, [('rstd', ri)], [('rstd', ri)], scale=-0.5)
                            STT('dve', hT.ap((4 + h) * 1024 + q0, [[1, nq]]), yb.ap(tdoff, [[1, nq]]), subln.ap(1, [[1, 1]]), rstd.ap(ri * 512, [[1, nq]]), ALU.mult, ALU.mult,
                                tdkey + [('rstd', ri), 'subln'], [('h', 4 + h, g)])
                        pipe.late_new.append(stage2)

                    maps = [None] if packed else [0, 1]
                    for m in maps:
                        bo = bankACC()
                        bd = bankACC()
                        nk = len(kts)
                        for i, kt in enumerate(kts):
                            kcol = kt * 128 if kt < 8 else 1024 + (kt - 8) * 128
                            kkey = [('k', h, kt // 4 if kt < 8 else 'ctx')]
                            vkey = [('v', kt)]

                            def smm(b, of, mm_, kcol=kcol, kkey=kkey, h=h, q0=q0, nq=nq, g=g):
                                MM(psap(b, of, [[1, nq]]), ar.ap(KT + h * 1280 + kcol, [[1, 128]], mm_ * 64, 64), ar.ap(QT + h * 1024 + q0, [[1, nq]], mm_ * 64, 64),
                                   True, True, kkey + [('q', h, g)], [('ps', b)])
                            if packed:
                                sa = lambda b, of, first, smm=smm: smm(b, of, 0)
                                sb = lambda b, of, first, smm=smm: smm(b, of, 1)
                            else:
                                sa = lambda b, of, first, smm=smm, m=m: smm(b, of, m)
                                sb = None

                            def post(ea, eb, ekey, i=i, kt=kt, vkey=vkey, bo=bo, bd=bd, nk=nk, ncols=ncols, nq=nq, h=h, m=m, maps=maps, accs=accs, sublnfin=sublnfin, packed=packed):
                                if packed:
                                    rhs = yb.ap(ea, [[eb - ea, 2], [1, nq]])
                                    oap = lambda bk: psap(bk, 0, [[nq, 2], [1, nq]])
                                else:
                                    rhs = yb.ap(ea, [[1, nq]])
                                    oap = lambda bk: psap(bk, 0, [[1, nq]])
                                MM(oap(bo), ar.ap(VT0 + kt * 512 + h * 128, [[1, 128]]), rhs, i == 0, i == nk - 1, ekey + vkey, [('ps', bo)])
                                MM(oap(bd), ones.ap(0, [[1, 128]]), rhs, i == 0, i == nk - 1, ekey + ['ones'], [('ps', bd)])
                                if i == nk - 1:
                                    tr = tmpi()
                                    ACTV(tmp.ap(tr * 512, [[1, ncols]]), psap(bd, 0, [[1, ncols]]), AF.Ln, [('ps', bd)], [('tmp', tr)])
                                    ACTV(tmp.ap(tr * 512, [[1, ncols]]), tmp.ap(tr * 512, [[1, ncols]]), AF.Exp, [('tmp', tr)], [('tmp', tr)], scale=-1.0)
                                    to = tmpi()
                                    TTO('dve', tmp.ap(to * 512, [[1, ncols]]), psap(bo, 0, [[1, ncols]]), tmp.ap(tr * 512, [[1, ncols]]), ALU.mult, [('ps', bo), ('tmp', tr)], [('tmp', to)])
                                    accs.append(to)
                                    if m == maps[-1]:
                                        sublnfin()
                            pipe.add(dict(sa=sa, sb=sb, post=post, nq=nq))

        deferred_pe = []

        def flush_deferred():
            while deferred_pe:
                deferred_pe.pop(0)()

        def post_evac(b, g, mo, bs):
            CP('act', yb.ap(g * 4096 + mo * 512, [[1, 512]]), psap(b, 0, [[1, 512]]), [('ps', b)], [('yb', g, mo)])
            si = sqi()
            ACTV(sqt.ap(si * 512, [[1, 512]]), psap(b, 0, [[1, 512]]), AF.Square, [('ps', b)], [('sq', si)])
            flush_deferred()
            deferred_pe.append(lambda: MM(psap(bs, 0, [[1, 512]]), ones.ap(0, [[1, 128]]), sqt.ap(si * 512, [[1, 512]]), mo == 0, mo == 7, [('sq', si), 'ones'], [('ps', bs)]))

        def post_update(u, li, w, g, bs):
            flush_deferred()
            ri = P.nxt('rstd', 2)
            ACTV(rstd.ap(ri * 512, [[1, 512]]), psap(bs, 0, [[1, 512]]), AF.Ln, [('ps', bs), 'epsb'], [('rstd', ri)], bias=cmap['eps'], scale=1.0 / 1024)
            ACTV(Et.ap(3 * 512, [[1, 512]]), rstd.ap(ri * 512, [[1, 512]]), AF.Exp, [('rstd', ri)], [('E', 3)], scale=-0.5)
            for mo in range(8):
                ti = P.nxt('Et3', 3)
                TTO('dve', Et.ap(ti * 512, [[1, 512]]), yb.ap(g * 4096 + mo * 512, [[1, 512]]), Et.ap(3 * 512, [[1, 512]]), ALU.mult, [('yb', g, mo), ('E', 3)], [('E', ti)])
                STT('dve', xT.ap(mo * 1024 + g * 512, [[1, 512]]), Et.ap(ti * 512, [[1, 512]]), coefap(li, w, mo, u), xT.ap(mo * 1024 + g * 512, [[1, 512]]), ALU.mult, ALU.add,
                    [('E', ti), ('coef', li), ('x', mo, g)], [('x', mo, g)])

        def wout_phase(u, li):
            mark('wout u%d l%d' % (u, li))
            s0 = wpiece('w_out', li * 1024, 0)
            s1 = wpiece('w_out', li * 1024, 512)
            for g in range(2):
                bs = bankB()
                for mo in range(8):
                    s = s0 if mo < 4 else s1
                    dense_fm(s, mo % 4, g, lambda b, g=g, mo=mo, bs=bs: post_evac(b, g, mo, bs))
                if g == 0:
                    post_update(u, li, 1, g, bs)
                else:
                    flush_deferred()
                    tailq.append(lambda bs=bs: post_update(u, li, 1, 1, bs))

        def mlp_phase(u, li):
            norm_mod(u, li, 2)
            mark('mlp1 u%d l%d' % (u, li))
            for pc in range(8):
                s = wpiece('mlp_w1', li * 1024, pc * 512)
                for g in range(2):
                    for mc in range(4):
                        m = pc * 4 + mc

                        def ev(b, m=m, g=g):
                            ei = Ei()
                            ACTV(Et.ap(ei * 512, [[1, 512]]), psap(b, 0, [[1, 512]]), AF.Relu, [('ps', b)], [('E', ei)])
                            TTO('dve', ar.ap(m * 1024 + g * 512, [[1, 512]]), Et.ap(ei * 512, [[1, 512]]), Et.ap(ei * 512, [[1, 512]]), ALU.mult, [('E', ei)], [('aT', m, g)])
                        dense_fm(s, mc, g, ev)
            mark('mlp2 u%d l%d' % (u, li))
            bss = [bankB(), bankB()]
            for ld in range(4):
                src = D['mlp_w2'].ap()[li * 4096:(li + 1) * 4096, ld * 256:(ld + 1) * 256].rearrange("(kc p) c -> p kc c", p=128)
                s2 = wload(src, [[256, 32], [1, 256]], stride=256)
                for g in range(2):
                    for mo in (2 * ld, 2 * ld + 1):
                        s = WV(s2.slot, (mo % 2) * 128, 256)
                        dense_fm(s, 0, g, lambda b, g=g, mo=mo: post_evac(b, g, mo, bss[g]), kcn=32, wstride=128, src=ar, srckey='aT')
                    if ld == 3 and g == 0:
                        post_update(u, li, 3, 0, bss[0])
            flush_deferred()
            tailq.append(lambda: post_update(u, li, 3, 1, bss[1]))

        def layer0(u):
            norm_mod(u, 0, 1)
            fence = [('h', 0, 0)]
            if u == 1 and 'ctx' not in SKIP:
                load_ctx_k('cdk', 8, KT, 1280, 4, 'k')
            s = wpiece('w_in_even', 0, 0)
            for c in range(4):
                for g in range(2):
                    dense_fm(s, c, g, lambda b, c=c, g=g: CP('act', ar.ap(AB + c * 1024 + g * 512, [[1, 512]]), psap(b, 0, [[1, 512]]), [('ps', b)], [('ab', c, g)]))
            mod_more()
            s = wpiece('w_in_even', 0, 512)
            for c in range(4):
                for g in range(2):
                    dense_fm(s, c, g, lambda b, c=c, g=g: CP('act', ar.ap(UU + c * 1024 + g * 512, [[1, 512]]), psap(b, 0, [[1, 512]]), [('ps', b)], [('u', c, g)]))
            mod_more()
            s = wpiece('w_in_even', 0, 1024)
            for c in range(4):
                for g in range(2):
                    dense_fm(s, c, g, lambda b, c=c, g=g: TTO('dve', ar.ap(UU + c * 1024 + g * 512, [[1, 512]]), psap(b, 0, [[1, 512]]), ar.ap(UU + c * 1024 + g * 512, [[1, 512]]),
                                                               ALU.mult, [('ps', b), ('u', c, g)], [('u', c, g)]))
            mod_more()
            s = wpiece('w_in_even', 0, 1536)
            for c in range(4):
                for g in range(2):
                    if u == 0:
                        dense_fm(s, c, g, lambda b, c=c, g=g: CP(evq(), ar.ap(QT + c * 1024 + g * 512, [[1, 512]]), psap(b, 0, [[1, 512]]), [('ps', b)], [('q', c, g)]))
                    else:
                        dense_fm(s, c, g, lambda b, c=c, g=g: rope_evac(b, g, QT + c * 1024 + g * 512, [('q', c, g)]))
            mod_more()
            s = wpiece('w_in_even', 0, 2048)
            for c in range(4):
                for g in range(2):
                    if u == 0:
                        dense_fm(s, c, g, lambda b, c=c, g=g: CP(evq(), ar.ap(KT + c * 1280 + g * 512, [[1, 512]]), psap(b, 0, [[1, 512]]), [('ps', b)], [('k', c, g)]))
                    else:
                        dense_fm(s, c, g, lambda b, c=c, g=g: rope_evac(b, g, KT + c * 1280 + g * 512, [('k', c, g)]))
            if u == 0 and 'ktm' not in SKIP:
                for t in range(8):
                    def ev(b, t=t):
                        ti = tmpi()
                        CP('dve', tmp.ap(ti * 512, [[1, 512]]), psap(b, 0, [[1, 512]]), [('ps', b)], [('tmp', ti)])
                        sq_, tt = t // 2, t % 2
                        out_ops.append(DMA('sp', bass.AP(D['ndk'], sq_ * 8 * 16384 + tt * 8192, [[64, 128], [16384, 8], [1, 64]]), tmp.ap(ti * 512, [[64, 8], [1, 64]]), [('tmp', ti)], []))
                    dense_tm(s, 0, 512, t, ev)
            s = wpiece('w_in_even', 0, 2560)
            for t in range(8):
                def ev(b, t=t):
                    CP('act', ar.ap(VT0 + t * 512, [[1, 512]]), psap(b, 0, [[1, 512]]), [('ps', b)], [('v', t)])
                    if u == 0 and 'vout' not in SKIP:
                        ti = tmpi()
                        CP('dve', tmp.ap(ti * 512, [[1, 512]]), psap(b, 0, [[1, 512]]), [('ps', b)], [('tmp', ti)])
                        sq_, tt = t // 2, t % 2
                        out_ops.append(DMA('sp', bass.AP(D['ndv'], sq_ * 4 * 32768 + tt * 16384, [[128, 128], [32768, 4], [1, 128]]), tmp.ap(ti * 512, [[128, 4], [1, 128]]), [('tmp', ti)], []))
                dense_tm(s, 0, 512, t, ev)
            mark('conv u%d' % u)
            nseq, sl = (4, 256) if u == 0 else (1, 1024)
            for c in range(4 if 'conv' not in SKIP else 0):
                ukeys = [('u', c, 0), ('u', c, 1)]
                t1 = tmpi(); t2 = tmpi()
                acc_keys = [('tmp', t1), ('tmp', t2)]
                acc = lambda off, n: tmp.ap(t1 * 512 + off, [[1, n]])
                if t2 != t1 + 1:
                    t1 = tmpi(); t2 = tmpi()
                    acc_keys = [('tmp', t1), ('tmp', t2)]
                base = t1 * 512
                TS('dve', tmp.ap(base, [[1, 1024]]), ar.ap(UU + c * 1024, [[1, 1024]]), cw.ap(c * 3 + 1, [[1, 1]]), None, ALU.mult, None, ukeys + ['cw'], acc_keys)
                STT('dve', tmp.ap(base + 1, [[sl, nseq], [1, sl - 1]]), ar.ap(UU + c * 1024, [[sl, nseq], [1, sl - 1]]), cw.ap(c * 3 + 0, [[1, 1]]),
                    tmp.ap(base + 1, [[sl, nseq], [1, sl - 1]]), ALU.mult, ALU.add, ukeys + ['cw'] + acc_keys, acc_keys)
                STT('dve', tmp.ap(base, [[sl, nseq], [1, sl - 1]]), ar.ap(UU + c * 1024 + 1, [[sl, nseq], [1, sl - 1]]), cw.ap(c * 3 + 2, [[1, 1]]),
                    tmp.ap(base, [[sl, nseq], [1, sl - 1]]), ALU.mult, ALU.add, ukeys + ['cw'] + acc_keys, acc_keys)
                TTO('dve', hT.ap(c * 1024, [[1, 1024]]), tmp.ap(base, [[1, 1024]]), ar.ap(AB + c * 1024, [[1, 1024]]), ALU.mult,
                    acc_keys + [('ab', c, 0), ('ab', c, 1)], [('h', c, 0), ('h', c, 1)])
            if u == 1 and 'ctx' not in SKIP:
                for tt in range(2):
                    DMA('pool', ar.ap(VT0 + (8 + tt) * 512, [[128, 4], [1, 128]]), bass.AP(D['cdv'], tt * 16384, [[128, 128], [32768, 4], [1, 128]]), fence, [('v', 8 + tt)])
            if stop_after == ('A', u):
                raise StopBuild('h')
            diff_attn(u)
            if stop_after == ('B', u):
                pipe.flush()
                raise StopBuild('h')
            pipe.flush()
            wout_phase(u, 0)
            if stop_after == ('C', u):
                raise StopBuild('x')
            mlp_phase(u, 0)

        def layer1(u):
            norm_mod(u, 1, 1)
            fence = [('h', 0, 0)]
            if u == 1:
                load_ctx_k('cnk', 8, KT, 1280, 4, 'k')
                load_ctx_k('csk', 2, K2, 1280, 2, 'k2', dup=True)
            s = wpiece('w_in_odd', 0, 0)
            for c in range(4):
                for g in range(2):
                    dense_fm(s, c, g, lambda b, c=c, g=g: CP(evq(), ar.ap(QT + c * 1024 + g * 512, [[1, 512]]), psap(b, 0, [[1, 512]]), [('ps', b)], [('q', c, g)]))
            s = wpiece('w_in_odd', 0, 512)
            for c in range(4):
                for g in range(2):
                    dense_fm(s, c, g, lambda b, c=c, g=g: CP(evq(), ar.ap(KT + c * 1280 + g * 512, [[1, 512]]), psap(b, 0, [[1, 512]]), [('ps', b)], [('k', c, g)]))
            if u == 0:
                for t in range(8):
                    def ev(b, t=t):
                        ti = tmpi()
                        CP('dve', tmp.ap(ti * 512, [[1, 512]]), psap(b, 0, [[1, 512]]), [('ps', b)], [('tmp', ti)])
                        sq_, tt = t // 2, t % 2
                        out_ops.append(DMA('sp', bass.AP(D['nnk'], sq_ * 8 * 16384 + tt * 8192, [[64, 128], [16384, 8], [1, 64]]), tmp.ap(ti * 512, [[64, 8], [1, 64]]), [('tmp', ti)], []))
                    dense_tm(s, 0, 512, t, ev)
            s = wpiece('w_in_odd', 0, 1024)
            for t in range(8):
                def ev(b, t=t):
                    CP('act', ar.ap(VT1 + t * 512, [[1, 512]]), psap(b, 0, [[1, 512]]), [('ps', b)], [('v', t)])
                    if u == 0:
                        ti = tmpi()
                        CP('dve', tmp.ap(ti * 512, [[1, 512]]), psap(b, 0, [[1, 512]]), [('ps', b)], [('tmp', ti)])
                        sq_, tt = t // 2, t % 2
                        out_ops.append(DMA('sp', bass.AP(D['nnv'], sq_ * 8 * 16384 + tt * 8192, [[64, 128], [16384, 8], [1, 64]]), tmp.ap(ti * 512, [[64, 8], [1, 64]]), [('tmp', ti)], []))
                dense_tm(s, 0, 512, t, ev)
            if u == 1:
                for i in range(7):
                    b = bankA()
                    for kc in range(8):
                        gk = [('h', kc, 0), ('h', kc, 1)]
                        MM(psap(b, 0, [[1, 512]]), hT.ap(kc * 1024 + 64 + i * 128, [[1, 128]]), wr.ap(s.base + kc * s.stride, [[1, 512]]), kc == 0, kc == 7, [s.key] + gk, [('ps', b)])
                    CP(evq(), ar.ap(VT2 + i * 512, [[1, 512]]), psap(b, 0, [[1, 512]]), [('ps', b)], [('v2', i)])
            s = wpiece('w_in_odd', 0, 1536)
            for c in range(4):
                for g in range(2):
                    if u == 0:
                        dense_fm(s, c, g, lambda b, c=c, g=g: CP(evq(), ar.ap(Q2 + c * 1024 + g * 512, [[1, 512]]), psap(b, 0, [[1, 512]]), [('ps', b)], [('q2', c, g)]))
                    else:
                        dense_fm(s, c, g, lambda b, c=c, g=g: rope_evac(b, g, Q2 + c * 1024 + g * 512, [('q2', c, g)]))
            srcA = bass.AP(D['w_in_odd'], 2048, [[2304, 128], [2304 * 128, 8], [1, 256]])
            extra = []
            for g_ in range(2):
                for dpl in range(2):
                    extra.append((2048 + g_ * 128 + dpl * 64, [[256, 8], [1, 64]], bass.AP(D['w_in_odd'], 2048 + g_ * 64, [[2304, 128], [2304 * 128, 8], [1, 64]])))
            s = wload(srcA, [[256, 8], [1, 256]], extra=extra, stride=256)
            for gk_ in range(2):
                for g in range(2):
                    b = bankA()
                    for kc in range(8):
                        MM(psap(b, 0, [[1, 512]]), wr.ap(s.base + 2048 + kc * 256 + gk_ * 128, [[1, 128]]), hT.ap(kc * 1024 + g * 512, [[1, 512]]), kc == 0, kc == 7,
                           [s.key, ('h', kc, g)], [('ps', b)])
                    if u == 0:
                        CP(evq(), ar.ap(K2 + gk_ * 1280 + g * 512, [[1, 512]]), psap(b, 0, [[1, 512]]), [('ps', b)], [('k2', gk_, g)])
                    else:
                        rope_evac(b, g, K2 + gk_ * 1280 + g * 512, [('k2', gk_, g)])
            for t in range(8):
                def ev(b, t=t):
                    for dpl in range(2):
                        CP(['act', 'dve'][dpl], ar.ap(VD + t * 256 + dpl * 64, [[128, 2], [1, 64]]), psap(b, 128, [[64, 2], [1, 64]]), [('ps', b)], [('vd', t)])
                    if u == 0:
                        ti = tmpi()
                        CP('dve', tmp.ap(ti * 512, [[1, 256]]), psap(b, 0, [[1, 256]]), [('ps', b)], [('tmp', ti)])
                        sq_, tt = t // 2, t % 2
                        out_ops.append(DMA('sp', bass.AP(D['nsk'], sq_ * 2 * 16384 + tt * 8192, [[64, 128], [16384, 2], [1, 64]]), tmp.ap(ti * 512, [[64, 2], [1, 64]]), [('tmp', ti)], []))
                        out_ops.append(DMA('sp', bass.AP(D['nsv'], sq_ * 2 * 16384 + tt * 8192, [[64, 128], [16384, 2], [1, 64]]), tmp.ap(ti * 512 + 128, [[64, 2], [1, 64]]), [('tmp', ti)], []))
                dense_tm(s, 0, 256, t, ev, wstride=256)

            if u == 1:
                for tt in range(2):
                    DMA('pool', ar.ap(VT1 + (8 + tt) * 512, [[64, 8], [1, 64]]), bass.AP(D['cnv'], tt * 8192, [[64, 128], [16384, 8], [1, 64]]), fence, [('v', 8 + tt)])
                    for dpl in range(2):
                        DMA('pool', ar.ap(VD + (8 + tt) * 256 + dpl * 64, [[128, 2], [1, 64]]), bass.AP(D['csv'], tt * 8192, [[64, 128], [16384, 2], [1, 64]]), fence, [('vd', 8 + tt)])
            mark('attn1 u%d' % u)

            def kt_c(c, col, vtile_ap, vkeys, kkeys, ma=None, mb=None, mkeys=None):
                return dict(ka=ar.ap(KT + c * 1280 + col, [[1, 128]], 0, 64), kb=ar.ap(KT + c * 1280 + col, [[1, 128]], 64, 64), kkeys=kkeys,
                            v=vtile_ap, vkeys=vkeys, ma=ma, mb=mb, mkeys=mkeys or [])

            def kt_d(gk_, col, vt, kkeys, m=None, mkeys=None):
                return dict(ka=ar.ap(K2 + gk_ * 1280 + col, [[1, 128]], 0, 64), kb=ar.ap(K2 + gk_ * 1280 + col, [[1, 128]], 64, 64), kkeys=kkeys,
                            v=ar.ap(VD + vt * 256 + gk_ * 128, [[1, 128]]), vkeys=[('vd', vt)], ma=m, mb=m, mkeys=mkeys or [])

            if u == 0:
                for sq_ in range(4):
                    g = sq_ // 2
                    q0 = sq_ * 256
                    for c in range(4):
                        kts = [kt_c(c, q0 + kt * 128, ar.ap(VT1 + (sq_ * 2 + kt) * 512 + c * 128, [[1, 128]]), [('v', sq_ * 2 + kt)], [('k', c, g)]) for kt in range(2)]
                        attn_pair(kts, ar.ap(QT + c * 1024 + q0, [[1, 256]], 0, 64), ar.ap(QT + c * 1024 + q0, [[1, 256]], 64, 64), [('q', c, g)], 256,
                                  c * 1024 + q0, [('h', c, g)])
                    for c in range(4):
                        gk_ = c // 2
                        kts = [kt_d(gk_, q0 + kt * 128, sq_ * 2 + kt, [('k2', gk_, g)]) for kt in range(2)]
                        attn_pair(kts, ar.ap(Q2 + c * 1024 + q0, [[1, 256]], 0, 64), ar.ap(Q2 + c * 1024 + q0, [[1, 256]], 64, 64), [('q2', c, g)], 256,
                                  (4 + c) * 1024 + q0, [('h', 4 + c, g)], sink_h=2 * c)
            else:
                groups = [(0, 4, 0), (4, 1, 0)] + [(r, 1, r - 4) for r in range(5, 12)] + [(12, 4, 8)]
                for c in range(4):
                    for (r0, nr, rs) in groups:
                        nq = nr * 64
                        q0 = r0 * 64
                        g = q0 // 512
                        kts = []
                        for jj in range(4):
                            ka_ = rs + 2 * jj
                            col = ka_ * 64
                            if ka_ % 2 == 0:
                                vap = ar.ap(VT1 + (ka_ // 2) * 512 + c * 128, [[1, 128]]); vk = [('v', ka_ // 2)]
                            else:
                                vap = ar.ap(VT2 + ((ka_ - 1) // 2) * 512 + c * 128, [[1, 128]]); vk = [('v2', (ka_ - 1) // 2)]
                            i0 = 6 - (ka_ - r0)
                            assert 0 <= i0 and i0 + nr <= 14
                            ma = bank.ap((2 * c) * 896 + i0 * 64, [[1, nq]])
                            mb = bank.ap((2 * c + 1) * 896 + i0 * 64, [[1, nq]])
                            kk = [('k', c, (col // 512)), ('k', c, ((col + 127) // 512))]
                            kts.append(kt_c(c, col, vap, vk, kk, ma, mb, ['bank']))
                        for tt in range(2):
                            kts.append(kt_c(c, 1024 + tt * 128, ar.ap(VT1 + (8 + tt) * 512 + c * 128, [[1, 128]]), [('v', 8 + tt)], [('k', c, 'ctx')]))
                        attn_pair(kts, ar.ap(QT + c * 1024 + q0, [[1, nq]], 0, 64), ar.ap(QT + c * 1024 + q0, [[1, nq]], 64, 64), [('q', c, g)], nq,
                                  c * 1024 + q0, [('h', c, g)])
                mark('attn1D u%d' % u)
                for c in range(4):
                    gk_ = c // 2
                    for n in range(8):
                        g = n // 4
                        q0 = n * 128
                        kts = []
                        if n > 0:
                            kts.append(kt_d(gk_, (n - 1) * 128, n - 1, [('k2', gk_, (n - 1) // 4)], mprev.ap(0, [[1, 128]]), ['mprev']))
                        kts.append(kt_d(gk_, n * 128, n, [('k2', gk_, g)]))
                        if n < 7:
                            kts.append(kt_d(gk_, (n + 1) * 128, n + 1, [('k2', gk_, (n + 1) // 4)], mnext.ap(0, [[1, 128]]), ['mnext']))
                        for tt in range(2):
                            kts.append(kt_d(gk_, 1024 + tt * 128, 8 + tt, [('k2', gk_, 'ctx')]))
                        attn_pair(kts, ar.ap(Q2 + c * 1024 + q0, [[1, 128]], 0, 64), ar.ap(Q2 + c * 1024 + q0, [[1, 128]], 64, 64), [('q2', c, g)], 128,
                                  (4 + c) * 1024 + q0, [('h', 4 + c, g)], sink_h=2 * c)
            pipe.flush()
            wout_phase(u, 1)
            mlp_phase(u, 1)

        epsb = sb('epsb', 1, F32)
        P.op('dve', lambda e: e.memset(epsb.ap(0, [[1, 1]]), EPS), writes=['epsb'])
        cmap['eps'] = epsb.ap(0, [[1, 1]])
        setup()
        load_x(0)
        compute_mod(0)
        modq.extend(range(4, 12))
        done = False
        for u in range(2):
            if u == 1:
                load_x(u)
            if stop_after == ('X', u):
                store_x(u)
                break
            if stop_after == ('N', u):
                norm_mod(u, 0, 1)
                for kc in range(8):
                    for g in range(2):
                        CP('dve', xT.ap(kc * 1024 + g * 512, [[1, 512]]), hT.ap(kc * 1024 + g * 512, [[1, 512]]), [('h', kc, g)], [('x', kc, g)])
                store_x(u)
                break
            try:
                layer0(u)
            except StopBuild as ex:
                if str(ex) == 'h':
                    for kc in range(8):
                        for g in range(2):
                            CP('dve', xT.ap(kc * 1024 + g * 512, [[1, 512]]), hT.ap(kc * 1024 + g * 512, [[1, 512]]), [('h', kc, g)], [('x', kc, g)])
                store_x(u)
                break
            if stop_after == ('L0', u):
                store_x(u)
                done = True
                break
            layer1(u)
            store_x(u)
            if stop_after == ('L1', u):
                done = True
                break
        mark('end')
        if os.environ.get('DBG_MARKS'):
            import json
            json.dump(marks, open(os.environ['DBG_MARKS'], 'w'))
        P.emit(final_wait_ops=out_ops)
        print("ops:", len(P.ops))
    return nc


def host_consts():
    c = {}
    c['c_ident'] = np.eye(128, dtype=np.float32)
    Rm = np.zeros((64, 64), np.float32)
    for i in range(2):
        for f in range(16):
            a, b = i * 32 + f, i * 32 + 16 + f
            Rm[a, b] = -1.0
            Rm[b, a] = 1.0
    R2 = np.zeros((128, 128), np.float32)
    R2[:64, :64] = Rm.T
    R2[64:, 64:] = Rm.T
    c['c_R2'] = R2
    t = np.arange(1024)
    rows = (t // 64).astype(np.float32)
    cols = (t % 64).astype(np.float32)
    inv = (1.0 / (np.float32(10000.0) ** (np.arange(16, dtype=np.float32) / np.float32(16)))).astype(np.float32)
    cos = np.zeros((128, 1024), np.float32)
    sin = np.zeros((128, 1024), np.float32)
    for p in range(128):
        d = p % 64
        i, f = d // 32, d % 16
        pos = rows if i == 0 else cols
        ang = (pos * inv[f]).astype(np.float32)
        cos[p] = np.cos(ang)
        sin[p] = np.sin(ang)
    c['c_cos'] = cos
    c['c_sin'] = sin
    j = np.arange(128)[:, None]
    i = np.arange(128)[None, :]
    c['c_mprev'] = ((j >= i).astype(np.float32) - 1.0) * 30000.0
    c['c_mnext'] = ((j <= i).astype(np.float32) - 1.0) * 30000.0
    qc = np.arange(64)
    cs = np.clip(qc - 8, 0, 48)
    kc = np.arange(64)
    ok = (kc[:, None] >= cs[None, :]) & (kc[:, None] < cs[None, :] + 16)
    c['c_colok'] = np.concatenate([ok, ok], axis=0).astype(np.float32)
    c['c_colneg'] = (c['c_colok'] - 1.0) * 30000.0
    return c


def rpb_layout(rpb):
    kc = np.arange(64)[:, None]
    qc = np.arange(64)[None, :]
    dc = np.clip(kc - qc + 15, 0, 30)
    out = np.zeros((128, 8, 14, 64), np.float32)
    for i in range(14):
        out[:64, :, i, :] = np.transpose(rpb[:, (6 - i) + 7][:, dc], (1, 0, 2))
        out[64:, :, i, :] = np.transpose(rpb[:, (7 - i) + 7][:, dc], (1, 0, 2))
    return np.ascontiguousarray(out.reshape(128, 8 * 14 * 64))


_NC_CACHE = {}


def kernel(x_prompt, x_sample, cache_diff_k, cache_diff_v, cache_na_k, cache_na_v, cache_swa_k, cache_swa_v, c, c_ctx, mod_w, mod_b,
           norm_mix_pre, norm_mix_post, norm_mlp_pre, norm_mlp_post, w_in_even, conv_w, lambda_q1, lambda_k1, lambda_q2, lambda_k2,
           subln, w_in_odd, rpb, sink, w_out, mlp_w1, mlp_w2, _stop_after=None):
    f = lambda a: np.ascontiguousarray(np.asarray(a, dtype=np.float32))
    if _stop_after not in _NC_CACHE:
        _NC_CACHE[_stop_after] = build(_stop_after)
    nc = _NC_CACHE[_stop_after]
    consts = host_consts()
    shared = dict(
        mod_w=f(mod_w).reshape(2048, 6144), mod_b=f(mod_b),
        gains=np.stack([f(norm_mix_pre).reshape(-1), f(norm_mix_post).reshape(-1), f(norm_mlp_pre).reshape(-1), f(norm_mlp_post).reshape(-1)]),
        w_in_even=f(w_in_even)[0], conv_w=f(conv_w)[0],
        lams=np.stack([f(lambda_q1)[0], f(lambda_k1)[0], f(lambda_q2)[0], f(lambda_k2)[0]]),
        subln=f(subln)[0].reshape(128, 1), w_in_odd=f(w_in_odd)[0], rpbT=rpb_layout(f(rpb)[0]), sink=f(sink)[0].reshape(1, 8),
        w_out=f(w_out).reshape(2048, 1024), mlp_w1=f(mlp_w1).reshape(2048, 4096), mlp_w2=f(mlp_w2).reshape(8192, 1024),
    )
    shared.update(consts)
    xp = f(x_prompt); xs = f(x_sample)
    in_maps = []
    for k in range(NCORES):
        m = dict(shared)
        m['xP'] = xp[4 * k:4 * k + 4].reshape(1024, 1024)
        m['xS'] = xs[k]
        m['cdk'] = f(cache_diff_k)[k, 0].reshape(8, 256, 64)
        m['cdv'] = f(cache_diff_v)[k, 0]
        m['cnk'] = f(cache_na_k)[k, 0]
        m['cnv'] = f(cache_na_v)[k, 0]
        m['csk'] = f(cache_swa_k)[k, 0]
        m['csv'] = f(cache_swa_v)[k, 0]
        m['cvec'] = np.stack([f(c_ctx), f(c)[k]])
        in_maps.append(m)
    in_maps = in_maps[:DBG_CORES]
    res = run_bass_kernel_spmd(nc, in_maps, core_ids=list(range(DBG_CORES)))
    R = list(res.results) + [res.results[0]] * (NCORES - DBG_CORES)
    y_prompt = np.concatenate([r['yP'].reshape(4, 256, 1024) for r in R], axis=0)
    y_sample = np.stack([r['yS'] for r in R], axis=0)
    ndk = np.concatenate([r['ndk'].reshape(4, 1, 4, 2, 256, 64) for r in R], axis=0)
    ndv = np.concatenate([r['ndv'].reshape(4, 1, 4, 256, 128) for r in R], axis=0)
    nnk = np.concatenate([r['nnk'].reshape(4, 1, 8, 256, 64) for r in R], axis=0)
    nnv = np.concatenate([r['nnv'].reshape(4, 1, 8, 256, 64) for r in R], axis=0)
    nsk = np.concatenate([r['nsk'].reshape(4, 1, 2, 256, 64) for r in R], axis=0)
    nsv = np.concatenate([r['nsv'].reshape(4, 1, 2, 256, 64) for r in R], axis=0)
    return (y_prompt, y_sample, ndk, ndv, nnk, nnv, nsk, nsv)
```

```python
import math
import os
SKIP = set(os.environ.get('DBG_SKIP', '').split(','))
DBG_CORES = int(os.environ.get('DBG_CORES', '8'))
from contextlib import ExitStack
import numpy as np
import concourse.bass as bass
import concourse.mybir as mybir
from concourse.bass_utils import run_bass_kernel_spmd

F32 = mybir.dt.float32
BF16 = mybir.dt.bfloat16
AF = mybir.ActivationFunctionType
ALU = mybir.AluOpType
AX = mybir.AxisListType
EPS = 1e-6
NCORES = 8


class Prog:
    def __init__(self, nc):
        self.nc = nc
        self.ops = []
        self.last_w = {}
        self.readers = {}
        self.rot = {}

    def nxt(self, name, n, base=0):
        v = self.rot.get(name, 0)
        self.rot[name] = v + 1
        return base + (v % n)

    def op(self, eng, fn, reads=(), writes=(), dma=False, semkey=None):
        idx = len(self.ops)
        raw = set()
        oth = set()
        for k in reads:
            w = self.last_w.get(k)
            if w is not None:
                raw.add(w)
            if isinstance(k, tuple) and k[0] == 'ps':
                for r in self.readers.get(k, ()):
                    oth.add(r)
        for k in writes:
            w = self.last_w.get(k)
            if w is not None:
                oth.add(w)
            for r in self.readers.get(k, ()):
                oth.add(r)
        for k in reads:
            self.readers.setdefault(k, []).append(idx)
        for k in writes:
            self.last_w[k] = idx
            self.readers[k] = []
        raw.discard(idx)
        oth.discard(idx)
        self.ops.append(dict(eng=eng, fn=fn, raw=raw, oth=oth - raw, dma=dma, semkey=semkey, sig=False))
        return idx

    def emit(self, final_wait_ops=()):
        nc = self.nc
        ops = self.ops
        for i, o in enumerate(ops):
            need = set(o['raw'])
            for d in o['oth']:
                p = ops[d]
                if p['eng'] == o['eng'] and not p['dma'] and not o['dma']:
                    continue
                need.add(d)
            o['need'] = need
            for d in need:
                ops[d]['sig'] = True
        for d in final_wait_ops:
            ops[d]['sig'] = True
        for o in ops:
            if o['dma']:
                o['sig'] = True
        stack = ExitStack()
        sems = {}
        cnt = {}
        rot = {}
        ROT = 12
        for i, o in enumerate(ops):
            if not o['sig']:
                continue
            if o['dma']:
                if o['semkey'] is not None:
                    name = 'dk_%s' % (o['semkey'],)
                else:
                    r = rot.get(o['eng'], 0)
                    rot[o['eng']] = r + 1
                    name = 'dr_%s_%d' % (o['eng'], r % ROT)
                cnt[name] = cnt.get(name, 0) + 16
            else:
                name = 'c_%s' % o['eng']
                cnt[name] = cnt.get(name, 0) + 1
            o['sem'] = name
            o['val'] = cnt[name]
            if name not in sems:
                sems[name] = stack.enter_context(nc.semaphore(name.replace(' ', '').replace("'", '').replace(',', '_').replace('(', '').replace(')', '')))
        per_eng = {e: [] for e in ['pe', 'act', 'dve', 'pool', 'sp']}
        for i, o in enumerate(ops):
            per_eng[o['eng']].append(i)
        final_wait_ops = list(final_wait_ops)

        def emit_eng(engname, engobj):
            waited = {}
            for i in per_eng[engname]:
                o = ops[i]
                w = {}
                for d in o['need']:
                    p = ops[d]
                    nm, v = p['sem'], p['val']
                    if waited.get(nm, 0) >= v:
                        continue
                    if w.get(nm, 0) < v:
                        w[nm] = v
                for nm, v in w.items():
                    engobj.wait_ge(sems[nm], v)
                    waited[nm] = v
                if o['dma'] and o['sig']:
                    pv = o['val'] - 16
                    if pv > 0 and waited.get(o['sem'], 0) < pv:
                        engobj.wait_ge(sems[o['sem']], pv)
                        waited[o['sem']] = pv
                ins = o['fn'](engobj)
                if o['sig']:
                    ins.then_inc(sems[o['sem']], 16 if o['dma'] else 1)
            if engname == 'sp':
                w = {}
                for d in final_wait_ops:
                    p = ops[d]
                    nm, v = p['sem'], p['val']
                    if w.get(nm, 0) < v:
                        w[nm] = v
                for nm, v in w.items():
                    if waited.get(nm, 0) < v:
                        engobj.wait_ge(sems[nm], v)

        with stack:
            with nc.Block() as block:
                @block.tensor
                def _(e):
                    emit_eng('pe', e)

                @block.scalar
                def _(e):
                    emit_eng('act', e)

                @block.vector
                def _(e):
                    emit_eng('dve', e)

                @block.gpsimd
                def _(e):
                    emit_eng('pool', e)

                @block.sync
                def _(e):
                    emit_eng('sp', e)


class StopBuild(Exception):
    pass


class TT:
    def __init__(self, h, free):
        self.h = h
        self.F = free

    def ap(self, off, dims, p0=0, pn=128):
        return bass.AP(self.h, p0 * self.F + off, [[self.F, pn]] + [list(d) for d in dims])


def build(stop_after=None):
    nc = bass.Bass("TRN2", target_bir_lowering=False)
    D = {}

    def din(name, shape):
        D[name] = nc.dram_tensor(name, list(shape), F32, kind="ExternalInput")

    def dout(name, shape):
        D[name] = nc.dram_tensor(name, list(shape), F32, kind="ExternalOutput")

    din('xP', [1024, 1024]); din('xS', [1024, 1024])
    din('cdk', [8, 256, 64]); din('cdv', [4, 256, 128])
    din('cnk', [8, 256, 64]); din('cnv', [8, 256, 64])
    din('csk', [2, 256, 64]); din('csv', [2, 256, 64])
    din('cvec', [2, 1024])
    din('mod_w', [2048, 6144]); din('mod_b', [2, 6144])
    din('gains', [4, 2048])
    din('w_in_even', [1024, 3072]); din('conv_w', [3, 512]); din('lams', [4, 64]); din('subln', [128, 1])
    din('w_in_odd', [1024, 2304]); din('rpbT', [128, 8 * 14 * 64]); din('sink', [1, 8])
    din('w_out', [2048, 1024]); din('mlp_w1', [2048, 4096]); din('mlp_w2', [8192, 1024])
    din('c_ident', [128, 128]); din('c_R2', [128, 128]); din('c_cos', [128, 1024]); din('c_sin', [128, 1024])
    din('c_mprev', [128, 128]); din('c_mnext', [128, 128]); din('c_colok', [128, 64]); din('c_colneg', [128, 64])
    dout('yP', [1024, 1024]); dout('yS', [1024, 1024])
    dout('ndk', [4, 8, 256, 64]); dout('ndv', [4, 4, 256, 128])
    dout('nnk', [4, 8, 256, 64]); dout('nnv', [4, 8, 256, 64])
    dout('nsk', [4, 2, 256, 64]); dout('nsv', [4, 2, 256, 64])

    es = ExitStack()

    def sb(name, free, dt):
        return TT(es.enter_context(nc.sbuf_tensor('s_' + name, [128, free], dt)), free)

    with es:
        xT = sb('xT', 8192, F32)
        hT = sb('hT', 8192, BF16)
        yb = sb('yb', 8192, BF16)
        wr = sb('wr', 2 * 8192, BF16)
        ar = sb('ar', 32768, BF16)
        bank = sb('bank', 7168, BF16)
        tmp = sb('tmp', 6 * 512, F32)
        Et = sb('Et', 4 * 512, BF16)
        sqt = sb('sqt', 4 * 512, BF16)
        rstd = sb('rstd', 2 * 512, F32)
        ident = sb('ident', 128, F32)
        ones = sb('ones', 128, BF16)
        R2 = sb('R2', 128, BF16)
        cosT = sb('cosT', 1024, BF16)
        sinT = sb('sinT', 1024, BF16)
        mprev = sb('mprev', 128, BF16)
        mnext = sb('mnext', 128, BF16)
        colok = sb('colok', 64, F32)
        colneg = sb('colneg', 64, F32)
        identb = sb('identb', 128, BF16)
        cT = sb('cT', 16, F32)
        sT = sb('sT', 16, BF16)
        modb = sb('modb', 96, F32)
        modv = sb('modv', 192, F32)
        gains = sb('gains', 64, F32)
        coef = sb('coef', 128, F32)
        cw = sb('cw', 12, F32)
        esink = sb('esink', 8, F32)
        lamt = sb('lamt', 256, F32)
        lams = sb('lams', 8, F32)
        subln = sb('subln', 2, F32)
        ps_all = TT(es.enter_context(nc.psum_tensor('ps_all', [128, 4096], F32)), 4096)

        P = Prog(nc)

        def MM(out, lhsT, rhs, start, stop, reads, writes, skip=False):
            if skip:
                P.op('pe', lambda e: e.matmul(out, lhsT, rhs, start=start, stop=stop, skip_group_check=True), reads=reads, writes=writes)
            else:
                P.op('pe', lambda e: e.matmul(out, lhsT, rhs, start=start, stop=stop), reads=reads, writes=writes)

        def TR(out, in_, reads, writes):
            P.op('pe', lambda e: e.transpose(out, in_, ident.ap(0, [[1, 128]])), reads=list(reads) + ['ident'], writes=writes)

        def ACTV(out, in_, func, reads, writes, bias=None, scale=None):
            kw = {}
            if bias is not None:
                kw['bias'] = bias
            if scale is not None:
                kw['scale'] = scale
            P.op('act', lambda e: e.activation(out, in_, func, **kw), reads=reads, writes=writes)

        def TTO(eng, out, in0, in1, op, reads, writes):
            P.op(eng, lambda e: e.tensor_tensor(out=out, in0=in0, in1=in1, op=op), reads=reads, writes=writes)

        def STT(eng, out, in0, scalar, in1, op0, op1, reads, writes):
            P.op(eng, lambda e: e.scalar_tensor_tensor(out=out, in0=in0, scalar=scalar, in1=in1, op0=op0, op1=op1), reads=reads, writes=writes)

        def TS(eng, out, in0, s1, s2, op0, op1, reads, writes):
            if s2 is None:
                P.op(eng, lambda e: e.tensor_scalar(out=out, in0=in0, scalar1=s1, scalar2=None, op0=op0), reads=reads, writes=writes)
            else:
                P.op(eng, lambda e: e.tensor_scalar(out=out, in0=in0, scalar1=s1, scalar2=s2, op0=op0, op1=op1), reads=reads, writes=writes)

        def CP(eng, out, in_, reads, writes):
            if eng == 'act':
                P.op('act', lambda e: e.activation(out, in_, AF.Copy), reads=reads, writes=writes)
            else:
                P.op(eng, lambda e: e.tensor_copy(out, in_), reads=reads, writes=writes)

        def RECIP(out, in_, reads, writes):
            P.op('dve', lambda e: e.reciprocal(out, in_), reads=reads, writes=writes)

        def DMA(q, out, in_, reads, writes, semkey=None, slow=False):
            if slow:
                return P.op(q, lambda e: e.dma_start(out=out, in_=in_, allow_slow_non_contiguous=True), reads=reads, writes=writes, dma=True, semkey=semkey)
            return P.op(q, lambda e: e.dma_start(out=out, in_=in_), reads=reads, writes=writes, dma=True, semkey=semkey)

        def psap(b, off, dims, p0=0, pn=128):
            return ps_all.ap(b * 512 + off, dims, p0, pn)

        def bankA():
            return P.nxt('bA', 4, 0)

        def bankB():
            return P.nxt('bB', 4, 4)

        def bankSCpair():
            return P.nxt('bSCp', 2) * 2

        def bankACC():
            return P.nxt('bACC', 4, 4)

        def tmpi():
            return P.nxt('tmp', 6)

        def Ei():
            return P.nxt('E', 4)

        def sqi():
            return P.nxt('sq', 4)

        def evq():
            return ['act', 'dve'][P.nxt('evq', 2)]

        out_ops = []
        cmap = {}
        marks = []

        def mark(name):
            marks.append((name, sum(1 for o in P.ops if o['eng'] == 'pe')))

        def setup():
            DMA('sp', ident.ap(0, [[1, 128]]), D['c_ident'].ap(), [], ['ident'])
            P.op('dve', lambda e: e.memset(ones.ap(0, [[1, 128]]), 1.0), writes=['ones'])
            DMA('pool', R2.ap(0, [[1, 128]]), D['c_R2'].ap(), [], ['R2'])
            DMA('pool', cosT.ap(0, [[1, 1024]]), D['c_cos'].ap(), [], ['cos'])
            DMA('pool', sinT.ap(0, [[1, 1024]]), D['c_sin'].ap(), [], ['sin'])
            DMA('pool', mprev.ap(0, [[1, 128]]), D['c_mprev'].ap(), [], ['mprev'])
            DMA('pool', mnext.ap(0, [[1, 128]]), D['c_mnext'].ap(), [], ['mnext'])
            DMA('sp', colok.ap(0, [[1, 64]]), D['c_colok'].ap(), [], ['colok'])
            DMA('sp', colneg.ap(0, [[1, 64]]), D['c_colneg'].ap(), [], ['colneg'])
            CP('dve', identb.ap(0, [[1, 128]]), ident.ap(0, [[1, 128]]), ['ident'], ['identb'])
            for j in range(2):
                DMA('sp', cT.ap(j, [[2, 8]]), bass.AP(D['cvec'], j * 1024, [[1, 128], [128, 8]]), [], ['cT'], slow=True)
                DMA('sp', modb.ap(j * 48, [[1, 48]]), bass.AP(D['mod_b'], j * 6144, [[1, 128], [128, 48]]), [], ['modb'], slow=True)
            for w in range(4):
                DMA('sp', gains.ap(w * 16, [[1, 16]]), bass.AP(D['gains'], w * 2048, [[1, 128], [128, 16]]), [], ['gains'], slow=True)
            for j in range(3):
                DMA('sp', cw.ap(j, [[3, 4]]), bass.AP(D['conv_w'], j * 512, [[1, 128], [128, 4]]), [], ['cw'], slow=True)
            DMA('sp', subln.ap(0, [[1, 1]]), D['subln'].ap(), [], ['subln0'])
            DMA('sp', esink.ap(0, [[1, 8]]), bass.AP(D['sink'], 0, [[0, 128], [1, 8]]), [], ['esink0'])
            DMA('sp', lamt.ap(0, [[1, 256]]), bass.AP(D['lams'], 0, [[0, 128], [1, 256]]), [], ['lamt'])
            ACTV(esink.ap(0, [[1, 8]]), esink.ap(0, [[1, 8]]), AF.Exp, ['esink0'], ['esink'])
            ACTV(sT.ap(0, [[1, 16]]), cT.ap(0, [[1, 16]]), AF.Silu, ['cT'], ['sT'])
            TTO('dve', lamt.ap(0, [[1, 64]]), lamt.ap(0, [[1, 64]]), lamt.ap(64, [[1, 64]]), ALU.mult, ['lamt'], ['lamt1'])
            TTO('dve', lamt.ap(128, [[1, 64]]), lamt.ap(128, [[1, 64]]), lamt.ap(192, [[1, 64]]), ALU.mult, ['lamt'], ['lamt2'])
            P.op('dve', lambda e: e.reduce_sum(lams.ap(0, [[1, 1]]), lamt.ap(0, [[1, 64]]), axis=AX.X), reads=['lamt1'], writes=['lams0'])
            P.op('dve', lambda e: e.reduce_sum(lams.ap(1, [[1, 1]]), lamt.ap(128, [[1, 64]]), axis=AX.X), reads=['lamt2'], writes=['lams1'])
            ACTV(lams.ap(2, [[1, 2]]), lams.ap(0, [[1, 2]]), AF.Exp, ['lams0', 'lams1'], ['lams2'])
            TTO('dve', lams.ap(4, [[1, 1]]), lams.ap(3, [[1, 1]]), lams.ap(2, [[1, 1]]), ALU.subtract, ['lams2'], ['lams4'])
            TS('dve', lams.ap(4, [[1, 1]]), lams.ap(4, [[1, 1]]), -0.2, None, ALU.add, None, ['lams4'], ['neglam'])
            TS('dve', subln.ap(1, [[1, 1]]), subln.ap(0, [[1, 1]]), 0.8, None, ALU.mult, None, ['subln0'], ['subln'])
            for hp in range(4):
                DMA('sp', tmp.ap(0, [[1, 1792]]), D['rpbT'].ap()[:, hp * 1792:(hp + 1) * 1792], [], [('tmp', i) for i in range(4)])
                ACTV(tmp.ap(0, [[1, 1792]]), tmp.ap(0, [[1, 1792]]), AF.Copy, [('tmp', i) for i in range(4)], [('tmp', i) for i in range(4)], scale=8.0)
                TTO('dve', tmp.ap(0, [[64, 28], [1, 64]]), tmp.ap(0, [[64, 28], [1, 64]]), colok.ap(0, [[0, 28], [1, 64]]), ALU.mult,
                    [('tmp', i) for i in range(4)] + ['colok'], [('tmp', i) for i in range(4)])
                TTO('dve', bank.ap(hp * 1792, [[64, 28], [1, 64]]), tmp.ap(0, [[64, 28], [1, 64]]), colneg.ap(0, [[0, 28], [1, 64]]), ALU.add,
                    [('tmp', i) for i in range(4)] + ['colneg'], ['bank'])

        class WV:
            def __init__(self, slot, off, stride):
                self.slot = slot
                self.base = slot * 8192 + off
                self.stride = stride
                self.key = ('w', slot)

        def wload(src_ap, dims, extra=None, stride=512):
            sl = P.nxt('wslot', 2)
            DMA('pool', wr.ap(sl * 8192, dims), src_ap, [], [('w', sl)], semkey='w%d' % sl)
            if extra:
                for (off, dims2, src2) in extra:
                    DMA('pool', wr.ap(sl * 8192 + off, dims2), src2, [], [('w', sl)], semkey='w%d' % sl)
            return WV(sl, 0, stride)

        wcache = {}

        def wpiece(name, row0, col0):
            c1 = (col0 // 1024) * 1024
            key = (name, row0, c1)
            if wcache.get(name, (None, None))[0] != key:
                src = D[name].ap()[row0:row0 + 1024, c1:c1 + 1024].rearrange("(kc p) c -> p kc c", p=128)
                wcache[name] = (key, wload(src, [[1024, 8], [1, 1024]], stride=1024))
            wv = wcache[name][1]
            return WV(wv.slot, col0 - c1, 1024)

        def mod_piece(li, pc):
            b = bankA()
            s = wpiece('mod_w', li * 1024, pc * 512)
            for mc in range(4):
                for kc in range(8):
                    MM(psap(b, mc * 2, [[1, 2]]), wr.ap(s.base + kc * s.stride + mc * 128, [[1, 128]]), sT.ap(kc * 2, [[1, 2]]),
                       kc == 0, kc == 7, [s.key, 'sT'], [('ps', b)])
            TTO('dve', modv.ap(li * 96 + pc * 8, [[2, 4], [1, 2]]), psap(b, 0, [[2, 4], [1, 2]]), modb.ap(li * 48 + pc * 4, [[1, 4], [0, 2]]), ALU.add,
                [('ps', b), 'modb'], [('modv', li)])
            if pc == 3:
                mod_finish(li, 0)
            if pc == 11:
                mod_finish(li, 1)

        def compute_mod(li):
            mark('mod l%d' % li)
            for pc in range(4):
                mod_piece(li, pc)

        modq = []

        def mod_more(n=1):
            for _ in range(n):
                if modq:
                    mod_piece(0, modq.pop(0))

        def mod_finish(li, part):
            def mv(c0):
                return modv.ap(li * 96 + c0 * 2, [[2, 8], [1, 2]])

            def gn(w):
                return gains.ap(w * 16 + li * 8, [[1, 8], [0, 2]])

            def cf(w):
                return coef.ap(li * 64 + w * 16, [[2, 8], [1, 2]])
            if part == 0:
                STT('dve', cf(0), mv(8), 1.0, gn(0), ALU.add, ALU.mult, [('modv', li), 'gains'], [('coef', li)])
                return
            TTO('dve', cf(1), mv(16), gn(1), ALU.mult, [('modv', li), 'gains'], [('coef', li)])
            STT('dve', cf(2), mv(32), 1.0, gn(2), ALU.add, ALU.mult, [('modv', li), 'gains'], [('coef', li)])
            TTO('dve', cf(3), mv(40), gn(3), ALU.mult, [('modv', li), 'gains'], [('coef', li)])

        def coefap(li, w, kc, j):
            return coef.ap(li * 64 + w * 16 + kc * 2 + j, [[1, 1]])

        def shap(li, which, kc, j):
            c0 = 0 if which == 1 else 24
            return modv.ap(li * 96 + (c0 + kc) * 2 + j, [[1, 1]])

        def load_x(u):
            mark('load_x u%d' % u)
            xd = D['xP'] if u == 0 else D['xS']
            for t in range(8):
                g = t // 4
                for hf in range(2):
                    ti = tmpi()
                    DMA('sp', tmp.ap(ti * 512, [[1, 512]]), xd.ap()[t * 128:(t + 1) * 128, hf * 512:(hf + 1) * 512], [], [('tmp', ti)])
                    b = bankA()
                    for c in range(4):
                        TR(psap(b, c * 128, [[1, 128]]), tmp.ap(ti * 512 + c * 128, [[1, 128]]), [('tmp', ti)], [('ps', b)])
                    CP(['dve', 'act'][g], xT.ap((hf * 4) * 1024 + t * 128, [[1024, 4], [1, 128]]), psap(b, 0, [[128, 4], [1, 128]]), [('ps', b)],
                       [('x', hf * 4 + c, g) for c in range(4)])

        def store_x(u):
            mark('store_x u%d' % u)
            yd = D['yP'] if u == 0 else D['yS']
            for t in range(8):
                if t == 4:
                    run_tail()
                g = t // 4
                for hf in range(2):
                    b = bankA()
                    for c in range(4):
                        TR(psap(b, c * 128, [[1, 128]]), xT.ap((hf * 4 + c) * 1024 + t * 128, [[1, 128]]), [('x', hf * 4 + c, g)], [('ps', b)])
                    ti = tmpi()
                    CP(evq(), tmp.ap(ti * 512, [[1, 512]]), psap(b, 0, [[1, 512]]), [('ps', b)], [('tmp', ti)])
                    out_ops.append(DMA('sp', yd.ap()[t * 128:(t + 1) * 128, hf * 512:(hf + 1) * 512], tmp.ap(ti * 512, [[1, 512]]), [('tmp', ti)], []))

        def rstd_from(b, ri, scale):
            ACTV(rstd.ap(ri * 512, [[1, 512]]), psap(b, 0, [[1, 512]]), AF.Ln, [('ps', b), 'epsb'], [('rstd', ri)], bias=cmap['eps'], scale=scale)
            ACTV(rstd.ap(ri * 512, [[1, 512]]), rstd.ap(ri * 512, [[1, 512]]), AF.Exp, [('rstd', ri)], [('rstd', ri)], scale=-0.5)

        tailq = []

        def run_tail():
            while tailq:
                tailq.pop(0)()

        def norm_mod(u, li, which, between=None):
            mark('norm u%d l%d w%d' % (u, li, which))
            j = u
            w = 0 if which == 1 else 2
            for g in range(2):
                b = bankB()
                for kc in range(8):
                    si = sqi()
                    ACTV(sqt.ap(si * 512, [[1, 512]]), xT.ap(kc * 1024 + g * 512, [[1, 512]]), AF.Square, [('x', kc, g)], [('sq', si)])
                    MM(psap(b, 0, [[1, 512]]), ones.ap(0, [[1, 128]]), sqt.ap(si * 512, [[1, 512]]), kc == 0, kc == 7, [('sq', si), 'ones'], [('ps', b)])
                ri = P.nxt('rstd', 2)
                rstd_from(b, ri, 1.0 / 1024)
                for kc in range(8):
                    ti = tmpi()
                    STT('dve', tmp.ap(ti * 512, [[1, 512]]), xT.ap(kc * 1024 + g * 512, [[1, 512]]), coefap(li, w, kc, j), rstd.ap(ri * 512, [[1, 512]]),
                        ALU.mult, ALU.mult, [('x', kc, g), ('coef', li), ('rstd', ri)], [('tmp', ti)])
                    ACTV(hT.ap(kc * 1024 + g * 512, [[1, 512]]), tmp.ap(ti * 512, [[1, 512]]), AF.Identity, [('tmp', ti), ('modv', li)], [('h', kc, g)],
                         bias=shap(li, which, kc, j), scale=1.0)
                if g == 0:
                    run_tail()
                    if between is not None:
                        between()

        def dense_fm(s, mc, g, evac, kcn=8, wstride=512, src=None, srckey='h'):
            b = bankA()
            for kc in range(kcn):
                MM(psap(b, 0, [[1, 512]]), wr.ap(s.base + kc * s.stride + mc * 128, [[1, 128]]),
                   (src or hT).ap(kc * 1024 + g * 512, [[1, 512]]), kc == 0, kc == kcn - 1, [s.key, (srckey, kc, g)], [('ps', b)])
            evac(b)

        def dense_tm(s, col0, ncols, t, evac, wstride=512):
            b = bankA()
            g = t // 4
            for kc in range(8):
                MM(psap(b, 0, [[1, ncols]]), hT.ap(kc * 1024 + t * 128, [[1, 128]]), wr.ap(s.base + kc * s.stride + col0, [[1, ncols]]),
                   kc == 0, kc == 7, [s.key, ('h', kc, g)], [('ps', b)])
            evac(b)

        QT, KT = 0, 4096
        VT0, AB, UU = 9216, 14336, 18432
        Q2, K2, VT1, VT2, VD = 9216, 13312, 15872, 20992, 24576

        def rope_evac(b, g, dst_off, dst_keys):
            if 'rope' in SKIP:
                CP(evq(), ar.ap(dst_off, [[1, 512]]), psap(b, 0, [[1, 512]]), [('ps', b)], dst_keys)
                return
            ei = Ei()
            CP('act', Et.ap(ei * 512, [[1, 512]]), psap(b, 0, [[1, 512]]), [('ps', b)], [('E', ei)])
            b2 = bankA()
            MM(psap(b2, 0, [[1, 512]]), R2.ap(0, [[1, 128]]), Et.ap(ei * 512, [[1, 512]]), True, True, [('E', ei), 'R2'], [('ps', b2)])
            t1 = tmpi()
            TTO('dve', tmp.ap(t1 * 512, [[1, 512]]), psap(b, 0, [[1, 512]]), cosT.ap(g * 512, [[1, 512]]), ALU.mult, [('ps', b), 'cos'], [('tmp', t1)])
            t2 = tmpi()
            TTO('dve', tmp.ap(t2 * 512, [[1, 512]]), psap(b2, 0, [[1, 512]]), sinT.ap(g * 512, [[1, 512]]), ALU.mult, [('ps', b2), 'sin'], [('tmp', t2)])
            TTO('dve', ar.ap(dst_off, [[1, 512]]), tmp.ap(t1 * 512, [[1, 512]]), tmp.ap(t2 * 512, [[1, 512]]), ALU.add, [('tmp', t1), ('tmp', t2)], dst_keys)

        def load_ctx_k(name, nh, dst_off, dst_stride, nchunk, keyname, dup=False):
            for tt in range(2):
                ti = tmpi()
                if not dup:
                    DMA('sp', tmp.ap(ti * 512, [[64, nh], [1, 64]]), bass.AP(D[name], tt * 8192, [[64, 128], [16384, nh], [1, 64]]), [], [('tmp', ti)])
                else:
                    for dpl in range(2):
                        DMA('sp', tmp.ap(ti * 512 + dpl * 64, [[128, nh], [1, 64]]), bass.AP(D[name], tt * 8192, [[64, 128], [16384, nh], [1, 64]]), [], [('tmp', ti)])
                b = bankA()
                for c in range(nchunk):
                    TR(psap(b, c * 128, [[1, 128]]), tmp.ap(ti * 512 + c * 128, [[1, 128]]), [('tmp', ti)], [('ps', b)])
                CP('dve', ar.ap(dst_off + 1024 + tt * 128, [[dst_stride, nchunk], [1, 128]]), psap(b, 0, [[128, nchunk], [1, 128]]), [('ps', b)],
                   [(keyname, c, 'ctx') for c in range(nchunk)])

        class Pipe:
            DEPTH = 2

            def __init__(self):
                self.round = []
                self.pending = []
                self.late_prev = []
                self.late_new = []

            def add(self, item):
                if self.round and (self.round[0].get('sb') is None) != (item.get('sb') is None):
                    self.emit_round()
                self.round.append(item)
                if len(self.round) == 2:
                    self.emit_round()

            def emit_round(self):
                items = self.round
                self.round = []
                if not items:
                    return
                er = P.nxt('Er', 6)
                ebase = er * 1024
                ekey = [('yb', er // 4, (er % 4) * 2), ('yb', er // 4, (er % 4) * 2 + 1)]
                paired = items[0].get('sb') is not None
                bx = bankSCpair()
                by = bx + 1
                if paired:
                    offs = []
                    o = 0
                    for it in items:
                        offs.append(o)
                        o += it['nq']
                    tot = o
                    for k, (it, of) in enumerate(zip(items, offs)):
                        it['sa'](bx, of, k == 0)
                    for k, (it, of) in enumerate(zip(items, offs)):
                        it['sb'](by, of, k == 0)
                    for it, of in zip(items, offs):
                        if it.get('bias'):
                            it['bias'](bx, by, of)
                    ACTV(yb.ap(ebase, [[512, 2], [1, tot]]), psap(bx, 0, [[512, 2], [1, tot]]), AF.Exp, [('ps', bx), ('ps', by)], ekey, scale=0.125)
                    for it, of in zip(items, offs):
                        it['ea'] = ebase + of
                        it['eb'] = ebase + 512 + of
                else:
                    banks = [bx, by]
                    for k, it in enumerate(items):
                        it['sa'](banks[k], 0, True)
                    if len(items) == 2 and items[0]['nq'] == 512 and items[1]['nq'] == 512:
                        ACTV(yb.ap(ebase, [[1, 1024]]), psap(bx, 0, [[1, 1024]]), AF.Exp, [('ps', bx), ('ps', by)], ekey, scale=0.125)
                    else:
                        for k, it in enumerate(items):
                            ACTV(yb.ap(ebase + k * 512, [[1, it['nq']]]), psap(banks[k], 0, [[1, it['nq']]]), AF.Exp, [('ps', banks[k])], ekey, scale=0.125)
                    for k, it in enumerate(items):
                        it['ea'] = ebase + k * 512
                        it['eb'] = None

                def pend(items=items, ekey=ekey):
                    for it in items:
                        it['post'](it['ea'], it['eb'], ekey)
                self.pending.append(pend)
                while len(self.pending) > self.DEPTH:
                    self.pending.pop(0)()
                for f in self.late_prev:
                    f()
                self.late_prev = self.late_new
                self.late_new = []

            def flush(self):
                self.emit_round()
                while self.pending:
                    self.pending.pop(0)()
                    for f in self.late_prev:
                        f()
                    self.late_prev = self.late_new
                    self.late_new = []
                for f in self.late_prev + self.late_new:
                    f()
                self.late_prev = []
                self.late_new = []

        pipe = Pipe()

        def attn_pair(ktiles, qa, qb, qkeys, nq, dst_off, dst_keys, sink_h=None):
            bo = bankACC()
            bd = bankACC()
            nk = len(ktiles)

            def final():
                ti = tmpi()
                if sink_h is not None:
                    ACTV(tmp.ap(ti * 512, [[1, nq]]), psap(bd, 0, [[1, nq]]), AF.Ln, [('ps', bd), 'esink'], [('tmp', ti)], bias=esink.ap(sink_h, [[1, 1]]), scale=1.0)
                    ACTV(tmp.ap(ti * 512 + nq, [[1, nq]]), psap(bd, nq, [[1, nq]]), AF.Ln, [('ps', bd), 'esink'], [('tmp', ti)], bias=esink.ap(sink_h + 1, [[1, 1]]), scale=1.0)
                else:
                    ACTV(tmp.ap(ti * 512, [[1, 2 * nq]]), psap(bd, 0, [[1, 2 * nq]]), AF.Ln, [('ps', bd)], [('tmp', ti)])
                ACTV(tmp.ap(ti * 512, [[1, 2 * nq]]), tmp.ap(ti * 512, [[1, 2 * nq]]), AF.Exp, [('tmp', ti)], [('tmp', ti)], scale=-1.0)
                TTO('dve', hT.ap(dst_off, [[1, nq]], 0, 64), psap(bo, 0, [[1, nq]], 0, 64), tmp.ap(ti * 512, [[1, nq]], 0, 64), ALU.mult, [('ps', bo), ('tmp', ti)], dst_keys)
                TTO('dve', hT.ap(dst_off, [[1, nq]], 64, 64), psap(bo, nq, [[1, nq]], 64, 64), tmp.ap(ti * 512 + nq, [[1, nq]], 64, 64), ALU.mult, [('ps', bo), ('tmp', ti)], dst_keys)

            for i, kt in enumerate(ktiles):
                hasb = kt.get('ma') is not None

                def sa(b, of, first, kt=kt, hasb=hasb):
                    MM(psap(b, of, [[1, nq]]), kt['ka'], qa, first, not hasb, kt['kkeys'] + qkeys, [('ps', b)], skip=True)

                def sb(b, of, first, kt=kt, hasb=hasb):
                    MM(psap(b, of, [[1, nq]]), kt['kb'], qb, first, not hasb, kt['kkeys'] + qkeys, [('ps', b)], skip=True)

                bias = None
                if hasb:
                    def bias(bx, by, of, kt=kt):
                        MM(psap(bx, of, [[1, nq]]), identb.ap(0, [[1, 128]]), kt['ma'], False, True, kt['mkeys'] + ['identb'], [('ps', bx)], skip=True)
                        MM(psap(by, of, [[1, nq]]), identb.ap(0, [[1, 128]]), kt['mb'], False, True, kt['mkeys'] + ['identb'], [('ps', by)], skip=True)

                def post(ea, eb, ekey, i=i, kt=kt):
                    MM(psap(bo, 0, [[1, nq]]), kt['v'], yb.ap(ea, [[1, nq]]), i == 0, False, ekey + kt['vkeys'], [('ps', bo)], skip=True)
                    MM(psap(bo, nq, [[1, nq]]), kt['v'], yb.ap(eb, [[1, nq]]), False, i == nk - 1, ekey + kt['vkeys'], [('ps', bo)], skip=True)
                    MM(psap(bd, 0, [[nq, 2], [1, nq]]), ones.ap(0, [[1, 128]]), yb.ap(ea, [[eb - ea, 2], [1, nq]]), i == 0, i == nk - 1, ekey + ['ones'], [('ps', bd)])
                    if i == nk - 1:
                        final()
                pipe.add(dict(sa=sa, sb=sb, bias=bias, post=post, nq=nq))

        def diff_attn(u):
            mark('diff_attn u%d' % u)
            if u == 0:
                blocks = [(s * 256, 256, [s * 2, s * 2 + 1]) for s in range(4)]
            else:
                blocks = [(0, 512, list(range(10))), (512, 512, list(range(10)))]
            nblk = 0
            for h in range(4):
                for (q0, nq, kts) in blocks:
                    if u == 0:
                        if modq:
                            mod_more()
                        elif nblk - 4 < 12:
                            mod_piece(1, nblk - 4)
                    nblk += 1
                    g = q0 // 512
                    packed = (nq == 256)
                    ncols = 2 * nq if packed else nq
                    accs = []

                    def sublnfin(accs=accs, h=h, q0=q0, nq=nq, g=g, packed=packed):
                        tdk = P.nxt('tdslot', 2)
                        tdoff = (6 + tdk) * 1024
                        tdkey = [('yb', 1, 4 + 2 * tdk), ('yb', 1, 5 + 2 * tdk)]
                        if packed:
                            o0 = tmp.ap(accs[0] * 512, [[1, nq]]); o1 = tmp.ap(accs[0] * 512 + nq, [[1, nq]]); rk = [('tmp', accs[0])]
                        else:
                            o0 = tmp.ap(accs[0] * 512, [[1, nq]]); o1 = tmp.ap(accs[1] * 512, [[1, nq]]); rk = [('tmp', accs[0]), ('tmp', accs[1])]
                        STT('dve', yb.ap(tdoff, [[1, nq]]), o1, lams.ap(4, [[1, 1]]), o0, ALU.mult, ALU.add, rk + ['neglam'], tdkey)
                        si = sqi()
                        ACTV(sqt.ap(si * 512, [[1, nq]]), yb.ap(tdoff, [[1, nq]]), AF.Square, tdkey, [('sq', si)])

                        def stage2(si=si, tdoff=tdoff, tdkey=tdkey):
                            bs = bankACC()
                            MM(psap(bs, 0, [[1, nq]]), ones.ap(0, [[1, 128]]), sqt.ap(si * 512, [[1, nq]]), True, True, [('sq', si), 'ones'], [('ps', bs)])
                            ri = P.nxt('rstd', 2)
                            ACTV(rstd.ap(ri * 512, [[1, nq]]), psap(bs, 0, [[1, nq]]), AF.Ln, [('ps', bs), 'epsb'], [('rstd', ri)], bias=cmap['eps'], scale=1.0 / 128)
                            ACTV(rstd.ap(ri * 512, [[1, nq]]), rstd.ap(ri * 512, [[1, nq]]), AF.Exp, [('rstd', ri)], [('rstd', ri)], scale=-0.5)
                            STT('dve', hT.ap((4 + h) * 1024 + q0, [[1, nq]]), yb.ap(tdoff, [[1, nq]]), subln.ap(1, [[1, 1]]), rstd.ap(ri * 512, [[1, nq]]), ALU.mult, ALU.mult,
                                tdkey + [('rstd', ri), 'subln'], [('h', 4 + h, g)])
                        pipe.late_new.append(stage2)

                    maps = [None] if packed else [0, 1]
                    for m in maps:
                        bo = bankACC()
                        bd = bankACC()
                        nk = len(kts)
                        for i, kt in enumerate(kts):
                            kcol = kt * 128 if kt < 8 else 1024 + (kt - 8) * 128
                            kkey = [('k', h, kt // 4 if kt < 8 else 'ctx')]
                            vkey = [('v', kt)]

                            def smm(b, of, mm_, kcol=kcol, kkey=kkey, h=h, q0=q0, nq=nq, g=g):
                                MM(psap(b, of, [[1, nq]]), ar.ap(KT + h * 1280 + kcol, [[1, 128]], mm_ * 64, 64), ar.ap(QT + h * 1024 + q0, [[1, nq]], mm_ * 64, 64),
                                   True, True, kkey + [('q', h, g)], [('ps', b)])
                            if packed:
                                sa = lambda b, of, first, smm=smm: smm(b, of, 0)
                                sb = lambda b, of, first, smm=smm: smm(b, of, 1)
                            else:
                                sa = lambda b, of, first, smm=smm, m=m: smm(b, of, m)
                                sb = None

                            def post(ea, eb, ekey, i=i, kt=kt, vkey=vkey, bo=bo, bd=bd, nk=nk, ncols=ncols, nq=nq, h=h, m=m, maps=maps, accs=accs, sublnfin=sublnfin, packed=packed):
                                if packed:
                                    rhs = yb.ap(ea, [[eb - ea, 2], [1, nq]])
                                    oap = lambda bk: psap(bk, 0, [[nq, 2], [1, nq]])
                                else:
                                    rhs = yb.ap(ea, [[1, nq]])
                                    oap = lambda bk: psap(bk, 0, [[1, nq]])
                                MM(oap(bo), ar.ap(VT0 + kt * 512 + h * 128, [[1, 128]]), rhs, i == 0, i == nk - 1, ekey + vkey, [('ps', bo)])
                                MM(oap(bd), ones.ap(0, [[1, 128]]), rhs, i == 0, i == nk - 1, ekey + ['ones'], [('ps', bd)])
                                if i == nk - 1:
                                    tr = tmpi()
                                    ACTV(tmp.ap(tr * 512, [[1, ncols]]), psap(bd, 0, [[1, ncols]]), AF.Ln, [('ps', bd)], [('tmp', tr)])
                                    ACTV(tmp.ap(tr * 512, [[1, ncols]]), tmp.ap(tr * 512, [[1, ncols]]), AF.Exp, [('tmp', tr)], [('tmp', tr)], scale=-1.0)
                                    to = tmpi()
                                    TTO('dve', tmp.ap(to * 512, [[1, ncols]]), psap(bo, 0, [[1, ncols]]), tmp.ap(tr * 512, [[1, ncols]]), ALU.mult, [('ps', bo), ('tmp', tr)], [('tmp', to)])
                                    accs.append(to)
                                    if m == maps[-1]:
                                        sublnfin()
                            pipe.add(dict(sa=sa, sb=sb, post=post, nq=nq))

        deferred_pe = []

        def flush_deferred():
            while deferred_pe:
                deferred_pe.pop(0)()

        def post_evac(b, g, mo, bs):
            CP('act', yb.ap(g * 4096 + mo * 512, [[1, 512]]), psap(b, 0, [[1, 512]]), [('ps', b)], [('yb', g, mo)])
            si = sqi()
            ACTV(sqt.ap(si * 512, [[1, 512]]), psap(b, 0, [[1, 512]]), AF.Square, [('ps', b)], [('sq', si)])
            flush_deferred()
            deferred_pe.append(lambda: MM(psap(bs, 0, [[1, 512]]), ones.ap(0, [[1, 128]]), sqt.ap(si * 512, [[1, 512]]), mo == 0, mo == 7, [('sq', si), 'ones'], [('ps', bs)]))

        def post_update(u, li, w, g, bs):
            flush_deferred()
            ri = P.nxt('rstd', 2)
            ACTV(rstd.ap(ri * 512, [[1, 512]]), psap(bs, 0, [[1, 512]]), AF.Ln, [('ps', bs), 'epsb'], [('rstd', ri)], bias=cmap['eps'], scale=1.0 / 1024)
            ACTV(Et.ap(3 * 512, [[1, 512]]), rstd.ap(ri * 512, [[1, 512]]), AF.Exp, [('rstd', ri)], [('E', 3)], scale=-0.5)
            for mo in range(8):
                ti = P.nxt('Et3', 3)
                TTO('dve', Et.ap(ti * 512, [[1, 512]]), yb.ap(g * 4096 + mo * 512, [[1, 512]]), Et.ap(3 * 512, [[1, 512]]), ALU.mult, [('yb', g, mo), ('E', 3)], [('E', ti)])
                STT('dve', xT.ap(mo * 1024 + g * 512, [[1, 512]]), Et.ap(ti * 512, [[1, 512]]), coefap(li, w, mo, u), xT.ap(mo * 1024 + g * 512, [[1, 512]]), ALU.mult, ALU.add,
                    [('E', ti), ('coef', li), ('x', mo, g)], [('x', mo, g)])

        def wout_phase(u, li):
            mark('wout u%d l%d' % (u, li))
            s0 = wpiece('w_out', li * 1024, 0)
            s1 = wpiece('w_out', li * 1024, 512)
            for g in range(2):
                bs = bankB()
                for mo in range(8):
                    s = s0 if mo < 4 else s1
                    dense_fm(s, mo % 4, g, lambda b, g=g, mo=mo, bs=bs: post_evac(b, g, mo, bs))
                if g == 0:
                    post_update(u, li, 1, g, bs)
                else:
                    flush_deferred()
                    tailq.append(lambda bs=bs: post_update(u, li, 1, 1, bs))

        def mlp_phase(u, li):
            norm_mod(u, li, 2)
            mark('mlp1 u%d l%d' % (u, li))
            for pc in range(8):
                s = wpiece('mlp_w1', li * 1024, pc * 512)
                for g in range(2):
                    for mc in range(4):
                        m = pc * 4 + mc

                        def ev(b, m=m, g=g):
                            ei = Ei()
                            ACTV(Et.ap(ei * 512, [[1, 512]]), psap(b, 0, [[1, 512]]), AF.Relu, [('ps', b)], [('E', ei)])
                            TTO('dve', ar.ap(m * 1024 + g * 512, [[1, 512]]), Et.ap(ei * 512, [[1, 512]]), Et.ap(ei * 512, [[1, 512]]), ALU.mult, [('E', ei)], [('aT', m, g)])
                        dense_fm(s, mc, g, ev)
            mark('mlp2 u%d l%d' % (u, li))
            bss = [bankB(), bankB()]
            for ld in range(4):
                src = D['mlp_w2'].ap()[li * 4096:(li + 1) * 4096, ld * 256:(ld + 1) * 256].rearrange("(kc p) c -> p kc c", p=128)
                s2 = wload(src, [[256, 32], [1, 256]], stride=256)
                for g in range(2):
                    for mo in (2 * ld, 2 * ld + 1):
                        s = WV(s2.slot, (mo % 2) * 128, 256)
                        dense_fm(s, 0, g, lambda b, g=g, mo=mo: post_evac(b, g, mo, bss[g]), kcn=32, wstride=128, src=ar, srckey='aT')
                    if ld == 3 and g == 0:
                        post_update(u, li, 3, 0, bss[0])
            flush_deferred()
            tailq.append(lambda: post_update(u, li, 3, 1, bss[1]))

        def layer0(u):
            def piece_ab(g):
                s = wpiece('w_in_even', 0, 0)
                for c in range(4):
                    dense_fm(s, c, g, lambda b, c=c, g=g: CP('act', ar.ap(AB + c * 1024 + g * 512, [[1, 512]]), psap(b, 0, [[1, 512]]), [('ps', b)], [('ab', c, g)]))

            def piece_ac(g):
                s = wpiece('w_in_even', 0, 512)
                for c in range(4):
                    dense_fm(s, c, g, lambda b, c=c, g=g: CP('act', ar.ap(UU + c * 1024 + g * 512, [[1, 512]]), psap(b, 0, [[1, 512]]), [('ps', b)], [('u', c, g)]))

            def first_g0():
                piece_ab(0)
                piece_ac(0)
            norm_mod(u, 0, 1, between=first_g0)
            fence = [('h', 0, 0)]
            piece_ab(1)
            mod_more()
            piece_ac(1)
            mod_more()
            s = wpiece('w_in_even', 0, 1024)
            for c in range(4):
                for g in range(2):
                    dense_fm(s, c, g, lambda b, c=c, g=g: TTO('dve', ar.ap(UU + c * 1024 + g * 512, [[1, 512]]), psap(b, 0, [[1, 512]]), ar.ap(UU + c * 1024 + g * 512, [[1, 512]]),
                                                               ALU.mult, [('ps', b), ('u', c, g)], [('u', c, g)]))
            mod_more()
            s = wpiece('w_in_even', 0, 1536)
            for c in range(4):
                for g in range(2):
                    if u == 0:
                        dense_fm(s, c, g, lambda b, c=c, g=g: CP(evq(), ar.ap(QT + c * 1024 + g * 512, [[1, 512]]), psap(b, 0, [[1, 512]]), [('ps', b)], [('q', c, g)]))
                    else:
                        dense_fm(s, c, g, lambda b, c=c, g=g: rope_evac(b, g, QT + c * 1024 + g * 512, [('q', c, g)]))
            mod_more()
            if u == 1 and 'ctx' not in SKIP:
                load_ctx_k('cdk', 8, KT, 1280, 4, 'k')
            s = wpiece('w_in_even', 0, 2048)
            for c in range(4):
                for g in range(2):
                    if u == 0:
                        dense_fm(s, c, g, lambda b, c=c, g=g: CP(evq(), ar.ap(KT + c * 1280 + g * 512, [[1, 512]]), psap(b, 0, [[1, 512]]), [('ps', b)], [('k', c, g)]))
                    else:
                        dense_fm(s, c, g, lambda b, c=c, g=g: rope_evac(b, g, KT + c * 1280 + g * 512, [('k', c, g)]))
            if u == 0 and 'ktm' not in SKIP:
                for t in range(8):
                    def ev(b, t=t):
                        ti = tmpi()
                        CP('dve', tmp.ap(ti * 512, [[1, 512]]), psap(b, 0, [[1, 512]]), [('ps', b)], [('tmp', ti)])
                        sq_, tt = t // 2, t % 2
                        out_ops.append(DMA('sp', bass.AP(D['ndk'], sq_ * 8 * 16384 + tt * 8192, [[64, 128], [16384, 8], [1, 64]]), tmp.ap(ti * 512, [[64, 8], [1, 64]]), [('tmp', ti)], []))
                    dense_tm(s, 0, 512, t, ev)
            s = wpiece('w_in_even', 0, 2560)
            for t in range(8):
                def ev(b, t=t):
                    CP('act', ar.ap(VT0 + t * 512, [[1, 512]]), psap(b, 0, [[1, 512]]), [('ps', b)], [('v', t)])
                    if u == 0 and 'vout' not in SKIP:
                        ti = tmpi()
                        CP('dve', tmp.ap(ti * 512, [[1, 512]]), psap(b, 0, [[1, 512]]), [('ps', b)], [('tmp', ti)])
                        sq_, tt = t // 2, t % 2
                        out_ops.append(DMA('sp', bass.AP(D['ndv'], sq_ * 4 * 32768 + tt * 16384, [[128, 128], [32768, 4], [1, 128]]), tmp.ap(ti * 512, [[128, 4], [1, 128]]), [('tmp', ti)], []))
                dense_tm(s, 0, 512, t, ev)
            mark('conv u%d' % u)
            nseq, sl = (4, 256) if u == 0 else (1, 1024)
            for c in range(4 if 'conv' not in SKIP else 0):
                ukeys = [('u', c, 0), ('u', c, 1)]
                t1 = tmpi(); t2 = tmpi()
                acc_keys = [('tmp', t1), ('tmp', t2)]
                acc = lambda off, n: tmp.ap(t1 * 512 + off, [[1, n]])
                if t2 != t1 + 1:
                    t1 = tmpi(); t2 = tmpi()
                    acc_keys = [('tmp', t1), ('tmp', t2)]
                base = t1 * 512
                TS('dve', tmp.ap(base, [[1, 1024]]), ar.ap(UU + c * 1024, [[1, 1024]]), cw.ap(c * 3 + 1, [[1, 1]]), None, ALU.mult, None, ukeys + ['cw'], acc_keys)
                STT('dve', tmp.ap(base + 1, [[sl, nseq], [1, sl - 1]]), ar.ap(UU + c * 1024, [[sl, nseq], [1, sl - 1]]), cw.ap(c * 3 + 0, [[1, 1]]),
                    tmp.ap(base + 1, [[sl, nseq], [1, sl - 1]]), ALU.mult, ALU.add, ukeys + ['cw'] + acc_keys, acc_keys)
                STT('dve', tmp.ap(base, [[sl, nseq], [1, sl - 1]]), ar.ap(UU + c * 1024 + 1, [[sl, nseq], [1, sl - 1]]), cw.ap(c * 3 + 2, [[1, 1]]),
                    tmp.ap(base, [[sl, nseq], [1, sl - 1]]), ALU.mult, ALU.add, ukeys + ['cw'] + acc_keys, acc_keys)
                TTO('dve', hT.ap(c * 1024, [[1, 1024]]), tmp.ap(base, [[1, 1024]]), ar.ap(AB + c * 1024, [[1, 1024]]), ALU.mult,
                    acc_keys + [('ab', c, 0), ('ab', c, 1)], [('h', c, 0), ('h', c, 1)])
            if u == 1 and 'ctx' not in SKIP:
                for tt in range(2):
                    DMA('pool', ar.ap(VT0 + (8 + tt) * 512, [[128, 4], [1, 128]]), bass.AP(D['cdv'], tt * 16384, [[128, 128], [32768, 4], [1, 128]]), fence, [('v', 8 + tt)])
            if stop_after == ('A', u):
                raise StopBuild('h')
            diff_attn(u)
            if stop_after == ('B', u):
                pipe.flush()
                raise StopBuild('h')
            pipe.flush()
            wout_phase(u, 0)
            if stop_after == ('C', u):
                raise StopBuild('x')
            mlp_phase(u, 0)

        def layer1(u):
            def piece_cq(g):
                s = wpiece('w_in_odd', 0, 0)
                for c in range(4):
                    dense_fm(s, c, g, lambda b, c=c, g=g: CP(evq(), ar.ap(QT + c * 1024 + g * 512, [[1, 512]]), psap(b, 0, [[1, 512]]), [('ps', b)], [('q', c, g)]))

            def piece_ck(g):
                s = wpiece('w_in_odd', 0, 512)
                for c in range(4):
                    dense_fm(s, c, g, lambda b, c=c, g=g: CP(evq(), ar.ap(KT + c * 1280 + g * 512, [[1, 512]]), psap(b, 0, [[1, 512]]), [('ps', b)], [('k', c, g)]))

            def first_g0():
                piece_cq(0)
                piece_ck(0)
            norm_mod(u, 1, 1, between=first_g0)
            fence = [('h', 0, 0)]
            piece_cq(1)
            piece_ck(1)
            s = wpiece('w_in_odd', 0, 512)
            if u == 1:
                load_ctx_k('cnk', 8, KT, 1280, 4, 'k')
                load_ctx_k('csk', 2, K2, 1280, 2, 'k2', dup=True)
            if u == 0:
                for t in range(8):
                    def ev(b, t=t):
                        ti = tmpi()
                        CP('dve', tmp.ap(ti * 512, [[1, 512]]), psap(b, 0, [[1, 512]]), [('ps', b)], [('tmp', ti)])
                        sq_, tt = t // 2, t % 2
                        out_ops.append(DMA('sp', bass.AP(D['nnk'], sq_ * 8 * 16384 + tt * 8192, [[64, 128], [16384, 8], [1, 64]]), tmp.ap(ti * 512, [[64, 8], [1, 64]]), [('tmp', ti)], []))
                    dense_tm(s, 0, 512, t, ev)
            s = wpiece('w_in_odd', 0, 1024)
            for t in range(8):
                def ev(b, t=t):
                    CP('act', ar.ap(VT1 + t * 512, [[1, 512]]), psap(b, 0, [[1, 512]]), [('ps', b)], [('v', t)])
                    if u == 0:
                        ti = tmpi()
                        CP('dve', tmp.ap(ti * 512, [[1, 512]]), psap(b, 0, [[1, 512]]), [('ps', b)], [('tmp', ti)])
                        sq_, tt = t // 2, t % 2
                        out_ops.append(DMA('sp', bass.AP(D['nnv'], sq_ * 8 * 16384 + tt * 8192, [[64, 128], [16384, 8], [1, 64]]), tmp.ap(ti * 512, [[64, 8], [1, 64]]), [('tmp', ti)], []))
                dense_tm(s, 0, 512, t, ev)
            if u == 1:
                for i in range(7):
                    b = bankA()
                    for kc in range(8):
                        gk = [('h', kc, 0), ('h', kc, 1)]
                        MM(psap(b, 0, [[1, 512]]), hT.ap(kc * 1024 + 64 + i * 128, [[1, 128]]), wr.ap(s.base + kc * s.stride, [[1, 512]]), kc == 0, kc == 7, [s.key] + gk, [('ps', b)])
                    CP(evq(), ar.ap(VT2 + i * 512, [[1, 512]]), psap(b, 0, [[1, 512]]), [('ps', b)], [('v2', i)])
            s = wpiece('w_in_odd', 0, 1536)
            for c in range(4):
                for g in range(2):
                    if u == 0:
                        dense_fm(s, c, g, lambda b, c=c, g=g: CP(evq(), ar.ap(Q2 + c * 1024 + g * 512, [[1, 512]]), psap(b, 0, [[1, 512]]), [('ps', b)], [('q2', c, g)]))
                    else:
                        dense_fm(s, c, g, lambda b, c=c, g=g: rope_evac(b, g, Q2 + c * 1024 + g * 512, [('q2', c, g)]))
            srcA = bass.AP(D['w_in_odd'], 2048, [[2304, 128], [2304 * 128, 8], [1, 256]])
            extra = []
            for g_ in range(2):
                for dpl in range(2):
                    extra.append((2048 + g_ * 128 + dpl * 64, [[256, 8], [1, 64]], bass.AP(D['w_in_odd'], 2048 + g_ * 64, [[2304, 128], [2304 * 128, 8], [1, 64]])))
            s = wload(srcA, [[256, 8], [1, 256]], extra=extra, stride=256)
            for gk_ in range(2):
                for g in range(2):
                    b = bankA()
                    for kc in range(8):
                        MM(psap(b, 0, [[1, 512]]), wr.ap(s.base + 2048 + kc * 256 + gk_ * 128, [[1, 128]]), hT.ap(kc * 1024 + g * 512, [[1, 512]]), kc == 0, kc == 7,
                           [s.key, ('h', kc, g)], [('ps', b)])
                    if u == 0:
                        CP(evq(), ar.ap(K2 + gk_ * 1280 + g * 512, [[1, 512]]), psap(b, 0, [[1, 512]]), [('ps', b)], [('k2', gk_, g)])
                    else:
                        rope_evac(b, g, K2 + gk_ * 1280 + g * 512, [('k2', gk_, g)])
            for t in range(8):
                def ev(b, t=t):
                    for dpl in range(2):
                        CP(['act', 'dve'][dpl], ar.ap(VD + t * 256 + dpl * 64, [[128, 2], [1, 64]]), psap(b, 128, [[64, 2], [1, 64]]), [('ps', b)], [('vd', t)])
                    if u == 0:
                        ti = tmpi()
                        CP('dve', tmp.ap(ti * 512, [[1, 256]]), psap(b, 0, [[1, 256]]), [('ps', b)], [('tmp', ti)])
                        sq_, tt = t // 2, t % 2
                        out_ops.append(DMA('sp', bass.AP(D['nsk'], sq_ * 2 * 16384 + tt * 8192, [[64, 128], [16384, 2], [1, 64]]), tmp.ap(ti * 512, [[64, 2], [1, 64]]), [('tmp', ti)], []))
                        out_ops.append(DMA('sp', bass.AP(D['nsv'], sq_ * 2 * 16384 + tt * 8192, [[64, 128], [16384, 2], [1, 64]]), tmp.ap(ti * 512 + 128, [[64, 2], [1, 64]]), [('tmp', ti)], []))
                dense_tm(s, 0, 256, t, ev, wstride=256)

            if u == 1:
                for tt in range(2):
                    DMA('pool', ar.ap(VT1 + (8 + tt) * 512, [[64, 8], [1, 64]]), bass.AP(D['cnv'], tt * 8192, [[64, 128], [16384, 8], [1, 64]]), fence, [('v', 8 + tt)])
                    for dpl in range(2):
                        DMA('pool', ar.ap(VD + (8 + tt) * 256 + dpl * 64, [[128, 2], [1, 64]]), bass.AP(D['csv'], tt * 8192, [[64, 128], [16384, 2], [1, 64]]), fence, [('vd', 8 + tt)])
            mark('attn1 u%d' % u)

            def kt_c(c, col, vtile_ap, vkeys, kkeys, ma=None, mb=None, mkeys=None):
                return dict(ka=ar.ap(KT + c * 1280 + col, [[1, 128]], 0, 64), kb=ar.ap(KT + c * 1280 + col, [[1, 128]], 64, 64), kkeys=kkeys,
                            v=vtile_ap, vkeys=vkeys, ma=ma, mb=mb, mkeys=mkeys or [])

            def kt_d(gk_, col, vt, kkeys, m=None, mkeys=None):
                return dict(ka=ar.ap(K2 + gk_ * 1280 + col, [[1, 128]], 0, 64), kb=ar.ap(K2 + gk_ * 1280 + col, [[1, 128]], 64, 64), kkeys=kkeys,
                            v=ar.ap(VD + vt * 256 + gk_ * 128, [[1, 128]]), vkeys=[('vd', vt)], ma=m, mb=m, mkeys=mkeys or [])

            if u == 0:
                for sq_ in range(4):
                    g = sq_ // 2
                    q0 = sq_ * 256
                    for c in range(4):
                        kts = [kt_c(c, q0 + kt * 128, ar.ap(VT1 + (sq_ * 2 + kt) * 512 + c * 128, [[1, 128]]), [('v', sq_ * 2 + kt)], [('k', c, g)]) for kt in range(2)]
                        attn_pair(kts, ar.ap(QT + c * 1024 + q0, [[1, 256]], 0, 64), ar.ap(QT + c * 1024 + q0, [[1, 256]], 64, 64), [('q', c, g)], 256,
                                  c * 1024 + q0, [('h', c, g)])
                    for c in range(4):
                        gk_ = c // 2
                        kts = [kt_d(gk_, q0 + kt * 128, sq_ * 2 + kt, [('k2', gk_, g)]) for kt in range(2)]
                        attn_pair(kts, ar.ap(Q2 + c * 1024 + q0, [[1, 256]], 0, 64), ar.ap(Q2 + c * 1024 + q0, [[1, 256]], 64, 64), [('q2', c, g)], 256,
                                  (4 + c) * 1024 + q0, [('h', 4 + c, g)], sink_h=2 * c)
            else:
                groups = [(0, 4, 0), (4, 1, 0)] + [(r, 1, r - 4) for r in range(5, 12)] + [(12, 4, 8)]
                for c in range(4):
                    for (r0, nr, rs) in groups:
                        nq = nr * 64
                        q0 = r0 * 64
                        g = q0 // 512
                        kts = []
                        for jj in range(4):
                            ka_ = rs + 2 * jj
                            col = ka_ * 64
                            if ka_ % 2 == 0:
                                vap = ar.ap(VT1 + (ka_ // 2) * 512 + c * 128, [[1, 128]]); vk = [('v', ka_ // 2)]
                            else:
                                vap = ar.ap(VT2 + ((ka_ - 1) // 2) * 512 + c * 128, [[1, 128]]); vk = [('v2', (ka_ - 1) // 2)]
                            i0 = 6 - (ka_ - r0)
                            assert 0 <= i0 and i0 + nr <= 14
                            ma = bank.ap((2 * c) * 896 + i0 * 64, [[1, nq]])
                            mb = bank.ap((2 * c + 1) * 896 + i0 * 64, [[1, nq]])
                            kk = [('k', c, (col // 512)), ('k', c, ((col + 127) // 512))]
                            kts.append(kt_c(c, col, vap, vk, kk, ma, mb, ['bank']))
                        for tt in range(2):
                            kts.append(kt_c(c, 1024 + tt * 128, ar.ap(VT1 + (8 + tt) * 512 + c * 128, [[1, 128]]), [('v', 8 + tt)], [('k', c, 'ctx')]))
                        attn_pair(kts, ar.ap(QT + c * 1024 + q0, [[1, nq]], 0, 64), ar.ap(QT + c * 1024 + q0, [[1, nq]], 64, 64), [('q', c, g)], nq,
                                  c * 1024 + q0, [('h', c, g)])
                mark('attn1D u%d' % u)
                for c in range(4):
                    gk_ = c // 2
                    for n in range(8):
                        g = n // 4
                        q0 = n * 128
                        kts = []
                        if n > 0:
                            kts.append(kt_d(gk_, (n - 1) * 128, n - 1, [('k2', gk_, (n - 1) // 4)], mprev.ap(0, [[1, 128]]), ['mprev']))
                        kts.append(kt_d(gk_, n * 128, n, [('k2', gk_, g)]))
                        if n < 7:
                            kts.append(kt_d(gk_, (n + 1) * 128, n + 1, [('k2', gk_, (n + 1) // 4)], mnext.ap(0, [[1, 128]]), ['mnext']))
                        for tt in range(2):
                            kts.append(kt_d(gk_, 1024 + tt * 128, 8 + tt, [('k2', gk_, 'ctx')]))
                        attn_pair(kts, ar.ap(Q2 + c * 1024 + q0, [[1, 128]], 0, 64), ar.ap(Q2 + c * 1024 + q0, [[1, 128]], 64, 64), [('q2', c, g)], 128,
                                  (4 + c) * 1024 + q0, [('h', 4 + c, g)], sink_h=2 * c)
            pipe.flush()
            wout_phase(u, 1)
            mlp_phase(u, 1)

        epsb = sb('epsb', 1, F32)
        P.op('dve', lambda e: e.memset(epsb.ap(0, [[1, 1]]), EPS), writes=['epsb'])
        cmap['eps'] = epsb.ap(0, [[1, 1]])
        setup()
        load_x(0)
        compute_mod(0)
        modq.extend(range(4, 12))
        done = False
        for u in range(2):
            if u == 1:
                load_x(u)
            if stop_after == ('X', u):
                store_x(u)
                break
            if stop_after == ('N', u):
                norm_mod(u, 0, 1)
                for kc in range(8):
                    for g in range(2):
                        CP('dve', xT.ap(kc * 1024 + g * 512, [[1, 512]]), hT.ap(kc * 1024 + g * 512, [[1, 512]]), [('h', kc, g)], [('x', kc, g)])
                store_x(u)
                break
            try:
                layer0(u)
            except StopBuild as ex:
                if str(ex) == 'h':
                    for kc in range(8):
                        for g in range(2):
                            CP('dve', xT.ap(kc * 1024 + g * 512, [[1, 512]]), hT.ap(kc * 1024 + g * 512, [[1, 512]]), [('h', kc, g)], [('x', kc, g)])
                store_x(u)
                break
            if stop_after == ('L0', u):
                store_x(u)
                done = True
                break
            layer1(u)
            store_x(u)
            if stop_after == ('L1', u):
                done = True
                break
        mark('end')
        if os.environ.get('DBG_MARKS'):
            import json
            json.dump(marks, open(os.environ['DBG_MARKS'], 'w'))
        P.emit(final_wait_ops=out_ops)
        print("ops:", len(P.ops))
    return nc


def host_consts():
    c = {}
    c['c_ident'] = np.eye(128, dtype=np.float32)
    Rm = np.zeros((64, 64), np.float32)
    for i in range(2):
        for f in range(16):
            a, b = i * 32 + f, i * 32 + 16 + f
            Rm[a, b] = -1.0
            Rm[b, a] = 1.0
    R2 = np.zeros((128, 128), np.float32)
    R2[:64, :64] = Rm.T
    R2[64:, 64:] = Rm.T
    c['c_R2'] = R2
    t = np.arange(1024)
    rows = (t // 64).astype(np.float32)
    cols = (t % 64).astype(np.float32)
    inv = (1.0 / (np.float32(10000.0) ** (np.arange(16, dtype=np.float32) / np.float32(16)))).astype(np.float32)
    cos = np.zeros((128, 1024), np.float32)
    sin = np.zeros((128, 1024), np.float32)
    for p in range(128):
        d = p % 64
        i, f = d // 32, d % 16
        pos = rows if i == 0 else cols
        ang = (pos * inv[f]).astype(np.float32)
        cos[p] = np.cos(ang)
        sin[p] = np.sin(ang)
    c['c_cos'] = cos
    c['c_sin'] = sin
    j = np.arange(128)[:, None]
    i = np.arange(128)[None, :]
    c['c_mprev'] = ((j >= i).astype(np.float32) - 1.0) * 30000.0
    c['c_mnext'] = ((j <= i).astype(np.float32) - 1.0) * 30000.0
    qc = np.arange(64)
    cs = np.clip(qc - 8, 0, 48)
    kc = np.arange(64)
    ok = (kc[:, None] >= cs[None, :]) & (kc[:, None] < cs[None, :] + 16)
    c['c_colok'] = np.concatenate([ok, ok], axis=0).astype(np.float32)
    c['c_colneg'] = (c['c_colok'] - 1.0) * 30000.0
    return c


def rpb_layout(rpb):
    kc = np.arange(64)[:, None]
    qc = np.arange(64)[None, :]
    dc = np.clip(kc - qc + 15, 0, 30)
    out = np.zeros((128, 8, 14, 64), np.float32)
    for i in range(14):
        out[:64, :, i, :] = np.transpose(rpb[:, (6 - i) + 7][:, dc], (1, 0, 2))
        out[64:, :, i, :] = np.transpose(rpb[:, (7 - i) + 7][:, dc], (1, 0, 2))
    return np.ascontiguousarray(out.reshape(128, 8 * 14 * 64))


_NC_CACHE = {}


def kernel(x_prompt, x_sample, cache_diff_k, cache_diff_v, cache_na_k, cache_na_v, cache_swa_k, cache_swa_v, c, c_ctx, mod_w, mod_b,
           norm_mix_pre, norm_mix_post, norm_mlp_pre, norm_mlp_post, w_in_even, conv_w, lambda_q1, lambda_k1, lambda_q2, lambda_k2,
           subln, w_in_odd, rpb, sink, w_out, mlp_w1, mlp_w2, _stop_after=None):
    f = lambda a: np.ascontiguousarray(np.asarray(a, dtype=np.float32))
    if _stop_after not in _NC_CACHE:
        _NC_CACHE[_stop_after] = build(_stop_after)
    nc = _NC_CACHE[_stop_after]
    consts = host_consts()
    shared = dict(
        mod_w=f(mod_w).reshape(2048, 6144), mod_b=f(mod_b),
        gains=np.stack([f(norm_mix_pre).reshape(-1), f(norm_mix_post).reshape(-1), f(norm_mlp_pre).reshape(-1), f(norm_mlp_post).reshape(-1)]),
        w_in_even=f(w_in_even)[0], conv_w=f(conv_w)[0],
        lams=np.stack([f(lambda_q1)[0], f(lambda_k1)[0], f(lambda_q2)[0], f(lambda_k2)[0]]),
        subln=f(subln)[0].reshape(128, 1), w_in_odd=f(w_in_odd)[0], rpbT=rpb_layout(f(rpb)[0]), sink=f(sink)[0].reshape(1, 8),
        w_out=f(w_out).reshape(2048, 1024), mlp_w1=f(mlp_w1).reshape(2048, 4096), mlp_w2=f(mlp_w2).reshape(8192, 1024),
    )
    shared.update(consts)
    xp = f(x_prompt); xs = f(x_sample)
    in_maps = []
    for k in range(NCORES):
        m = dict(shared)
        m['xP'] = xp[4 * k:4 * k + 4].reshape(1024, 1024)
        m['xS'] = xs[k]
        m['cdk'] = f(cache_diff_k)[k, 0].reshape(8, 256, 64)
        m['cdv'] = f(cache_diff_v)[k, 0]
        m['cnk'] = f(cache_na_k)[k, 0]
        m['cnv'] = f(cache_na_v)[k, 0]
        m['csk'] = f(cache_swa_k)[k, 0]
        m['csv'] = f(cache_swa_v)[k, 0]
        m['cvec'] = np.stack([f(c_ctx), f(c)[k]])
        in_maps.append(m)
    in_maps = in_maps[:DBG_CORES]
    res = run_bass_kernel_spmd(nc, in_maps, core_ids=list(range(DBG_CORES)))
    R = list(res.results) + [res.results[0]] * (NCORES - DBG_CORES)
    y_prompt = np.concatenate([r['yP'].reshape(4, 256, 1024) for r in R], axis=0)
    y_sample = np.stack([r['yS'] for r in R], axis=0)
    ndk = np.concatenate([r['ndk'].reshape(4, 1, 4, 2, 256, 64) for r in R], axis=0)
    ndv = np.concatenate([r['ndv'].reshape(4, 1, 4, 256, 128) for r in R], axis=0)
    nnk = np.concatenate([r['nnk'].reshape(4, 1, 8, 256, 64) for r in R], axis=0)
    nnv = np.concatenate([r['nnv'].reshape(4, 1, 8, 256, 64) for r in R], axis=0)
    nsk = np.concatenate([r['nsk'].reshape(4, 1, 2, 256, 64) for r in R], axis=0)
    nsv = np.concatenate([r['nsv'].reshape(4, 1, 2, 256, 64) for r in R], axis=0)
    return (y_prompt, y_sample, ndk, ndv, nnk, nnv, nsk, nsv)
```
